# Optimizing a Trainium2 kernel written in Bass

```python
import jax, jax.numpy as jnp
from jax import lax
import numpy as np

D_MODEL = 1024
BATCH = 32
SEQ = 2048
DEPTH = 2
DEC_BATCH = 8
DEC_SEQ = 32
PAST_LEN = 2048

CHUNK = 64
N_MIXERS = 2
N_LAYERS_A = (DEPTH + 1) // 2
N_LAYERS_B = DEPTH // 2
EPS = 1e-6
NEG_INF = -1e30

A_HEADS = 16
A_HEAD_DIM = D_MODEL // A_HEADS
A_BAND_CHUNKS = 8
A_WINDOW = A_BAND_CHUNKS * CHUNK
REL_CLIP = 128

B_HEADS = 16
B_NOPE = 64
B_ROPE = 32
B_VDIM = 64
KV_LORA = 256
Q_LORA = 384
B_IN = Q_LORA + KV_LORA + B_ROPE + B_HEADS * B_VDIM
ROPE_THETA = 10000.0
Q_BLOCK = 128

kernel_name = "hybrid_streaming_band_mla_step"


def rms_norm(x, g):
    xf = x.astype(jnp.float32)
    y = xf * lax.rsqrt(jnp.mean(xf * xf, axis=-1, keepdims=True) + EPS)
    return (y * g.astype(jnp.float32)).astype(x.dtype)


def ada_norm(x, c, g, w_ada, b_ada):
    mod = jax.nn.silu(c) @ w_ada + b_ada
    shift, scale, gate = jnp.split(mod[:, None, :], 3, axis=-1)
    return rms_norm(x, g) * (1 + scale) + shift, gate


def rotary(x, pos):
    half = x.shape[-1] // 2
    inv = ROPE_THETA ** (-jnp.arange(half, dtype=jnp.float32) / half)
    ang = pos.astype(jnp.float32)[:, None] * inv[None, :]
    cos = jnp.cos(ang)[:, None, :]
    sin = jnp.sin(ang)[:, None, :]
    xf = x.astype(jnp.float32)
    x1, x2 = xf[..., :half], xf[..., half:]
    return jnp.concatenate([x1 * cos - x2 * sin, x2 * cos + x1 * sin], -1).astype(x.dtype)


def chunk_mask(qpos, kpos, n_prev):
    qc = (qpos // CHUNK)[:, None]
    kc = (kpos // CHUNK)[None, :]
    m = (kc <= qc) & (kpos[None, :] >= 0)
    if n_prev is not None:
        m = m & (kc >= qc - n_prev)
    return m


def rel_bias(table, qpos, kpos):
    rel = jnp.clip(qpos[:, None] - kpos[None, :], -REL_CLIP, REL_CLIP) + REL_CLIP
    return table[:, rel].astype(jnp.float32)


def attend(q, k, v, bias, mask, scale):
    s = jnp.einsum('bqhd,bkhd->bhqk', q, k).astype(jnp.float32) * scale
    if bias is not None:
        s = s + bias[None]
    s = jnp.where(mask[None, None], s, NEG_INF)
    p = jax.nn.softmax(s, axis=-1).astype(v.dtype)
    return jnp.einsum('bhqk,bkhd->bqhd', p, v)


def a_project(h, w_in, g_q, g_k):
    B, L, _ = h.shape
    q, k, v, z = jnp.split(h @ w_in, 4, axis=-1)
    q = rms_norm(q.reshape(B, L, A_HEADS, A_HEAD_DIM), g_q)
    k = rms_norm(k.reshape(B, L, A_HEADS, A_HEAD_DIM), g_k)
    v = v.reshape(B, L, A_HEADS, A_HEAD_DIM)
    return q, k, v, z


def a_prompt(h, w_in, g_q, g_k, table, w_out):
    B, S, _ = h.shape
    q, k, v, z = a_project(h, w_in, g_q, g_k)
    n_chunks = S // CHUNK
    pad = A_WINDOW
    band = A_WINDOW + CHUNK
    kp = jnp.pad(k, ((0, 0), (pad, 0), (0, 0), (0, 0)))
    vp = jnp.pad(v, ((0, 0), (pad, 0), (0, 0), (0, 0)))
    qc = q.reshape(B, n_chunks, CHUNK, A_HEADS, A_HEAD_DIM).transpose(1, 0, 2, 3, 4)
    q_off = jnp.arange(CHUNK, dtype=jnp.int32)
    k_off = jnp.arange(band, dtype=jnp.int32)
    scale = A_HEAD_DIM ** -0.5

    def one_chunk(args):
        ci, qi = args
        start = ci * CHUNK
        kb = lax.dynamic_slice_in_dim(kp, start, band, axis=1)
        vb = lax.dynamic_slice_in_dim(vp, start, band, axis=1)
        qpos = start + q_off
        kpos = start - pad + k_off
        return attend(qi, kb, vb, rel_bias(table, qpos, kpos),
                      chunk_mask(qpos, kpos, A_BAND_CHUNKS), scale)

    o = lax.map(one_chunk, (jnp.arange(n_chunks, dtype=jnp.int32), qc))
    o = o.transpose(1, 0, 2, 3, 4).reshape(B, S, A_HEADS * A_HEAD_DIM)
    y = (o * jax.nn.silu(z)) @ w_out
    rows = min(A_WINDOW, S)
    return y, k[:, S - rows:], v[:, S - rows:]


def a_sample(h, cache_k, cache_v, w_in, g_q, g_k, table, w_out):
    B, T, _ = h.shape
    q, k, v, z = a_project(h, w_in, g_q, g_k)
    n_cache = cache_k.shape[1]
    kk = jnp.concatenate([cache_k, k], axis=1)
    vv = jnp.concatenate([cache_v, v], axis=1)
    qpos = PAST_LEN + jnp.arange(T, dtype=jnp.int32)
    kpos = jnp.concatenate([PAST_LEN - n_cache + jnp.arange(n_cache, dtype=jnp.int32), qpos])
    o = attend(q, kk, vv, rel_bias(table, qpos, kpos),
               chunk_mask(qpos, kpos, A_BAND_CHUNKS), A_HEAD_DIM ** -0.5)
    y = (o.reshape(B, T, A_HEADS * A_HEAD_DIM) * jax.nn.silu(z)) @ w_out
    return y, k, v


def mla_inputs(h, pos, w_in, g_cq, w_uq, g_ckv, g_qn, g_qr, g_kr):
    B, L, _ = h.shape
    c_q, c_kv, k_r, z = jnp.split(h @ w_in, [Q_LORA, Q_LORA + KV_LORA, Q_LORA + KV_LORA + B_ROPE], axis=-1)
    q = (rms_norm(c_q, g_cq) @ w_uq).reshape(B, L, B_HEADS, B_NOPE + B_ROPE)
    q = jnp.concatenate([rms_norm(q[..., :B_NOPE], g_qn),
                         rotary(rms_norm(q[..., B_NOPE:], g_qr), pos)], axis=-1)
    c_kv = rms_norm(c_kv, g_ckv)
    k_r = rotary(rms_norm(k_r, g_kr)[:, :, None, :], pos)[:, :, 0, :]
    return q, c_kv, k_r, z


def mla_keys(c_kv, k_r, w_ukv, g_kn):
    B, L, _ = c_kv.shape
    kv = (c_kv @ w_ukv).reshape(B, L, B_HEADS, B_NOPE + B_VDIM)
    k = jnp.concatenate([rms_norm(kv[..., :B_NOPE], g_kn),
                         jnp.broadcast_to(k_r[:, :, None, :], (B, L, B_HEADS, B_ROPE))], axis=-1)
    return k, kv[..., B_NOPE:]


def b_prompt(h, w_in, g_cq, w_uq, g_ckv, w_ukv, g_qn, g_qr, g_kn, g_kr, w_out):
    B, S, _ = h.shape
    pos = jnp.arange(S, dtype=jnp.int32)
    q, c_kv, k_r, z = mla_inputs(h, pos, w_in, g_cq, w_uq, g_ckv, g_qn, g_qr, g_kr)
    k, v = mla_keys(c_kv, k_r, w_ukv, g_kn)
    n_blocks = S // Q_BLOCK
    qb = q.reshape(B, n_blocks, Q_BLOCK, B_HEADS, B_NOPE + B_ROPE).transpose(1, 0, 2, 3, 4)
    q_off = jnp.arange(Q_BLOCK, dtype=jnp.int32)
    scale = (B_NOPE + B_ROPE) ** -0.5

    def one_block(args):
        bi, qi = args
        qpos = bi * Q_BLOCK + q_off
        return attend(qi, k, v, None, chunk_mask(qpos, pos, None), scale)

    o = lax.map(one_block, (jnp.arange(n_blocks, dtype=jnp.int32), qb))
    o = o.transpose(1, 0, 2, 3, 4).reshape(B, S, B_HEADS * B_VDIM)
    y = (o * jax.nn.silu(z)) @ w_out
    return y, c_kv, k_r


def b_sample(h, cache_ckv, cache_kr, w_in, g_cq, w_uq, g_ckv, w_ukv, g_qn, g_qr, g_kn, g_kr, w_out):
    B, T, _ = h.shape
    qpos = PAST_LEN + jnp.arange(T, dtype=jnp.int32)
    q, c_kv, k_r, z = mla_inputs(h, qpos, w_in, g_cq, w_uq, g_ckv, g_qn, g_qr, g_kr)
    k, v = mla_keys(jnp.concatenate([cache_ckv, c_kv], axis=1),
                    jnp.concatenate([cache_kr, k_r], axis=1), w_ukv, g_kn)
    kpos = jnp.arange(PAST_LEN + T, dtype=jnp.int32)
    o = attend(q, k, v, None, chunk_mask(qpos, kpos, None), (B_NOPE + B_ROPE) ** -0.5)
    y = (o.reshape(B, T, B_HEADS * B_VDIM) * jax.nn.silu(z)) @ w_out
    return y, c_kv, k_r


def setup_inputs(seed: int = 0) -> dict:
    key = jax.random.key(seed)
    ks = jax.random.split(key, 32)
    f32 = jnp.float32

    def nrm(k, shape, s):
        return jax.random.normal(k, shape, f32) * s

    def gain(k, shape):
        return 1.0 + 0.02 * jax.random.normal(k, shape, f32)

    a_cache = min(A_WINDOW, PAST_LEN)
    D = D_MODEL
    return {
        "x_prompt": nrm(ks[0], (BATCH, SEQ, D), 1.0),
        "x_sample": nrm(ks[1], (DEC_BATCH, DEC_SEQ, D), 1.0),
        "cache_a_k": nrm(ks[2], (N_LAYERS_A, DEC_BATCH, a_cache, A_HEADS, A_HEAD_DIM), 1.0),
        "cache_a_v": nrm(ks[3], (N_LAYERS_A, DEC_BATCH, a_cache, A_HEADS, A_HEAD_DIM), 1.0),
        "cache_mla_ckv": nrm(ks[4], (N_LAYERS_B, DEC_BATCH, PAST_LEN, KV_LORA), 1.0),
        "cache_mla_krope": nrm(ks[5], (N_LAYERS_B, DEC_BATCH, PAST_LEN, B_ROPE), 1.0),
        "c_prompt": nrm(ks[6], (BATCH, D), 1.0),
        "c_sample": nrm(ks[7], (DEC_BATCH, D), 1.0),
        "norm_g": gain(ks[8], (DEPTH, D)),
        "ada_w": nrm(ks[9], (DEPTH, D, 3 * D), 0.5 * D ** -0.5),
        "ada_b": nrm(ks[10], (DEPTH, 3 * D), 0.02),
        "a_w_in": nrm(ks[11], (N_LAYERS_A, D, 4 * D), D ** -0.5),
        "a_g_q": gain(ks[12], (N_LAYERS_A, A_HEAD_DIM)),
        "a_g_k": gain(ks[13], (N_LAYERS_A, A_HEAD_DIM)),
        "a_rel_bias": nrm(ks[14], (N_LAYERS_A, A_HEADS, 2 * REL_CLIP + 1), 0.5),
        "a_w_out": nrm(ks[15], (N_LAYERS_A, D, D), D ** -0.5),
        "b_w_in": nrm(ks[16], (N_LAYERS_B, D, B_IN), D ** -0.5),
        "b_g_cq": gain(ks[17], (N_LAYERS_B, Q_LORA)),
        "b_w_uq": nrm(ks[18], (N_LAYERS_B, Q_LORA, B_HEADS * (B_NOPE + B_ROPE)), Q_LORA ** -0.5),
        "b_g_ckv": gain(ks[19], (N_LAYERS_B, KV_LORA)),
        "b_w_ukv": nrm(ks[20], (N_LAYERS_B, KV_LORA, B_HEADS * (B_NOPE + B_VDIM)), KV_LORA ** -0.5),
        "b_g_qn": gain(ks[21], (N_LAYERS_B, B_NOPE)),
        "b_g_qr": gain(ks[22], (N_LAYERS_B, B_ROPE)),
        "b_g_kn": gain(ks[23], (N_LAYERS_B, B_NOPE)),
        "b_g_kr": gain(ks[24], (N_LAYERS_B, B_ROPE)),
        "b_w_out": nrm(ks[25], (N_LAYERS_B, B_HEADS * B_VDIM, D), (B_HEADS * B_VDIM) ** -0.5),
    }


def reference(x_prompt, x_sample, cache_a_k, cache_a_v, cache_mla_ckv, cache_mla_krope,
              c_prompt, c_sample, norm_g, ada_w, ada_b,
              a_w_in, a_g_q, a_g_k, a_rel_bias, a_w_out,
              b_w_in, b_g_cq, b_w_uq, b_g_ckv, b_w_ukv, b_g_qn, b_g_qr, b_g_kn, b_g_kr, b_w_out):
    y_p, y_s = x_prompt, x_sample
    akp, avp, aks, avs = [], [], [], []
    bcp, brp, bcs, brs = [], [], [], []
    for i in range(DEPTH):
        j = i // N_MIXERS
        h_p, gate_p = ada_norm(y_p, c_prompt, norm_g[i], ada_w[i], ada_b[i])
        h_s, gate_s = ada_norm(y_s, c_sample, norm_g[i], ada_w[i], ada_b[i])
        if i % N_MIXERS == 0:
            out_p, k_new_p, v_new_p = a_prompt(h_p, a_w_in[j], a_g_q[j], a_g_k[j], a_rel_bias[j], a_w_out[j])
            out_s, k_new_s, v_new_s = a_sample(h_s, cache_a_k[j], cache_a_v[j], a_w_in[j], a_g_q[j],
                                               a_g_k[j], a_rel_bias[j], a_w_out[j])
            akp.append(k_new_p); avp.append(v_new_p); aks.append(k_new_s); avs.append(v_new_s)
        else:
            out_p, ckv_p, kr_p = b_prompt(h_p, b_w_in[j], b_g_cq[j], b_w_uq[j], b_g_ckv[j], b_w_ukv[j],
                                          b_g_qn[j], b_g_qr[j], b_g_kn[j], b_g_kr[j], b_w_out[j])
            out_s, ckv_s, kr_s = b_sample(h_s, cache_mla_ckv[j], cache_mla_krope[j], b_w_in[j], b_g_cq[j],
                                          b_w_uq[j], b_g_ckv[j], b_w_ukv[j], b_g_qn[j], b_g_qr[j],
                                          b_g_kn[j], b_g_kr[j], b_w_out[j])
            bcp.append(ckv_p); brp.append(kr_p); bcs.append(ckv_s); brs.append(kr_s)
        y_p = y_p + gate_p * out_p
        y_s = y_s + gate_s * out_s
    return (y_p, y_s,
            jnp.stack(akp), jnp.stack(avp), jnp.stack(aks), jnp.stack(avs),
            jnp.stack(bcp), jnp.stack(brp), jnp.stack(bcs), jnp.stack(brs))
```

```python
import numpy as np
import concourse.bass as bass
import concourse.mybir as mybir
from concourse.bass_utils import run_bass_kernel_spmd
from contextlib import ExitStack

F32 = mybir.dt.float32
BF16 = mybir.dt.bfloat16
AF = mybir.ActivationFunctionType
ALU = mybir.AluOpType
AX = mybir.AxisListType

D = 1024
EPS = 1e-6
NCORES = 8


class T:
    __slots__ = ("name", "w", "r")

    def __init__(self, name=""):
        self.name = name
        self.w = None
        self.r = {}


class Op:
    __slots__ = ("eng", "fn", "deps", "dma", "val", "needed", "kind")


class Sched:
    ENG = ("pe", "act", "dve", "pool", "sp")

    def __init__(self):
        self.q = {e: [] for e in self.ENG}
        self.dmacnt = {}
        self.lastdma = {}
        self.nbar = 0

    def op(self, eng, fn, rd=(), wr=(), dma=None, kind="c", extra_deps=()):
        o = Op()
        o.eng, o.fn, o.dma, o.kind = eng, fn, dma, kind
        o.needed = False
        o.val = None
        deps = list(extra_deps)
        for t in rd:
            if t.w is not None:
                deps.append((t.w, 0))
        for t in wr:
            if t.w is not None:
                deps.append((t.w, 1))
            for r in t.r.values():
                if r is not o:
                    deps.append((r, 2))
        o.deps = deps
        for d, k in deps:
            if d.dma is None:
                if d.eng != eng or eng != "pe":
                    d.needed = True
        wrs = set(id(t) for t in wr)
        for t in wr:
            t.w = o
            t.r = {}
        key = dma if dma is not None else eng
        for t in rd:
            if id(t) not in wrs:
                t.r[key] = o
        if dma is not None:
            c = self.dmacnt.get(dma, 0) + 16
            self.dmacnt[dma] = c
            o.val = c
            self.lastdma[dma] = o
        self.q[eng].append(o)
        return o

    def barrier(self, markers, pe_wr=()):
        bt = [T("bar%d_%s" % (self.nbar, e)) for e in self.ENG]
        self.nbar += 1
        alld = [(o, 0) for o in self.lastdma.values()]
        for e, t in zip(self.ENG, bt):
            if e == "sp":
                self.op("sp", None, wr=[t], kind="seminc", extra_deps=alld)
            elif e == "pool":
                self.op("pool", markers[e], wr=[t], extra_deps=alld)
            elif e == "pe":
                self.op(e, markers[e], wr=[t] + list(pe_wr))
            else:
                self.op(e, markers[e], wr=[t])
        for e in self.ENG:
            self.op(e, None, rd=bt, kind="wait")

    def emit(self, nc, stack):
        for e in self.ENG:
            cnt = 0
            for o in self.q[e]:
                if o.dma is None:
                    if o.needed:
                        cnt += 1
                    o.val = cnt
        sems = {}
        for e in self.ENG:
            sems[e] = stack.enter_context(nc.semaphore("s_" + e))
        for k in self.dmacnt:
            sems[k] = stack.enter_context(nc.semaphore("d_" + k))
        self.sems = sems
        block = stack.enter_context(nc.Block())
        final = [(k, v) for k, v in self.dmacnt.items()]

        def run(eng, ename):
            waited = {}
            for o in self.q[ename]:
                for d, kind in o.deps:
                    key = d.dma if d.dma is not None else d.eng
                    if d.dma is None and d.eng == ename and ename == "pe":
                        continue
                    if waited.get(key, 0) >= d.val:
                        continue
                    eng.wait_ge(sems[key], d.val)
                    waited[key] = d.val
                if o.kind == "wait":
                    continue
                if o.kind == "seminc":
                    if o.needed:
                        eng.sem_inc(sems[ename], 1)
                    continue
                ins = o.fn(eng)
                if o.dma is not None:
                    ins.then_inc(sems[o.dma], 16)
                elif o.needed:
                    ins.then_inc(sems[ename], 1)
            if ename in ("sp", "pool"):
                for k, v in final:
                    if waited.get(k, 0) < v:
                        eng.wait_ge(sems[k], v)

        @block.sync
        def _(e):
            run(e, "sp")

        @block.scalar
        def _(e):
            run(e, "act")

        @block.vector
        def _(e):
            run(e, "dve")

        @block.gpsimd
        def _(e):
            run(e, "pool")

        @block.tensor
        def _(e):
            run(e, "pe")


class Ring:
    def __init__(self, items):
        self.items = items
        self.i = 0

    def next(self):
        it = self.items[self.i % len(self.items)]
        self.i += 1
        return it


class Arena:
    def __init__(self, ar, nbytes):
        self.ar = ar
        self.cap = nbytes
        self.off = 0
        self.marks = []

    def _take(self, nbytes):
        nbytes = (nbytes + 63) // 64 * 64
        o = self.off
        self.off += nbytes
        assert self.off <= self.cap, "SBUF arena overflow %d > %d" % (self.off, self.cap)
        return o

    def f32(self, n, parts=128):
        o = self._take(n * 4)
        return self.ar[0:parts, o // 4:o // 4 + n]

    def bf16(self, n, parts=128):
        o = self._take(n * 2)
        return self.ar[0:parts, o // 4:o // 4 + (n + 1) // 2].bitcast(BF16)[:, 0:n]

    def mark(self):
        return self.off

    def reset(self, m):
        self.off = m


PERM_COLS = np.array([(j8 + 8 * rg) * 64 + d for j8 in range(8) for rg in range(2) for d in range(64)])


def build(nseq, S, PAST, debug_layers=2):
    NT = S // 128
    NS = nseq + 1
    NTC = PAST // 128
    AR = min(512, S)
    NAO = AR // 128
    NTB = max(NT, NTC + 1)
    nc = bass.Bass("TRN2", target_bir_lowering=False)

    def din(name, shape):
        return nc.dram_tensor(name, list(shape), F32, kind="ExternalInput").ap()

    def dout(name, shape):
        return nc.dram_tensor(name, list(shape), F32, kind="ExternalOutput").ap()

    def dscr(name, shape):
        return nc.dram_tensor(name, list(shape), F32).ap()

    I = {}
    for name, shape in [
        ("xp", (nseq * S, D)), ("xs", (128, D)), ("ck", (512, D)), ("cv", (512, D)),
        ("cckv", (PAST, 256)), ("ckr", (PAST, 32)), ("scT", (128, 8 * NS)),
        ("norm_g", (2, D)), ("ada_w", (2 * D, 3 * D)), ("ada_b", (2, 3 * D)),
        ("a_w_in", (D, 4 * D)), ("a_g_q", (1, 64)), ("a_g_k", (1, 64)),
        ("erel", (128, 16 * 2 * 128)), ("cbias", (1, 16)), ("a_w_out", (D, D)),
        ("b_w_in", (D, 1696)), ("b_g_cq", (1, 384)), ("b_w_uq", (384, 1536)), ("b_g_ckv", (1, 256)),
        ("b_w_ukv", (256, 2048)), ("b_g_qn", (1, 64)), ("b_g_qr", (1, 32)), ("b_g_kn", (1, 64)),
        ("b_g_kr", (1, 32)), ("b_w_out", (D, D)),
        ("cosp", (128, NT * 16)), ("sinp", (128, NT * 16)), ("coss", (128, 16)), ("sins", (128, 16)),
        ("ident", (128, 128)),
    ]:
        I[name] = din(name, shape)
    O = {}
    for name, shape in [
        ("yp", (nseq * S, D)), ("ys", (128, D)), ("akp", (nseq * AR, D)), ("avp", (nseq * AR, D)),
        ("aks", (128, D)), ("avs", (128, D)), ("bcp", (nseq * S, 256)), ("brp", (nseq * S, 32)),
        ("bcs", (128, 256)), ("brs", (128, 32)),
    ]:
        O[name] = dout(name, shape)
    y0p = dscr("y0p", (nseq * S, D))
    y0s = dscr("y0s", (128, D))
    modd = dscr("modd", (2 * NS, 3 * D))

    sch = Sched()
    stack = ExitStack()
    ARENA_BYTES = 207 * 1024
    arena_t = stack.enter_context(nc.sbuf_tensor("arena", [128, ARENA_BYTES // 4], F32))
    A = Arena(arena_t, ARENA_BYTES)
    psb = [stack.enter_context(nc.psum_tensor("psb%d" % i, [128, 512], F32)) for i in range(8)]

    def dma_in(out_ap, in_ap, t, rd=(), key=None):
        sch.op("sp", lambda e, o=out_ap, i=in_ap: e.dma_start(out=o, in_=i), rd=rd, wr=[t], dma=key or ("l_" + t.name))

    def dma_out(out_ap, in_ap, t, wr=(), key=None):
        sch.op("pool", lambda e, o=out_ap, i=in_ap: e.dma_start(out=o, in_=i), rd=[t], wr=wr, dma=key or ("s_" + t.name))

    def mm(pst, out_ap, pairs, rd, start=True, stop=True):
        def fn(e, out_ap=out_ap, pairs=pairs, start=start, stop=stop):
            n = len(pairs)
            for i, (l, r) in enumerate(pairs):
                ins = e.matmul(out_ap, lhsT=l, rhs=r, start=(start and i == 0), stop=(stop and i == n - 1))
            return ins
        sch.op("pe", fn, rd=rd, wr=[pst])

    def transposes(pst, items, rd, ident):
        def fn(e, items=items, ident=ident):
            for o, i in items:
                ins = e.transpose(out=o, in_=i, identity=ident)
            return ins
        sch.op("pe", fn, rd=rd, wr=[pst])

    def act(out, in_, func, rd, wr, **kw):
        sch.op("act", lambda e, o=out, i=in_, f=func, kw=kw: e.activation(out=o, in_=i, func=f, **kw), rd=rd, wr=wr)

    def vop(eng, name, rd, wr, *args, **kw):
        sch.op(eng, lambda e, name=name, args=args, kw=kw: getattr(e, name)(*args, **kw), rd=rd, wr=wr)

    cp_rr = [0]

    def evac_copy(out, in_, rd, wr):
        cp_rr[0] += 1
        if cp_rr[0] % 2:
            act(out, in_, AF.Copy, rd, wr)
        else:
            vop("dve", "tensor_copy", rd, wr, out=out, in_=in_)

    def bc(ap, shape):
        return ap.to_broadcast(list(shape))

    identf = A.f32(128)
    identb = A.bf16(128)
    t_ident = T("ident")
    dma_in(identf, I["ident"], t_ident)
    t_identb = T("identb")
    act(identb, identf, AF.Copy, [t_ident], [t_identb])
    bar_scr = {e: A.f32(16) for e in ("act", "dve", "pool")}
    t_scr = T("barscr")

    def do_barrier():
        sch.barrier({
            "act": lambda e: e.activation(out=bar_scr["act"], in_=identf[:, 0:16], func=AF.Copy),
            "dve": lambda e: e.tensor_copy(out=bar_scr["dve"], in_=identf[:, 0:16]),
            "pool": lambda e: e.tensor_copy(out=bar_scr["pool"], in_=identf[:, 0:16]),
            "pe": lambda e: e.transpose(out=psb[2][:, :].bitcast(BF16)[:, 0:128], in_=identb, identity=identb),
        }, pe_wr=[t_tb])

    mmb = Ring([(psb[0][:, :], T("mm0")), (psb[1][:, :], T("mm1"))])
    tb_ap = psb[2][:, :].bitcast(BF16)
    t_tb = T("tb")

    def load_bc(name, n):
        ap = A.f32(n)
        t = T("g_" + name)
        dma_in(ap, bc(I[name][0:1, :], [128, n]), t)
        return ap, t

    gq, t_gq = load_bc("a_g_q", 64)
    gk, t_gk = load_bc("a_g_k", 64)
    cb, t_cb = load_bc("cbias", 16)
    gcq, t_gcq = load_bc("b_g_cq", 384)
    gckv, t_gckv = load_bc("b_g_ckv", 256)
    gqn, t_gqn = load_bc("b_g_qn", 64)
    gqr, t_gqr = load_bc("b_g_qr", 32)
    gkn, t_gkn = load_bc("b_g_kn", 64)
    gkr, t_gkr = load_bc("b_g_kr", 32)
    ncb = A.f32(16)
    t_ncb = T("ncb")
    vop("dve", "tensor_scalar", [t_cb], [t_ncb], out=ncb, in0=cb, scalar1=-1.0, scalar2=None, op0=ALU.mult)
    invn3 = A.f32(3)
    invn8 = A.f32(8)
    t_invn = T("invn")
    for ap_, v in ((invn3[:, 0:1], 1.0 / 384), (invn3[:, 1:2], 1.0 / 256), (invn3[:, 2:3], 1.0 / 32),
                   (invn8[:, 0:4], 1.0 / 64), (invn8[:, 4:8], 1.0 / 32)):
        vop("pool", "memset", [], [t_invn], ap_, v)

    def stats_tile(n):
        return A.f32(n)

    persist_mark = A.mark()

    wA_in = A.bf16(8 * 4096)
    wA_in_v = wA_in.rearrange("p (k n) -> p k n", n=4096)
    wA_out = A.bf16(8 * 1024)
    wA_out_v = wA_out.rearrange("p (k n) -> p k n", n=1024)
    t_wAin = [T("wAin%d" % i) for i in range(8)]
    t_wAout = [T("wAout%d" % i) for i in range(2)]
    Eb = A.bf16(16 * 2 * 128)
    Ebv = Eb.rearrange("p (h j q) -> p h j q", j=2, q=128)
    t_E = T("E")
    M0 = A.bf16(128)
    t_M0 = T("M0")
    vop("pool", "memset", [], [t_M0], M0, 1.0)
    vop("pool", "memset", [], [t_M0], M0[0:64, 64:128], 0.0)
    wA_mark = A.mark()

    stg = Ring([(A.f32(2048), T("stg%d" % i)) for i in range(4)])
    cast_rr = [0]

    def cast(out, in_, rd, wr):
        cast_rr[0] += 1
        k = cast_rr[0] % 3
        if k == 0:
            act(out, in_, AF.Copy, rd, wr)
        elif k == 1:
            vop("dve", "tensor_copy", rd, wr, out=out, in_=in_)
        else:
            vop("pool", "tensor_copy", rd, wr, out=out, in_=in_)

    def load_weight(src, K, N, dst_v, tlist, tcols):
        cw = 2048 // K
        for c0 in range(0, N, cw):
            w_ = min(cw, N - c0)
            sap, st = stg.next()
            sv = sap[:, 0:K * w_].rearrange("p (k n) -> p k n", n=w_)
            dma_in(sv, src[:, c0:c0 + w_].rearrange("(k p) n -> p k n", p=128), st)
            cast(dst_v[:, :, c0:c0 + w_], sv, [st], [tlist[c0 // tcols]])

    sc = A.f32(8 * NS)
    t_sc = T("sc")
    dma_in(sc, I["scT"], t_sc)
    act(sc, sc, AF.Silu, [t_sc], [t_sc])
    ones1 = A.f32(8, parts=1)
    t_ones = T("ones1")
    vop("dve", "memset", [], [t_ones], ones1, 1.0)
    adab = Ring([(A.f32(256, parts=1), T("adab%d" % i)) for i in range(4)])
    modc = Ring([(A.f32(256, parts=NS), T("modc%d" % i)) for i in range(4)])
    scv = sc.rearrange("p (k s) -> p k s", s=NS)
    for l in range(2):
        for c in range(12):
            c0 = c * 256
            sap, st = stg.next()
            sv = sap.rearrange("p (k n) -> p k n", n=256)
            dma_in(sv, I["ada_w"][l * D:(l + 1) * D, c0:c0 + 256].rearrange("(k p) n -> p k n", p=128), st)
            bap, bt = adab.next()
            dma_in(bap, I["ada_b"][l:l + 1, c0:c0 + 256], bt)
            pap, pt = mmb.next()
            pairs = [(scv[:, k, :], sv[:, k, :]) for k in range(8)] + [(ones1[0:1, 0:NS], bap)]
            mm(pt, pap[0:NS, 0:256], pairs, [t_sc, st, bt, t_ones])
            map_, mt = modc.next()
            act(map_, pap[0:NS, 0:256], AF.Copy, [pt], [mt])
            dma_out(modd[l * NS:(l + 1) * NS, c0:c0 + 256], map_, mt)

    load_weight(I["a_w_in"], 8, 4096, wA_in_v, t_wAin, 512)
    load_weight(I["a_w_out"], 8, 1024, wA_out_v, t_wAout, 512)
    for hh in range(0, 16, 8):
        sap, st = stg.next()
        dma_in(sap, I["erel"][:, hh * 256:(hh + 8) * 256], st)
        for h in range(hh, hh + 8):
            act(Eb[:, h * 256:(h + 1) * 256], sap[:, (h - hh) * 256:(h - hh + 1) * 256], AF.Exp, [st, t_ncb], [t_E],
                bias=ncb[:, h:h + 1])
    vop("pool", "memset", [], [t_E], Ebv[64:128, :, 1, 0:64], 0.0)
    do_barrier()

    A.reset(wA_mark)
    KR = 6
    kT = A.bf16(8 * KR * 128)
    kTv = kT.rearrange("p (j n) -> p j n", n=KR * 128)
    t_kT = [T("kT%d" % i) for i in range(KR)]
    Vp = A.bf16(KR * 16 * 65)
    Vpv = Vp.rearrange("p (s h d) -> p s h d", h=16, d=65)
    t_Vp = [T("Vp%d" % i) for i in range(KR)]
    vop("pool", "memset", [], t_Vp, Vpv[:, :, :, 64:65], 1.0)
    xs_ = Ring([(A.f32(1024), T("x%d" % i)) for i in range(6)])
    dtmp = A.f32(1024)
    t_dtmp = T("dtmp")
    hb = A.bf16(1024)
    t_hb = T("hb")
    hTs = Ring([(A.bf16(1024), T("hT%d" % i)) for i in range(2)])
    pfs = Ring([(A.f32(1024), T("pf%d" % i)) for i in range(3)])
    qnb = A.bf16(1024)
    t_qnb = T("qnb")
    knb = A.bf16(1024)
    t_knb = T("knb")
    qTs = Ring([(A.bf16(1024), T("qT%d" % i)) for i in range(2)])
    zss = Ring([(A.bf16(1024), T("zs%d" % i)) for i in range(3)])
    PTs = Ring([(A.bf16(512), T("PT%d" % i)) for i in range(3)])
    ogs = Ring([(A.bf16(1024), T("og%d" % i)) for i in range(2)])
    ogT = A.bf16(1024)
    t_ogT = T("ogT")
    bc3 = A.f32(2048)
    t_bc3 = T("bc3")
    gates = Ring([(A.f32(1024), T("gate%d" % i)) for i in range(2)])
    cur_gate = [None]
    G1 = A.f32(1024)
    t_G1 = T("G1")
    st_ss = A.f32(4)
    t_ss = T("ss")
    st_16 = A.f32(64)
    t_16 = T("st16")
    rden = A.f32(16)
    t_rden = T("rden")
    sringA = Ring([(psb[3][:, :], T("sa0")), (psb[4][:, :], T("sa1")), (psb[7][:, :], T("sa2"))])
    t_O = [T("O0"), T("O1")]

    def o_ap(h):
        if h < 7:
            return psb[5][:, h * 65:(h + 1) * 65], t_O[0]
        if h < 14:
            return psb[6][:, (h - 7) * 65:(h - 6) * 65], t_O[1]
        return psb[5][:, (h - 14) * 65:(h - 13) * 65], t_O[0]

    def load_mod(l, s):
        dma_in(bc3, bc(modd[l * NS + s:l * NS + s + 1, 0:2048], [128, 2048]), t_bc3, key="l_bc3")
        gap, gt = gates.next()
        dma_in(gap, bc(modd[l * NS + s:l * NS + s + 1, 2048:3072], [128, 1024]), gt)
        cur_gate[0] = (gap, gt)
        dma_in(G1, bc(I["norm_g"][l:l + 1, :], [128, 1024]), t_G1, key="l_G1")
        vop("dve", "scalar_tensor_tensor", [t_bc3, t_G1], [t_G1], out=G1, in0=bc3[:, 1024:2048], scalar=1.0, in1=G1,
            op0=ALU.add, op1=ALU.mult)

    def rstd_from(ss_ap, n, tss, scale, scr_ap):
        act(scr_ap, ss_ap, AF.Ln, [tss], [tss], scale=scale, bias=EPS)
        act(ss_ap, scr_ap, AF.Exp, [tss], [tss], scale=-0.5)

    def ada_norm_tile(xap, xt):
        act(junk, xap, AF.Square, [xt], [t_junk, t_ss], accum_out=st_ss[:, 0:1])
        rstd_from(st_ss[:, 0:1], 1, t_ss, 1.0 / D, st_ss[:, 1:2])
        vop("dve", "scalar_tensor_tensor", [xt, t_ss, t_G1], [t_dtmp], out=dtmp, in0=xap, scalar=st_ss[:, 0:1], in1=G1,
            op0=ALU.mult, op1=ALU.mult)
        vop("dve", "tensor_tensor", [t_dtmp, t_bc3], [t_hb], out=hb, in0=dtmp, in1=bc3[:, 0:1024], op=ALU.add)
        transposes(t_tb, [(tb_ap[:, k * 128:(k + 1) * 128], hb[:, k * 128:(k + 1) * 128]) for k in range(8)],
                   [t_hb, t_identb], identb)
        hT, hTt = hTs.next()
        evac_copy(hT, tb_ap, [t_tb], [hTt])
        return hT.rearrange("p (k n) -> p k n", n=128), hTt

    def head_rms(pf, pft, g_ap, g_t, out_bf, out_t, also_f32):
        gb = bc(g_ap.unsqueeze(1), [128, 8, 64])
        for hf in range(2):
            cs_ = slice(hf * 512, hf * 512 + 512)
            pv = pf[:, cs_].rearrange("p (h d) -> p h d", d=64)
            dv = dtmp[:, cs_].rearrange("p (h d) -> p h d", d=64)
            s16 = st_16[:, hf * 8:hf * 8 + 8]
            vop("dve", "tensor_tensor", [pft], [t_dtmp], out=dtmp[:, cs_], in0=pf[:, cs_], in1=pf[:, cs_], op=ALU.mult)
            vop("dve", "tensor_reduce", [t_dtmp], [t_16], out=s16, in_=dv, axis=AX.X, op=ALU.add)
            yield
        rstd_from(st_16[:, 0:16], 16, t_16, 1.0 / 64, st_16[:, 16:32])
        yield
        for hf in range(2):
            cs_ = slice(hf * 512, hf * 512 + 512)
            pv = pf[:, cs_].rearrange("p (h d) -> p h d", d=64)
            s16 = st_16[:, hf * 8:hf * 8 + 8]
            vop("dve", "tensor_tensor", [pft, t_16], [pft], out=pv, in0=pv, in1=bc(s16.unsqueeze(2), [128, 8, 64]), op=ALU.mult)
            yield
            if also_f32:
                vop("dve", "tensor_tensor", [pft, g_t], [pft], out=pv, in0=pv, in1=gb, op=ALU.mult)
                act(out_bf[:, cs_], pf[:, cs_], AF.Copy, [pft], [out_t])
            else:
                vop("dve", "tensor_tensor", [pft, g_t], [out_t], out=out_bf[:, cs_].rearrange("p (h d) -> p h d", d=64), in0=pv, in1=gb,
                    op=ALU.mult)
            yield

    class Ctx:
        pass

    def sample_cache_tile(c):
        j = c.j
        slot = c.ti % KR
        pf, pft = pfs.next()
        dma_in(pf, I["ck"][j * 128:(j + 1) * 128, :], pft)
        vop("pool", "tensor_copy", [pft], [t_knb], out=knb, in_=pf)
        transposes(t_tb, [(tb_ap[:, q * 128:(q + 1) * 128], knb[:, q * 128:(q + 1) * 128]) for q in range(8)],
                   [t_knb, t_identb], identb)
        evac_copy(kTv[:, :, slot * 128:(slot + 1) * 128], tb_ap.rearrange("p (j n) -> p j n", n=128), [t_tb], [t_kT[slot]])
        pf2, pft2 = pfs.next()
        dma_in(pf2, I["cv"][j * 128:(j + 1) * 128, :], pft2)
        vop("dve", "tensor_copy", [pft2], [t_Vp[slot]], out=Vpv[:, slot, :, 0:64], in_=pf2.rearrange("p (h d) -> p h d", d=64))

    def load_x(c):
        c.x, c.xt = xs_.next()
        dma_in(c.x, c.xsrc, c.xt)

    def norm_part(c):
        if c.premod is not None:
            c.premod()
        c.gate = cur_gate[0]
        xap, xt = c.x, c.xt
        act(hb, xap, AF.Square, [xt], [t_hb, t_ss], accum_out=st_ss[:, 0:1])
        rstd_from(st_ss[:, 0:1], 1, t_ss, 1.0 / D, st_ss[:, 1:2])
        for hf in range(2):
            cs_ = slice(hf * 512, hf * 512 + 512)
            vop("dve", "scalar_tensor_tensor", [xt, t_ss, t_G1], [t_dtmp], out=dtmp[:, cs_], in0=xap[:, cs_], scalar=st_ss[:, 0:1],
                in1=G1[:, cs_], op0=ALU.mult, op1=ALU.mult)
            vop("dve", "tensor_tensor", [t_dtmp, t_bc3], [t_hb], out=hb[:, cs_], in0=dtmp[:, cs_], in1=bc3[:, cs_], op=ALU.add)
            yield

    def hT_part(c):
        transposes(t_tb, [(tb_ap[:, k * 128:(k + 1) * 128], hb[:, k * 128:(k + 1) * 128]) for k in range(8)],
                   [t_hb, t_identb], identb)
        hT, hTt = hTs.next()
        evac_copy(hT, tb_ap, [t_tb], [hTt])
        c.hT, c.hTt = hT.rearrange("p (k n) -> p k n", n=128), hTt

    def frontA(c):
        for c2 in c.load_list:
            load_x(c2)
        if c.kind == "cache":
            sample_cache_tile(c)
            yield
            if c.norm_next is not None:
                yield from norm_part(c.norm_next)
                hT_part(c.norm_next)
            return
        hT, hTt = c.hT, c.hTt
        slot = c.ti % KR
        c.slot = slot
        pfq, pfqt = pfs.next()
        pfk, pfkt = pfs.next()
        zs, zst = zss.next()
        c.zs, c.zst = zs, zst
        pfv = pfvt = None
        if c.store_kv is not None:
            pfv, pfvt = pfs.next()
        for cg in range(8):
            pap, pt = mmb.next()
            mm(pt, pap, [(hT[:, k, :], wA_in_v[:, k, cg * 512:(cg + 1) * 512]) for k in range(8)],
               [hTt, t_wAin[cg]])
            half = slice((cg % 2) * 512, (cg % 2) * 512 + 512)
            if cg < 2:
                act(pfq[:, half], pap, AF.Copy, [pt], [pfqt])
            elif cg < 4:
                act(pfk[:, half], pap, AF.Copy, [pt], [pfkt])
            elif cg < 6:
                h0 = (cg - 4) * 8
                if pfv is not None:
                    act(pfv[:, half], pap, AF.Copy, [pt], [pfvt])
                    vop("pool", "tensor_copy", [pfvt], [t_Vp[slot]], out=Vpv[:, slot, h0:h0 + 8, 0:64],
                        in_=pfv[:, half].rearrange("p (h d) -> p h d", d=64))
                else:
                    vop("dve", "tensor_copy", [pt], [t_Vp[slot]], out=Vpv[:, slot, h0:h0 + 8, 0:64],
                        in_=pap.rearrange("p (h d) -> p h d", d=64))
                if cg == 5 and c.sample:
                    vop("pool", "memset", [], [t_Vp[slot]], Vpv[32:64, slot, :, :], 0.0)
                    vop("pool", "memset", [], [t_Vp[slot]], Vpv[64:128, slot, :, :], 0.0)
            elif cg == 6:
                z6 = (pap, pt)
            else:
                act(zs[:, 0:512], z6[0], AF.Silu, [z6[1]], [zst])
                act(zs[:, 512:1024], pap, AF.Silu, [pt], [zst])
            yield
            if cg == 0 and c.norm_next is not None:
                yield from norm_part(c.norm_next)
            if cg == 1:
                yield from head_rms(pfq, pfqt, gq, t_gq, qnb, t_qnb, False)
            if cg == 3:
                yield from head_rms(pfk, pfkt, gk, t_gk, knb, t_knb, True)
                if c.store_kv is not None:
                    dma_out(c.store_kv[0], pfk, pfkt)
                yield
            if cg == 5:
                transposes(t_tb, [(tb_ap[:, j * 128:(j + 1) * 128], qnb[:, j * 128:(j + 1) * 128]) for j in range(8)],
                           [t_qnb, t_identb], identb)
                qT, qTt = qTs.next()
                c.qT, c.qTt = qT.rearrange("p (j n) -> p j n", n=128), qTt
                evac_copy(qT, tb_ap, [t_tb], [qTt])
                if pfv is not None:
                    dma_out(c.store_kv[1], pfv, pfvt)
                yield
            if cg == 7:
                transposes(t_tb, [(tb_ap[:, j * 128:(j + 1) * 128], knb[:, j * 128:(j + 1) * 128]) for j in range(8)],
                           [t_knb, t_identb], identb)
                evac_copy(kTv[:, :, slot * 128:(slot + 1) * 128], tb_ap.rearrange("p (j n) -> p j n", n=128), [t_tb],
                          [t_kT[slot]])
                yield
        if c.norm_next is not None:
            hT_part(c.norm_next)

    def attnA(c):
        if c.kind == "cache":
            return
        ti = c.ti
        js = [j for j in range(5) if ti - 4 + j >= c.first_ti]
        og, ogt = ogs.next()
        c.og, c.ogt = og, ogt

        def norm_hook(h):
            def f():
                g0, g1, src = {6: (0, 7, psb[5][:, 0:455]), 13: (7, 14, psb[6][:, 0:455]), 15: (14, 16, psb[5][:, 0:130])}[h]
                ng = g1 - g0
                ov = src.rearrange("p (h d) -> p h d", d=65)
                tO = o_ap(g0)[1]
                vop("dve", "reciprocal", [tO], [t_rden], out=rden[:, g0:g1], in_=ov[:, :, 64])
                dv = dtmp[:, g0 * 64:g1 * 64].rearrange("p (h d) -> p h d", d=64)
                vop("dve", "tensor_tensor", [tO, t_rden], [t_dtmp], out=dv, in0=ov[:, :, 0:64],
                    in1=bc(rden[:, g0:g1].unsqueeze(2), [128, ng, 64]), op=ALU.mult)
                vop("pool", "tensor_tensor", [t_dtmp, c.zst], [ogt], out=og[:, g0 * 64:g1 * 64],
                    in0=dtmp[:, g0 * 64:g1 * 64], in1=c.zs[:, g0 * 64:g1 * 64], op=ALU.mult)
            return f

        tiles = []
        for h in range(16):
            rg, j8 = h // 8, h % 8
            ps = slice(rg * 64, rg * 64 + 64)
            oap, ot = o_ap(h)
            for j in js:
                sl = (ti - 4 + j) % KR
                emul = None
                if j >= 3:
                    emul = (Ebv[:, h, j - 3, :], t_E)
                elif j == 0 and not c.sample:
                    emul = (M0, t_M0)
                tiles.append(dict(l=kTv[ps, j8, sl * 128:(sl + 1) * 128], r=c.qT[ps, j8, :], rdq=[t_kT[sl], c.qTt],
                                  V=Vpv[:, sl, h, :], Vt=t_Vp[sl], o=oap, ot=ot, start=(j == js[0]), stop=(j == js[-1]),
                                  emul=emul, masks=[],
                                  after=(norm_hook(h) if (h in (6, 13, 15) and j == js[-1]) else None)))
        yield from attn_stream(tiles, 0.125, sringA, PTs)

    def backA(c, wout_v, t_wout):
        if c.kind == "cache":
            return
        transposes(t_tb, [(tb_ap[:, k * 128:(k + 1) * 128], c.og[:, k * 128:(k + 1) * 128]) for k in range(8)],
                   [c.ogt, t_identb], identb)
        evac_copy(ogT, tb_ap, [t_tb], [t_ogT])
        ogTv = ogT.rearrange("p (k n) -> p k n", n=128)
        yield
        for cg in range(2):
            pap, pt = mmb.next()
            mm(pt, pap, [(ogTv[:, k, :], wout_v[:, k, cg * 512:(cg + 1) * 512]) for k in range(8)], [t_ogT, t_wout[cg]])
            cs = slice(cg * 512, cg * 512 + 512)
            vop("dve", "tensor_tensor", [pt, c.gate[1]], [t_dtmp], out=dtmp[:, cs], in0=pap, in1=c.gate[0][:, cs], op=ALU.mult)
            vop("dve", "tensor_tensor", [t_dtmp, c.xt], [c.xt], out=c.x[:, cs], in0=dtmp[:, cs], in1=c.x[:, cs], op=ALU.add)
            yield
        for dst in c.ydst:
            dma_out(dst, c.x, c.xt)

    def attn_stream(tiles, scale, sring, ptring, lookahead=2):
        items = [tiles[i:i + 4] for i in range(0, len(tiles), 4)]

        def issue_qk(item):
            sap, st_ = sring.next()
            mats = [(sap[:, i * 128:(i + 1) * 128], tl["l"], tl["r"]) for i, tl in enumerate(item)]

            def fn(e, mats=mats):
                for o, l, r in mats:
                    ins = e.matmul(o, lhsT=l, rhs=r, start=True, stop=True)
                return ins
            rd = []
            for tl in item:
                for t_ in tl["rdq"]:
                    if t_ not in rd:
                        rd.append(t_)
            sch.op("pe", fn, rd=rd, wr=[st_])
            return sap, st_

        def softmax_part(item, sap, st_):
            n = len(item)
            PT, PTt = ptring.next()
            act(PT[:, 0:n * 128], sap[:, 0:n * 128], AF.Exp, [st_], [PTt], scale=scale)
            for idx, tl in enumerate(item):
                if tl["emul"] is not None:
                    eap, et = tl["emul"]
                    vop("dve", "tensor_tensor", [PTt, et], [PTt], out=PT[:, idx * 128:(idx + 1) * 128],
                        in0=PT[:, idx * 128:(idx + 1) * 128], in1=eap, op=ALU.mult)
            for idx, tl in enumerate(item):
                for (ps_, c0, c1) in tl["masks"]:
                    vop("pool", "memset", [], [PTt], PT[ps_, idx * 128 + c0:idx * 128 + c1], 0.0)
            return PT, PTt

        pend = [issue_qk(it) for it in items[:lookahead]]
        soft = [softmax_part(items[0], *pend.pop(0))]
        yield
        for i, item in enumerate(items):
            if i + lookahead < len(items):
                pend.append(issue_qk(items[i + lookahead]))
            if i + 1 < len(items):
                soft.append(softmax_part(items[i + 1], *pend.pop(0)))
            PT, PTt = soft.pop(0)
            mats = [(tl["o"], PT[:, idx * 128:(idx + 1) * 128], tl["V"], tl["start"], tl["stop"]) for idx, tl in enumerate(item)]

            def fnpv(e, mats=mats):
                for o, l, r, s0, s1 in mats:
                    ins = e.matmul(o, lhsT=l, rhs=r, start=s0, stop=s1)
                return ins
            rd = [PTt]
            wr = []
            for tl in item:
                if tl["Vt"] not in rd:
                    rd.append(tl["Vt"])
                if tl["ot"] not in wr:
                    wr.append(tl["ot"])
            sch.op("pe", fnpv, rd=rd, wr=wr)
            for tl in item:
                if tl["after"] is not None:
                    tl["after"]()
            yield

    def run_pipeline(tiles, stages, late_first=False):
        ns = len(stages)
        for step in range(len(tiles) + ns - 1):
            gens = []
            for k in (reversed(range(ns)) if late_first else range(ns)):
                n = step - k
                if 0 <= n < len(tiles):
                    if k == 0 and tiles[n].pre is not None:
                        tiles[n].pre()
                    gens.append(stages[k](tiles[n]))
            while gens:
                for g in list(gens):
                    try:
                        next(g)
                    except StopIteration:
                        gens.remove(g)

    tilesA = []
    gti = 0
    for s in range(nseq):
        for t in range(NT):
            c = Ctx()
            c.kind = "prompt"
            c.sample = False
            c.ti = gti + t
            c.first_ti = gti
            c.xsrc = I["xp"][s * S + t * 128:s * S + (t + 1) * 128, :]
            c.ydst = [y0p[s * S + t * 128:s * S + (t + 1) * 128, :]]
            if debug_layers == 1:
                c.ydst = [O["yp"][s * S + t * 128:s * S + (t + 1) * 128, :]]
            c.store_kv = None
            import os
            if t >= NT - NAO and not os.environ.get("DBG_NOSTORE"):
                r0 = s * AR + (t - (NT - NAO)) * 128
                c.store_kv = (O["akp"][r0:r0 + 128, :], O["avp"][r0:r0 + 128, :])
            c.pre = None
            c.premod = (lambda s=s: load_mod(0, s)) if t == 0 else None
            tilesA.append(c)
        gti += NT
    for j in range(4):
        c = Ctx()
        c.kind = "cache"
        c.ti = gti + j
        c.j = j
        c.pre = None
        tilesA.append(c)
    c = Ctx()
    c.kind = "sample"
    c.sample = True
    c.first_ti = gti
    c.ti = gti + 4
    c.xsrc = I["xs"]
    c.ydst = [y0s] if debug_layers != 1 else [O["ys"]]
    c.store_kv = (O["aks"], O["avs"])
    c.pre = None
    c.premod = lambda: load_mod(0, nseq)
    tilesA.append(c)
    import os
    if os.environ.get("DBG_MEM"):
        print("layer A arena used", A.off, "of", A.cap)
    if debug_layers == 0:
        tilesA = []
    if debug_layers < 0:
        tilesA = tilesA[:-debug_layers]
    for c in tilesA:
        c.load_list = []
        c.norm_next = None
    normal = [i for i, c in enumerate(tilesA) if c.kind != "cache"]
    prologue_load, prologue_norm = [], []
    for i in normal:
        (tilesA[i - 2].load_list if i >= 2 else prologue_load).append(tilesA[i])
        if i >= 1:
            tilesA[i - 1].norm_next = tilesA[i]
        else:
            prologue_norm.append(tilesA[i])
    for c in prologue_load:
        load_x(c)
    for c in prologue_norm:
        for _ in norm_part(c):
            pass
        hT_part(c)
    run_pipeline(tilesA, [frontA, attnA, lambda c: backA(c, wA_out_v, t_wAout)], late_first=True)
    do_barrier()

    if debug_layers >= 2:
        A.reset(persist_mark)
        wBin = A.bf16(8 * 1696)
        wBin_v = wBin.rearrange("p (k n) -> p k n", n=1696)
        wuq = A.bf16(3 * 1536)
        wuq_v = wuq.rearrange("p (k n) -> p k n", n=1536)
        wukv = A.bf16(2 * 2048)
        wukv_v = wukv.rearrange("p (k n) -> p k n", n=2048)
        wBout = A.bf16(8 * 1024)
        wBout_v = wBout.rearrange("p (k n) -> p k n", n=1024)
        t_wBin = [T("wBin%d" % i) for i in range(4)]
        t_wuq = [T("wuq%d" % i) for i in range(4)]
        t_wukv = [T("wukv%d" % i) for i in range(4)]
        t_wBout = [T("wBout%d" % i) for i in range(2)]
        wB_mark = A.mark()
        stg = Ring([(A.f32(2048), T("stgB%d" % i)) for i in range(4)])
        load_weight(I["b_w_in"], 8, 1696, wBin_v, t_wBin, 512)
        load_weight(I["b_w_uq"], 3, 1536, wuq_v, t_wuq, 384)
        load_weight(I["b_w_ukv"], 2, 2048, wukv_v, t_wukv, 512)
        load_weight(I["b_w_out"], 8, 1024, wBout_v, t_wBout, 512)
        do_barrier()
        A.reset(wB_mark)
        ZT = max(NT, 1)
        zsB = A.bf16(ZT * 1024)
        zsB_v = zsB.rearrange("p (t n) -> p t n", n=1024)
        t_zsB = [T("zsB%d" % i) for i in range(ZT)]
        cqnT = A.bf16(3 * ZT * 128)
        cqnT_v = cqnT.rearrange("p (k n) -> p k n", n=ZT * 128)
        t_cq = [T("cq%d" % i) for i in range(ZT)]
        ckvnT = A.bf16(2 * NTB * 128)
        ckvnT_v = ckvnT.rearrange("p (k n) -> p k n", n=NTB * 128)
        t_ckv = [T("ckv%d" % i) for i in range(NTB)]
        krb = A.bf16(NTB * 32)
        krb_v = krb.rearrange("p (t n) -> p t n", n=32)
        t_kr = [T("kr%d" % i) for i in range(NTB)]
        kTg = A.bf16(4 * NTB * 128)
        kTg_v = kTg.rearrange("p (h n) -> p h n", n=NTB * 128)
        t_kTg = [T("kTg%d" % i) for i in range(NTB)]
        vop("pool", "memset", [], t_kTg, kTg, 0.0)
        Vg = A.bf16(NTB * 4 * 65)
        Vg_v = Vg.rearrange("p (t h d) -> p t h d", h=4, d=65)
        t_Vg = [T("Vg%d" % i) for i in range(NTB)]
        vop("pool", "memset", [], t_Vg, Vg_v[:, :, :, 64:65], 1.0)
        cosb = A.f32((NT + 1) * 16)
        sinb = A.f32((NT + 1) * 16)
        cos_v = cosb.rearrange("p (t f) -> p t f", f=16)
        sin_v = sinb.rearrange("p (t f) -> p t f", f=16)
        t_cs = T("cossin")
        dma_in(cosb[:, 0:NT * 16], I["cosp"], t_cs, key="l_cos")
        dma_in(sinb[:, 0:NT * 16], I["sinp"], t_cs, key="l_cos")
        dma_in(cos_v[:, NT, :], I["coss"], t_cs, key="l_cos")
        dma_in(sin_v[:, NT, :], I["sins"], t_cs, key="l_cos")
        gateB = A.f32(1024)
        t_gateB = T("gateB")
        work_mark = A.mark()
        SCB = 96.0 ** -0.5
        sbanks = Ring([(psb[3][:, :], T("sB0")), (psb[4][:, :], T("sB1")), (psb[7][:, :], T("sB2"))])
        obanks = Ring([(psb[5][:, :], T("oB0")), (psb[6][:, :], T("oB1"))])

        def ring_f32(name, n, k, parts=128):
            return Ring([(A.f32(n, parts=parts), T("%s%d" % (name, i))) for i in range(k)])

        def ring_bf16(name, n, k, parts=128):
            return Ring([(A.bf16(n, parts=parts), T("%s%d" % (name, i))) for i in range(k)])

        def rope(src3, nh, csi, out1, out2, rd, wr, rts):
            x1, x2 = src3[:, :, 0:16], src3[:, :, 16:32]
            cosx = bc(cos_v[:, csi, :].unsqueeze(1), [128, nh, 16])
            sinx = bc(sin_v[:, csi, :].unsqueeze(1), [128, nh, 16])
            rt_, t_rt = rts.next()
            a1 = rt_[:, 0:nh * 16].rearrange("p (h f) -> p h f", f=16)
            a2 = rt_[:, 64:64 + nh * 16].rearrange("p (h f) -> p h f", f=16)
            b1 = rt_[:, 128:128 + nh * 16].rearrange("p (h f) -> p h f", f=16)
            b2 = rt_[:, 192:192 + nh * 16].rearrange("p (h f) -> p h f", f=16)
            vop("pool", "tensor_tensor", rd + [t_cs], [t_rt], out=a1, in0=x1, in1=cosx, op=ALU.mult)
            vop("pool", "tensor_tensor", rd + [t_cs], [t_rt], out=a2, in0=x2, in1=sinx, op=ALU.mult)
            vop("pool", "tensor_tensor", rd + [t_cs], [t_rt], out=b1, in0=x2, in1=cosx, op=ALU.mult)
            vop("pool", "tensor_tensor", rd + [t_cs], [t_rt], out=b2, in0=x1, in1=sinx, op=ALU.mult)
            vop("pool", "tensor_tensor", [t_rt], wr, out=out1, in0=a1, in1=a2, op=ALU.subtract)
            vop("pool", "tensor_tensor", [t_rt], wr, out=out2, in0=b1, in1=b2, op=ALU.add)

        W = Ctx()

        def sweep1_bufs(s):
            do_barrier()
            A.reset(work_mark)
            W.xs = ring_f32("x1_", 1024, 3)
            W.dt = ring_f32("dt1_", 1024, 2)
            W.hb = ring_bf16("hb1_", 1024, 2)
            W.hT = ring_bf16("hT1_", 1024, 2)
            W.pj = ring_f32("pj", 672, 3)
            W.cqb = ring_bf16("cqb", 384, 2)
            W.ckvf = ring_f32("ckvf", 256, 3)
            W.ckvb = ring_bf16("ckvb", 256, 2)
            W.krn = ring_f32("krn", 32, 2)
            W.krf = ring_f32("krf", 32, 3)
            W.rt = ring_f32("rt", 256, 2)
            W.ss = ring_f32("ss1_", 4, 3)
            W.st3 = ring_f32("st3_", 8, 3)
            W.bc3 = A.f32(2048)
            W.t_bc3 = T("bc3B")
            W.G1 = A.f32(1024)
            W.t_G1 = T("G1B")
            dma_in(W.bc3, bc(modd[NS + s:NS + s + 1, 0:2048], [128, 2048]), W.t_bc3)
            dma_in(gateB, bc(modd[NS + s:NS + s + 1, 2048:3072], [128, 1024]), t_gateB)
            dma_in(W.G1, bc(I["norm_g"][1:2, :], [128, 1024]), W.t_G1)
            vop("dve", "scalar_tensor_tensor", [W.t_bc3, W.t_G1], [W.t_G1], out=W.G1, in0=W.bc3[:, 1024:2048], scalar=1.0, in1=W.G1,
                op0=ALU.add, op1=ALU.mult)

        def s1_load(c):
            if c.kind == "cache":
                return
            c.x, c.xt = W.xs.next()
            dma_in(c.x, c.ysrc, c.xt)
            return
            yield

        def s1_norm(c):
            if c.kind == "cache":
                return
            xap, xt = c.x, c.xt
            ss, sst = W.ss.next()
            hb_, hbt = W.hb.next()
            dt_, dtt = W.dt.next()
            act(hb_, xap, AF.Square, [xt], [hbt, sst], accum_out=ss[:, 0:1])
            rstd_from(ss[:, 0:1], 1, sst, 1.0 / D, ss[:, 1:2])
            vop("dve", "scalar_tensor_tensor", [xt, sst, W.t_G1], [dtt], out=dt_, in0=xap, scalar=ss[:, 0:1], in1=W.G1,
                op0=ALU.mult, op1=ALU.mult)
            vop("dve", "tensor_tensor", [dtt, W.t_bc3], [hbt], out=hb_, in0=dt_, in1=W.bc3[:, 0:1024], op=ALU.add)
            c.hb_, c.hbt = hb_, hbt
            yield

        def s1_tr(c):
            if c.kind == "cache":
                return
            hb_, hbt = c.hb_, c.hbt
            transposes(t_tb, [(tb_ap[:, k * 128:(k + 1) * 128], hb_[:, k * 128:(k + 1) * 128]) for k in range(8)],
                       [hbt, t_identb], identb)
            hT, hTt = W.hT.next()
            evac_copy(hT, tb_ap, [t_tb], [hTt])
            c.hT, c.hTt = hT.rearrange("p (k n) -> p k n", n=128), hTt
            yield

        def s1_mm(c):
            if c.kind == "cache":
                return
            c.pj, c.pjt = W.pj.next()
            for cg in range(4):
                w_ = 512 if cg < 3 else 160
                pap, pt = mmb.next()
                mm(pt, pap[:, 0:w_], [(c.hT[:, k, :], wBin_v[:, k, cg * 512:cg * 512 + w_]) for k in range(8)], [c.hTt, t_wBin[cg]])
                if cg == 0:
                    z0 = (pap, pt)
                elif cg == 1:
                    act(zsB_v[:, c.zt, 0:512], z0[0], AF.Silu, [z0[1]], [t_zsB[c.zt]])
                    act(zsB_v[:, c.zt, 512:1024], pap, AF.Silu, [pt], [t_zsB[c.zt]])
                else:
                    act(c.pj[:, (cg - 2) * 512:(cg - 2) * 512 + w_], pap[:, 0:w_], AF.Copy, [pt], [c.pjt])
                yield

        def s1_e1(c):
            if c.kind == "cache":
                kt = c.kt
                c.cf, c.cft = W.ckvf.next()
                dma_in(c.cf, I["cckv"][kt * 128:(kt + 1) * 128, :], c.cft)
                c.kr_, c.krt = W.krf.next()
                dma_in(c.kr_, I["ckr"][kt * 128:(kt + 1) * 128, :], c.krt)
                c.ckvb, c.ckvbt = W.ckvb.next()
                vop("pool", "tensor_copy", [c.cft], [c.ckvbt], out=c.ckvb, in_=c.cf)
                vop("pool", "tensor_copy", [c.krt], [t_kr[c.kt]], out=krb_v[:, c.kt, :], in_=c.kr_)
                return
            pj, pjt = c.pj, c.pjt
            dt_, dtt = W.dt.next()
            st3, t_st3 = W.st3.next()
            vop("dve", "tensor_tensor", [pjt], [dtt], out=dt_[:, 0:672], in0=pj, in1=pj, op=ALU.mult)
            vop("dve", "tensor_reduce", [dtt], [t_st3], out=st3[:, 0:1], in_=dt_[:, 0:384], axis=AX.X, op=ALU.add)
            vop("dve", "tensor_reduce", [dtt], [t_st3], out=st3[:, 1:2], in_=dt_[:, 384:640], axis=AX.X, op=ALU.add)
            vop("dve", "tensor_reduce", [dtt], [t_st3], out=st3[:, 2:3], in_=dt_[:, 640:672], axis=AX.X, op=ALU.add)
            vop("dve", "tensor_tensor", [t_st3, t_invn], [t_st3], out=st3[:, 0:3], in0=st3[:, 0:3], in1=invn3, op=ALU.mult)
            rstd_from(st3[:, 0:3], 3, t_st3, 1.0, st3[:, 4:7])
            yield
            c.cqb, c.cqbt = W.cqb.next()
            vop("dve", "scalar_tensor_tensor", [pjt, t_st3, t_gcq], [c.cqbt], out=c.cqb, in0=pj[:, 0:384], scalar=st3[:, 0:1], in1=gcq,
                op0=ALU.mult, op1=ALU.mult)
            c.cf, c.cft = W.ckvf.next()
            vop("dve", "scalar_tensor_tensor", [pjt, t_st3, t_gckv], [c.cft], out=c.cf, in0=pj[:, 384:640], scalar=st3[:, 1:2], in1=gckv,
                op0=ALU.mult, op1=ALU.mult)
            dma_out(c.ckv_dst, c.cf, c.cft)
            c.ckvb, c.ckvbt = W.ckvb.next()
            vop("pool", "tensor_copy", [c.cft], [c.ckvbt], out=c.ckvb, in_=c.cf)
            c.krn, c.krnt = W.krn.next()
            vop("dve", "scalar_tensor_tensor", [pjt, t_st3, t_gkr], [c.krnt], out=c.krn, in0=pj[:, 640:672], scalar=st3[:, 2:3], in1=gkr,
                op0=ALU.mult, op1=ALU.mult)
            yield

        def s1_e2(c):
            kt, zt = c.kt, c.zt
            if c.kind != "cache":
                transposes(t_tb, [(tb_ap[:, k * 128:(k + 1) * 128], c.cqb[:, k * 128:(k + 1) * 128]) for k in range(3)],
                           [c.cqbt, t_identb], identb)
                evac_copy(cqnT_v[:, :, zt * 128:(zt + 1) * 128], tb_ap[:, 0:384].rearrange("p (k n) -> p k n", n=128), [t_tb], [t_cq[zt]])
                yield
            transposes(t_tb, [(tb_ap[:, k * 128:(k + 1) * 128], c.ckvb[:, k * 128:(k + 1) * 128]) for k in range(2)],
                       [c.ckvbt, t_identb], identb)
            evac_copy(ckvnT_v[:, :, kt * 128:(kt + 1) * 128], tb_ap[:, 0:256].rearrange("p (k n) -> p k n", n=128), [t_tb], [t_ckv[kt]])
            yield
            if c.kind != "cache":
                kr_, krt = W.krf.next()
                k3 = c.krn.rearrange("p (h f) -> p h f", h=1)
                o3 = kr_.rearrange("p (h f) -> p h f", h=1)
                rope(k3, 1, c.cs, o3[:, :, 0:16], o3[:, :, 16:32], [c.krnt], [krt], W.rt)
                dma_out(c.kr_dst, kr_, krt)
                vop("pool", "tensor_copy", [krt], [t_kr[kt]], out=krb_v[:, kt, :], in_=kr_)
                yield

        def sweep2_bufs():
            do_barrier()
            A.reset(work_mark)
            W.qg = ring_f32("qg", 384, 3)
            W.kf = ring_f32("kf", 256, 3)
            W.dsq = ring_f32("dsq", 384, 2)
            W.qnb = ring_bf16("qnb", 384, 3)
            W.knb = ring_bf16("knb", 384, 3)
            W.qT = ring_bf16("qTB", 512, 4)
            for qap_, qt_ in W.qT.items:
                vop("pool", "memset", [], [qt_], qap_, 0.0)
            W.PT = ring_bf16("PTB", 512, 4)
            W.don = ring_f32("don", 256, 2)
            W.rt = ring_f32("rt", 256, 2)
            W.st8 = ring_f32("st8_", 16, 3)
            W.st4 = ring_f32("st4_", 8, 3)
            W.rden = ring_f32("rden", 4, 2)

        def p2_mm(c):
            g, kt, zt = c.g, c.kt, c.zt
            if c.kind != "cache":
                pap, pt = mmb.next()
                mm(pt, pap[:, 0:384], [(cqnT_v[:, k, zt * 128:(zt + 1) * 128], wuq_v[:, k, g * 384:(g + 1) * 384]) for k in range(3)],
                   [t_cq[zt], t_wuq[g]])
                c.qg, c.qgt = W.qg.next()
                act(c.qg, pap[:, 0:384], AF.Copy, [pt], [c.qgt])
                yield
            pap, pt = mmb.next()
            mm(pt, pap, [(ckvnT_v[:, k, kt * 128:(kt + 1) * 128], wukv_v[:, k, g * 512:(g + 1) * 512]) for k in range(2)],
               [t_ckv[kt], t_wukv[g]])
            p3 = pap.rearrange("p (h d) -> p h d", d=128)
            c.kf, c.kft = W.kf.next()
            act(c.kf.rearrange("p (h d) -> p h d", d=64), p3[:, :, 0:64], AF.Copy, [pt], [c.kft])
            act(Vg_v[:, kt, :, 0:64], p3[:, :, 64:128], AF.Copy, [pt], [t_Vg[kt]])
            yield

        def p2_norm(c):
            kt = c.kt
            if c.kind != "cache":
                qg, qgt = c.qg, c.qgt
                q3 = qg.rearrange("p (h d) -> p h d", d=96)
                ds, dst_ = W.dsq.next()
                d3 = ds.rearrange("p (h d) -> p h d", d=96)
                st8, t_st8 = W.st8.next()
                vop("dve", "tensor_tensor", [qgt], [dst_], out=ds, in0=qg, in1=qg, op=ALU.mult)
                vop("dve", "tensor_reduce", [dst_], [t_st8], out=st8[:, 0:4], in_=d3[:, :, 0:64], axis=AX.X, op=ALU.add)
                vop("dve", "tensor_reduce", [dst_], [t_st8], out=st8[:, 4:8], in_=d3[:, :, 64:96], axis=AX.X, op=ALU.add)
                vop("dve", "tensor_tensor", [t_st8, t_invn], [t_st8], out=st8[:, 0:8], in0=st8[:, 0:8], in1=invn8, op=ALU.mult)
                rstd_from(st8[:, 0:8], 8, t_st8, 1.0, st8[:, 8:16])
                c.st8, c.t_st8 = st8, t_st8
            kf, kft = c.kf, c.kft
            ds, dst_ = W.dsq.next()
            st4, t_st4 = W.st4.next()
            vop("dve", "tensor_tensor", [kft], [dst_], out=ds[:, 0:256], in0=kf, in1=kf, op=ALU.mult)
            vop("dve", "tensor_reduce", [dst_], [t_st4], out=st4[:, 0:4], in_=ds[:, 0:256].rearrange("p (h d) -> p h d", d=64),
                axis=AX.X, op=ALU.add)
            rstd_from(st4[:, 0:4], 4, t_st4, 1.0 / 64, st4[:, 4:8])
            c.st4, c.t_st4 = st4, t_st4
            yield
            if c.kind != "cache":
                st8, t_st8 = c.st8, c.t_st8
                c.qnb, c.qnbt = W.qnb.next()
                qn3 = c.qnb.rearrange("p (h d) -> p h d", d=96)
                vop("dve", "tensor_tensor", [qgt, t_st8], [qgt], out=q3[:, :, 0:64], in0=q3[:, :, 0:64],
                    in1=bc(st8[:, 0:4].unsqueeze(2), [128, 4, 64]), op=ALU.mult)
                vop("dve", "tensor_tensor", [qgt, t_gqn], [c.qnbt], out=qn3[:, :, 0:64], in0=q3[:, :, 0:64],
                    in1=bc(gqn.unsqueeze(1), [128, 4, 64]), op=ALU.mult)
                vop("dve", "tensor_tensor", [qgt, t_st8], [qgt], out=q3[:, :, 64:96], in0=q3[:, :, 64:96],
                    in1=bc(st8[:, 4:8].unsqueeze(2), [128, 4, 32]), op=ALU.mult)
                vop("dve", "tensor_tensor", [qgt, t_gqr], [qgt], out=q3[:, :, 64:96], in0=q3[:, :, 64:96],
                    in1=bc(gqr.unsqueeze(1), [128, 4, 32]), op=ALU.mult)
                yield
                rope(q3[:, :, 64:96], 4, c.cs, qn3[:, :, 64:80], qn3[:, :, 80:96], [qgt], [c.qnbt], W.rt)
                yield
            k3 = kf.rearrange("p (h d) -> p h d", d=64)
            c.knb, c.knbt = W.knb.next()
            kn3 = c.knb.rearrange("p (h d) -> p h d", d=96)
            vop("dve", "tensor_tensor", [kft, c.t_st4], [kft], out=k3, in0=k3, in1=bc(c.st4[:, 0:4].unsqueeze(2), [128, 4, 64]), op=ALU.mult)
            vop("dve", "tensor_tensor", [kft, t_gkn], [c.knbt], out=kn3[:, :, 0:64], in0=k3, in1=bc(gkn.unsqueeze(1), [128, 4, 64]),
                op=ALU.mult)
            vop("pool", "tensor_copy", [t_kr[kt]], [c.knbt], out=kn3[:, :, 64:96], in_=bc(krb_v[:, kt, :].unsqueeze(1), [128, 4, 32]))
            yield

        def p2_tr(c):
            kt = c.kt
            if c.kind != "cache":
                qn3 = c.qnb.rearrange("p (h d) -> p h d", d=96)
                transposes(t_tb, [(tb_ap[0:96, h * 128:(h + 1) * 128], qn3[:, h, :]) for h in range(4)], [c.qnbt, t_identb], identb)
                qT, qTt = W.qT.next()
                c.qT, c.qTt = qT.rearrange("p (h n) -> p h n", n=128), qTt
                evac_copy(qT[0:96, :], tb_ap[0:96, 0:512], [t_tb], [qTt])
                yield
            kn3 = c.knb.rearrange("p (h d) -> p h d", d=96)
            transposes(t_tb, [(tb_ap[0:96, h * 128:(h + 1) * 128], kn3[:, h, :]) for h in range(4)], [c.knbt, t_identb], identb)
            evac_copy(kTg_v[0:96, :, kt * 128:(kt + 1) * 128], tb_ap[0:96, 0:512].rearrange("p (h n) -> p h n", n=128), [t_tb], [t_kTg[kt]])
            yield

        def a2(c):
            if c.kind == "cache":
                return
            g, kt, zt = c.g, c.kt, c.zt
            kts = list(range(c.kt0, kt + 1))
            oap, ot = obanks.next()

            def norm_hook():
                ov = oap[:, 0:260].rearrange("p (h d) -> p h d", d=65)
                rden, t_rden = W.rden.next()
                don, t_don = W.don.next()
                vop("dve", "reciprocal", [ot], [t_rden], out=rden[:, 0:4], in_=ov[:, :, 64])
                vop("dve", "tensor_tensor", [ot, t_rden], [t_don], out=don.rearrange("p (h d) -> p h d", d=64),
                    in0=ov[:, :, 0:64], in1=bc(rden[:, 0:4].unsqueeze(2), [128, 4, 64]), op=ALU.mult)
                vop("pool", "tensor_tensor", [t_don, t_zsB[zt]], [t_zsB[zt]], out=zsB_v[:, zt, g * 256:(g + 1) * 256],
                    in0=don, in1=zsB_v[:, zt, g * 256:(g + 1) * 256], op=ALU.mult)

            tiles = []
            for hh in range(4):
                for k in kts:
                    masks = []
                    if k == kt:
                        masks = ([(slice(32, 64), 0, 128), (slice(64, 128), 0, 128)] if c.kind == "sample"
                                 else [(slice(64, 128), 0, 64)])
                    tiles.append(dict(l=kTg_v[:, hh, k * 128:(k + 1) * 128], r=c.qT[:, hh, :], rdq=[t_kTg[k], c.qTt],
                                      V=Vg_v[:, k, hh, :], Vt=t_Vg[k], o=oap[:, hh * 65:(hh + 1) * 65], ot=ot,
                                      start=(k == kts[0]), stop=(k == kts[-1]), emul=None, masks=masks,
                                      after=(norm_hook if (hh == 3 and k == kts[-1]) else None)))
            yield from attn_stream(tiles, SCB, sbanks, W.PT)

        def sweep3_bufs():
            do_barrier()
            A.reset(work_mark)
            W.xs = ring_f32("x3_", 1024, 4)
            W.ogT = ring_bf16("ogT", 1024, 3)
            W.dt = ring_f32("dt3_", 1024, 2)

        def b3_load(c):
            if c.kind == "cache":
                return
            c.x, c.xt = W.xs.next()
            dma_in(c.x, c.ysrc, c.xt)
            return
            yield

        def b3a(c):
            if c.kind == "cache":
                return
            zt = c.zt
            transposes(t_tb, [(tb_ap[:, k * 128:(k + 1) * 128], zsB_v[:, zt, k * 128:(k + 1) * 128]) for k in range(8)],
                       [t_zsB[zt], t_identb], identb)
            ogT, ogTt = W.ogT.next()
            c.ogT, c.ogTt = ogT.rearrange("p (k n) -> p k n", n=128), ogTt
            evac_copy(ogT, tb_ap, [t_tb], [ogTt])
            yield

        def b3b(c):
            if c.kind == "cache":
                return
            dt_, dtt = W.dt.next()
            for cg in range(2):
                pap, pt = mmb.next()
                mm(pt, pap, [(c.ogT[:, k, :], wBout_v[:, k, cg * 512:(cg + 1) * 512]) for k in range(8)], [c.ogTt, t_wBout[cg]])
                cs_ = slice(cg * 512, cg * 512 + 512)
                vop("dve", "tensor_tensor", [pt, t_gateB], [dtt], out=dt_[:, cs_], in0=pap, in1=gateB[:, cs_], op=ALU.mult)
                vop("dve", "tensor_tensor", [dtt, c.xt], [c.xt], out=c.x[:, cs_], in0=dt_[:, cs_], in1=c.x[:, cs_], op=ALU.add)
                yield
            dma_out(c.y_dst, c.x, c.xt)

        import os
        DBG_B = os.environ.get("DBG_B", "")

        def run_seqB(s, tiles):
            if DBG_B == "w" or (DBG_B and s > 0):
                return
            sweep1_bufs(s)
            run_pipeline(tiles, [s1_load, s1_norm, s1_tr, s1_mm, s1_e1, s1_e2])
            if DBG_B == "s1":
                return
            sweep2_bufs()
            for g in range(4):
                for c in tiles:
                    c.g = g
                run_pipeline(tiles, [p2_mm, p2_norm, p2_tr, a2])
                if DBG_B == "s2":
                    return
            sweep3_bufs()
            run_pipeline(tiles, [b3_load, b3a, b3b])

        for s in range(nseq):
            tiles = []
            for t in range(NT):
                c = Ctx()
                c.kind = "prompt"
                c.pre = None
                c.kt, c.zt, c.cs, c.kt0 = t, t, t, 0
                r = slice(s * S + t * 128, s * S + (t + 1) * 128)
                c.ysrc = y0p[r, :]
                c.ckv_dst, c.kr_dst, c.y_dst = O["bcp"][r, :], O["brp"][r, :], O["yp"][r, :]
                tiles.append(c)
            run_seqB(s, tiles)
        tiles = []
        for t in range(NTC):
            c = Ctx()
            c.kind = "cache"
            c.pre = None
            c.kt = t
            c.zt = 0
            tiles.append(c)
        c = Ctx()
        c.kind = "sample"
        c.pre = None
        c.kt, c.zt, c.cs, c.kt0 = NTC, 0, NT, 0
        c.ysrc = y0s
        c.ckv_dst, c.kr_dst, c.y_dst = O["bcs"], O["brs"], O["ys"]
        tiles.append(c)
        run_seqB(nseq, tiles)

    sch.emit(nc, stack)
    stack.close()
    return nc


def build_layer_B(env):
    raise NotImplementedError


ROPE_THETA = 10000.0
B_COLS = np.concatenate([np.arange(672, 1696), np.arange(0, 672)])


def rope_tables(pos):
    inv = (np.float32(ROPE_THETA) ** (-np.arange(16, dtype=np.float32) / np.float32(16))).astype(np.float32)
    ang = pos.astype(np.float32)[:, None] * inv[None, :]
    return np.cos(ang).astype(np.float32), np.sin(ang).astype(np.float32)


def shared_inputs(inp, S, PAST):
    f = lambda a: np.ascontiguousarray(np.asarray(a, dtype=np.float32))
    sh = {}
    sh["norm_g"] = f(inp["norm_g"])
    sh["ada_w"] = f(inp["ada_w"]).reshape(2 * D, 3 * D)
    sh["ada_b"] = f(inp["ada_b"])
    w = f(inp["a_w_in"])[0]
    w = np.concatenate([w[:, 0:1024][:, PERM_COLS], w[:, 1024:2048][:, PERM_COLS], w[:, 2048:]], axis=1)
    sh["a_w_in"] = f(w)
    sh["a_g_q"] = f(inp["a_g_q"])
    sh["a_g_k"] = f(inp["a_g_k"])
    tab = f(inp["a_rel_bias"])[0]
    ki = np.arange(128)[:, None, None]
    jj = np.arange(2)[None, :, None]
    qi = np.arange(128)[None, None, :]
    rel = np.clip((4 - (3 + jj)) * 128 + qi - ki, -128, 128) + 128
    sh["erel"] = f(np.transpose(tab[:, rel], (1, 0, 2, 3)).reshape(128, 16 * 2 * 128))
    sh["cbias"] = f(tab[:, 256][None, :])
    sh["a_w_out"] = f(inp["a_w_out"])[0]
    sh["b_w_in"] = f(f(inp["b_w_in"])[0][:, B_COLS])
    for k in ("b_g_cq", "b_g_ckv", "b_g_qn", "b_g_qr", "b_g_kn", "b_g_kr"):
        sh[k] = f(inp[k])
    sh["b_w_uq"] = f(inp["b_w_uq"])[0]
    sh["b_w_ukv"] = f(inp["b_w_ukv"])[0]
    sh["b_w_out"] = f(inp["b_w_out"])[0]
    cp, sp_ = rope_tables(np.arange(S))
    NT = S // 128
    sh["cosp"] = f(cp.reshape(NT, 128, 16).transpose(1, 0, 2).reshape(128, NT * 16))
    sh["sinp"] = f(sp_.reshape(NT, 128, 16).transpose(1, 0, 2).reshape(128, NT * 16))
    sh["coss"], sh["sins"] = rope_tables(PAST + np.arange(128))
    sh["ident"] = np.eye(128, dtype=np.float32)
    return sh


def core_inputs(inp, sh, core, nseq, S, PAST):
    f = lambda a: np.ascontiguousarray(np.asarray(a, dtype=np.float32))
    m = dict(sh)
    m["xp"] = f(inp["x_prompt"][core * nseq:(core + 1) * nseq]).reshape(nseq * S, D)
    xs = np.zeros((128, D), np.float32)
    xs[:32] = np.asarray(inp["x_sample"][core])
    m["xs"] = xs
    m["ck"] = f(np.asarray(inp["cache_a_k"])[0, core].reshape(512, D)[:, PERM_COLS])
    m["cv"] = f(np.asarray(inp["cache_a_v"])[0, core].reshape(512, D))
    m["cckv"] = f(np.asarray(inp["cache_mla_ckv"])[0, core])
    m["ckr"] = f(np.asarray(inp["cache_mla_krope"])[0, core])
    c = np.concatenate([np.asarray(inp["c_prompt"])[core * nseq:(core + 1) * nseq], np.asarray(inp["c_sample"])[core:core + 1]], 0)
    NS = nseq + 1
    m["scT"] = f(c.T.reshape(8, 128, NS).transpose(1, 0, 2).reshape(128, 8 * NS))
    return m


_NC_CACHE = {}


def run_cores(inp, ncores, nseq, S, PAST, debug_layers=2):
    key = (nseq, S, PAST, debug_layers)
    if key not in _NC_CACHE:
        _NC_CACHE[key] = build(nseq, S, PAST, debug_layers)
    nc = _NC_CACHE[key]
    sh = shared_inputs(inp, S, PAST)
    in_maps = [core_inputs(inp, sh, c, nseq, S, PAST) for c in range(ncores)]
    res = run_bass_kernel_spmd(nc, in_maps, core_ids=list(range(ncores)))
    return res.results


def assemble(results, ncores, nseq, S):
    AR = min(512, S)
    inv = np.empty(1024, np.int64)
    inv[PERM_COLS] = np.arange(1024)
    cat = lambda k: np.concatenate([r[k] for r in results], 0)
    y_p = cat("yp").reshape(ncores * nseq, S, D)
    y_s = np.stack([r["ys"][:32] for r in results], 0)
    akp = cat("akp")[:, inv].reshape(1, ncores * nseq, AR, 16, 64)
    avp = cat("avp").reshape(1, ncores * nseq, AR, 16, 64)
    aks = np.stack([r["aks"][:32][:, inv] for r in results], 0).reshape(1, ncores, 32, 16, 64)
    avs = np.stack([r["avs"][:32] for r in results], 0).reshape(1, ncores, 32, 16, 64)
    bcp = cat("bcp").reshape(1, ncores * nseq, S, 256)
    brp = cat("brp").reshape(1, ncores * nseq, S, 32)
    bcs = np.stack([r["bcs"][:32] for r in results], 0).reshape(1, ncores, 32, 256)
    brs = np.stack([r["brs"][:32] for r in results], 0).reshape(1, ncores, 32, 32)
    return tuple(np.ascontiguousarray(a, dtype=np.float32) for a in (y_p, y_s, akp, avp, aks, avs, bcp, brp, bcs, brs))


def kernel(**inputs):
    res = run_cores(inputs, NCORES, 4, 2048, 2048)
    return assemble(res, NCORES, 4, 2048)
```

```python
import numpy as np
import concourse.bass as bass
import concourse.mybir as mybir
from concourse.bass_utils import run_bass_kernel_spmd
from contextlib import ExitStack

F32 = mybir.dt.float32
BF16 = mybir.dt.bfloat16
AF = mybir.ActivationFunctionType
ALU = mybir.AluOpType
AX = mybir.AxisListType

D = 1024
EPS = 1e-6
NCORES = 8


class T:
    __slots__ = ("name", "w", "r")

    def __init__(self, name=""):
        self.name = name
        self.w = None
        self.r = {}


class Op:
    __slots__ = ("eng", "fn", "deps", "dma", "val", "needed", "kind")


class Sched:
    ENG = ("pe", "act", "dve", "pool", "sp")

    def __init__(self):
        self.q = {e: [] for e in self.ENG}
        self.dmacnt = {}
        self.lastdma = {}
        self.nbar = 0

    def op(self, eng, fn, rd=(), wr=(), dma=None, kind="c", extra_deps=()):
        o = Op()
        o.eng, o.fn, o.dma, o.kind = eng, fn, dma, kind
        o.needed = False
        o.val = None
        deps = list(extra_deps)
        for t in rd:
            if t.w is not None:
                deps.append((t.w, 0))
        for t in wr:
            if t.w is not None:
                deps.append((t.w, 1))
            for r in t.r.values():
                if r is not o:
                    deps.append((r, 2))
        o.deps = deps
        for d, k in deps:
            if d.dma is None:
                if d.eng != eng or eng != "pe":
                    d.needed = True
        wrs = set(id(t) for t in wr)
        for t in wr:
            t.w = o
            t.r = {}
        key = dma if dma is not None else eng
        for t in rd:
            if id(t) not in wrs:
                t.r[key] = o
        if dma is not None:
            c = self.dmacnt.get(dma, 0) + 16
            self.dmacnt[dma] = c
            o.val = c
            self.lastdma[dma] = o
        self.q[eng].append(o)
        return o

    def barrier(self, markers, pe_wr=()):
        bt = [T("bar%d_%s" % (self.nbar, e)) for e in self.ENG]
        self.nbar += 1
        alld = [(o, 0) for o in self.lastdma.values()]
        for e, t in zip(self.ENG, bt):
            if e == "sp":
                self.op("sp", None, wr=[t], kind="seminc", extra_deps=alld)
            elif e == "pool":
                self.op("pool", markers[e], wr=[t], extra_deps=alld)
            elif e == "pe":
                self.op(e, markers[e], wr=[t] + list(pe_wr))
            else:
                self.op(e, markers[e], wr=[t])
        for e in self.ENG:
            self.op(e, None, rd=bt, kind="wait")

    def emit(self, nc, stack):
        for e in self.ENG:
            cnt = 0
            for o in self.q[e]:
                if o.dma is None:
                    if o.needed:
                        cnt += 1
                    o.val = cnt
        sems = {}
        for e in self.ENG:
            sems[e] = stack.enter_context(nc.semaphore("s_" + e))
        for k in self.dmacnt:
            sems[k] = stack.enter_context(nc.semaphore("d_" + k))
        self.sems = sems
        block = stack.enter_context(nc.Block())
        final = [(k, v) for k, v in self.dmacnt.items()]

        def run(eng, ename):
            waited = {}
            for o in self.q[ename]:
                for d, kind in o.deps:
                    key = d.dma if d.dma is not None else d.eng
                    if d.dma is None and d.eng == ename and ename == "pe":
                        continue
                    if waited.get(key, 0) >= d.val:
                        continue
                    eng.wait_ge(sems[key], d.val)
                    waited[key] = d.val
                if o.kind == "wait":
                    continue
                if o.kind == "seminc":
                    if o.needed:
                        eng.sem_inc(sems[ename], 1)
                    continue
                ins = o.fn(eng)
                if o.dma is not None:
                    ins.then_inc(sems[o.dma], 16)
                elif o.needed:
                    ins.then_inc(sems[ename], 1)
            if ename in ("sp", "pool"):
                for k, v in final:
                    if waited.get(k, 0) < v:
                        eng.wait_ge(sems[k], v)

        @block.sync
        def _(e):
            run(e, "sp")

        @block.scalar
        def _(e):
            run(e, "act")

        @block.vector
        def _(e):
            run(e, "dve")

        @block.gpsimd
        def _(e):
            run(e, "pool")

        @block.tensor
        def _(e):
            run(e, "pe")


class Ring:
    def __init__(self, items):
        self.items = items
        self.i = 0

    def next(self):
        it = self.items[self.i % len(self.items)]
        self.i += 1
        return it


class Arena:
    def __init__(self, ar, nbytes):
        self.ar = ar
        self.cap = nbytes
        self.off = 0
        self.marks = []

    def _take(self, nbytes):
        nbytes = (nbytes + 63) // 64 * 64
        o = self.off
        self.off += nbytes
        assert self.off <= self.cap, "SBUF arena overflow %d > %d" % (self.off, self.cap)
        return o

    def f32(self, n, parts=128):
        o = self._take(n * 4)
        return self.ar[0:parts, o // 4:o // 4 + n]

    def bf16(self, n, parts=128):
        o = self._take(n * 2)
        return self.ar[0:parts, o // 4:o // 4 + (n + 1) // 2].bitcast(BF16)[:, 0:n]

    def mark(self):
        return self.off

    def reset(self, m):
        self.off = m


PERM_COLS = np.array([(j8 + 8 * rg) * 64 + d for j8 in range(8) for rg in range(2) for d in range(64)])


def build(nseq, S, PAST, debug_layers=2):
    NT = S // 128
    NS = nseq + 1
    NTC = PAST // 128
    AR = min(512, S)
    NAO = AR // 128
    NTB = max(NT, NTC + 1)
    nc = bass.Bass("TRN2", target_bir_lowering=False)

    def din(name, shape):
        return nc.dram_tensor(name, list(shape), F32, kind="ExternalInput").ap()

    def dout(name, shape):
        return nc.dram_tensor(name, list(shape), F32, kind="ExternalOutput").ap()

    def dscr(name, shape):
        return nc.dram_tensor(name, list(shape), F32).ap()

    I = {}
    for name, shape in [
        ("xp", (nseq * S, D)), ("xs", (128, D)), ("ck", (512, D)), ("cv", (512, D)),
        ("cckv", (PAST, 256)), ("ckr", (PAST, 32)), ("scT", (128, 8 * NS)),
        ("norm_g", (2, D)), ("ada_w", (2 * D, 3 * D)), ("ada_b", (2, 3 * D)),
        ("a_w_in", (D, 4 * D)), ("a_g_q", (1, 64)), ("a_g_k", (1, 64)),
        ("erel", (128, 16 * 2 * 128)), ("cbias", (1, 16)), ("a_w_out", (D, D)),
        ("b_w_in", (D, 1696)), ("b_g_cq", (1, 384)), ("b_w_uq", (384, 1536)), ("b_g_ckv", (1, 256)),
        ("b_w_ukv", (256, 2048)), ("b_g_qn", (1, 64)), ("b_g_qr", (1, 32)), ("b_g_kn", (1, 64)),
        ("b_g_kr", (1, 32)), ("b_w_out", (D, D)),
        ("cosp", (128, NT * 16)), ("sinp", (128, NT * 16)), ("coss", (128, 16)), ("sins", (128, 16)),
        ("ident", (128, 128)),
    ]:
        I[name] = din(name, shape)
    O = {}
    for name, shape in [
        ("yp", (nseq * S, D)), ("ys", (128, D)), ("akp", (nseq * AR, D)), ("avp", (nseq * AR, D)),
        ("aks", (128, D)), ("avs", (128, D)), ("bcp", (nseq * S, 256)), ("brp", (nseq * S, 32)),
        ("bcs", (128, 256)), ("brs", (128, 32)),
    ]:
        O[name] = dout(name, shape)
    y0p = dscr("y0p", (nseq * S, D))
    y0s = dscr("y0s", (128, D))
    modd = dscr("modd", (2 * NS, 3 * D))

    sch = Sched()
    stack = ExitStack()
    ARENA_BYTES = 207 * 1024
    arena_t = stack.enter_context(nc.sbuf_tensor("arena", [128, ARENA_BYTES // 4], F32))
    A = Arena(arena_t, ARENA_BYTES)
    psb = [stack.enter_context(nc.psum_tensor("psb%d" % i, [128, 512], F32)) for i in range(8)]

    def tl_(t):
        return list(t) if isinstance(t, (list, tuple)) else [t]

    def dma_in(out_ap, in_ap, t, rd=(), key=None):
        sch.op("sp", lambda e, o=out_ap, i=in_ap: e.dma_start(out=o, in_=i), rd=rd, wr=tl_(t), dma=key or ("l_" + tl_(t)[0].name))

    def dma_out(out_ap, in_ap, t, wr=(), key=None):
        sch.op("pool", lambda e, o=out_ap, i=in_ap: e.dma_start(out=o, in_=i), rd=tl_(t), wr=wr, dma=key or ("s_" + tl_(t)[0].name))

    def mm(pst, out_ap, pairs, rd, start=True, stop=True):
        def fn(e, out_ap=out_ap, pairs=pairs, start=start, stop=stop):
            n = len(pairs)
            for i, (l, r) in enumerate(pairs):
                ins = e.matmul(out_ap, lhsT=l, rhs=r, start=(start and i == 0), stop=(stop and i == n - 1))
            return ins
        sch.op("pe", fn, rd=rd, wr=[pst])

    def transposes(pst, items, rd, ident):
        def fn(e, items=items, ident=ident):
            for o, i in items:
                ins = e.transpose(out=o, in_=i, identity=ident)
            return ins
        sch.op("pe", fn, rd=rd, wr=[pst])

    def act(out, in_, func, rd, wr, **kw):
        sch.op("act", lambda e, o=out, i=in_, f=func, kw=kw: e.activation(out=o, in_=i, func=f, **kw), rd=rd, wr=wr)

    def vop(eng, name, rd, wr, *args, **kw):
        sch.op(eng, lambda e, name=name, args=args, kw=kw: getattr(e, name)(*args, **kw), rd=rd, wr=wr)

    cp_rr = [0]

    def evac_copy(out, in_, rd, wr):
        cp_rr[0] += 1
        if cp_rr[0] % 2:
            act(out, in_, AF.Copy, rd, wr)
        else:
            vop("dve", "tensor_copy", rd, wr, out=out, in_=in_)

    def bc(ap, shape):
        return ap.to_broadcast(list(shape))

    identf = A.f32(128)
    identb = A.bf16(128)
    t_ident = T("ident")
    dma_in(identf, I["ident"], t_ident)
    t_identb = T("identb")
    act(identb, identf, AF.Copy, [t_ident], [t_identb])
    bar_scr = {e: A.f32(16) for e in ("act", "dve", "pool")}
    t_scr = T("barscr")

    def do_barrier():
        sch.barrier({
            "act": lambda e: e.activation(out=bar_scr["act"], in_=identf[:, 0:16], func=AF.Copy),
            "dve": lambda e: e.tensor_copy(out=bar_scr["dve"], in_=identf[:, 0:16]),
            "pool": lambda e: e.tensor_copy(out=bar_scr["pool"], in_=identf[:, 0:16]),
            "pe": lambda e: e.transpose(out=psb[2][:, :].bitcast(BF16)[:, 0:128], in_=identb, identity=identb),
        }, pe_wr=[t_tb])

    mmb = Ring([(psb[0][:, :], T("mm0")), (psb[1][:, :], T("mm1"))])
    tb_ap = psb[2][:, :].bitcast(BF16)
    t_tb = T("tb")

    def load_bc(name, n):
        ap = A.f32(n)
        t = T("g_" + name)
        dma_in(ap, bc(I[name][0:1, :], [128, n]), t)
        return ap, t

    gq, t_gq = load_bc("a_g_q", 64)
    gk, t_gk = load_bc("a_g_k", 64)
    cb, t_cb = load_bc("cbias", 16)
    gcq, t_gcq = load_bc("b_g_cq", 384)
    gckv, t_gckv = load_bc("b_g_ckv", 256)
    gqn, t_gqn = load_bc("b_g_qn", 64)
    gqr, t_gqr = load_bc("b_g_qr", 32)
    gkn, t_gkn = load_bc("b_g_kn", 64)
    gkr, t_gkr = load_bc("b_g_kr", 32)
    ncb = A.f32(16)
    t_ncb = T("ncb")
    vop("dve", "tensor_scalar", [t_cb], [t_ncb], out=ncb, in0=cb, scalar1=-1.0, scalar2=None, op0=ALU.mult)
    invn3 = A.f32(3)
    invn8 = A.f32(8)
    t_invn = T("invn")
    for ap_, v in ((invn3[:, 0:1], 1.0 / 384), (invn3[:, 1:2], 1.0 / 256), (invn3[:, 2:3], 1.0 / 32),
                   (invn8[:, 0:4], 1.0 / 64), (invn8[:, 4:8], 1.0 / 32)):
        vop("pool", "memset", [], [t_invn], ap_, v)

    def stats_tile(n):
        return A.f32(n)

    persist_mark = A.mark()

    wA_in = A.bf16(8 * 4096)
    wA_in_v = wA_in.rearrange("p (k n) -> p k n", n=4096)
    wA_out = A.bf16(8 * 1024)
    wA_out_v = wA_out.rearrange("p (k n) -> p k n", n=1024)
    t_wAin = [T("wAin%d" % i) for i in range(8)]
    t_wAout = [T("wAout%d" % i) for i in range(2)]
    Eb = A.bf16(16 * 2 * 128)
    Ebv = Eb.rearrange("p (h j q) -> p h j q", j=2, q=128)
    t_E = T("E")
    M0 = A.bf16(128)
    t_M0 = T("M0")
    vop("pool", "memset", [], [t_M0], M0, 1.0)
    vop("pool", "memset", [], [t_M0], M0[0:64, 64:128], 0.0)
    wA_mark = A.mark()

    stg = Ring([(A.f32(2048), T("stg%d" % i)) for i in range(4)])
    cast_rr = [0]

    def cast(out, in_, rd, wr):
        cast_rr[0] += 1
        k = cast_rr[0] % 3
        if k == 0:
            act(out, in_, AF.Copy, rd, wr)
        elif k == 1:
            vop("dve", "tensor_copy", rd, wr, out=out, in_=in_)
        else:
            vop("pool", "tensor_copy", rd, wr, out=out, in_=in_)

    def load_weight(src, K, N, dst_v, tlist, tcols):
        cw = 2048 // K
        for c0 in range(0, N, cw):
            w_ = min(cw, N - c0)
            sap, st = stg.next()
            sv = sap[:, 0:K * w_].rearrange("p (k n) -> p k n", n=w_)
            dma_in(sv, src[:, c0:c0 + w_].rearrange("(k p) n -> p k n", p=128), st)
            cast(dst_v[:, :, c0:c0 + w_], sv, [st], [tlist[c0 // tcols]])

    sc = A.f32(8 * NS)
    t_sc = T("sc")
    dma_in(sc, I["scT"], t_sc)
    act(sc, sc, AF.Silu, [t_sc], [t_sc])
    ones1 = A.f32(8, parts=1)
    t_ones = T("ones1")
    vop("dve", "memset", [], [t_ones], ones1, 1.0)
    adab = Ring([(A.f32(256, parts=1), T("adab%d" % i)) for i in range(4)])
    modc = Ring([(A.f32(256, parts=NS), T("modc%d" % i)) for i in range(4)])
    scv = sc.rearrange("p (k s) -> p k s", s=NS)
    for l in range(2):
        for c in range(12):
            c0 = c * 256
            sap, st = stg.next()
            sv = sap.rearrange("p (k n) -> p k n", n=256)
            dma_in(sv, I["ada_w"][l * D:(l + 1) * D, c0:c0 + 256].rearrange("(k p) n -> p k n", p=128), st)
            bap, bt = adab.next()
            dma_in(bap, I["ada_b"][l:l + 1, c0:c0 + 256], bt)
            pap, pt = mmb.next()
            pairs = [(scv[:, k, :], sv[:, k, :]) for k in range(8)] + [(ones1[0:1, 0:NS], bap)]
            mm(pt, pap[0:NS, 0:256], pairs, [t_sc, st, bt, t_ones])
            map_, mt = modc.next()
            act(map_, pap[0:NS, 0:256], AF.Copy, [pt], [mt])
            dma_out(modd[l * NS:(l + 1) * NS, c0:c0 + 256], map_, mt)

    load_weight(I["a_w_in"], 8, 4096, wA_in_v, t_wAin, 512)
    load_weight(I["a_w_out"], 8, 1024, wA_out_v, t_wAout, 512)
    for hh in range(0, 16, 8):
        sap, st = stg.next()
        dma_in(sap, I["erel"][:, hh * 256:(hh + 8) * 256], st)
        for h in range(hh, hh + 8):
            act(Eb[:, h * 256:(h + 1) * 256], sap[:, (h - hh) * 256:(h - hh + 1) * 256], AF.Exp, [st, t_ncb], [t_E],
                bias=ncb[:, h:h + 1])
    vop("pool", "memset", [], [t_E], Ebv[64:128, :, 1, 0:64], 0.0)
    do_barrier()

    A.reset(wA_mark)
    KR = 6
    kT = A.bf16(8 * KR * 128)
    kTv = kT.rearrange("p (j n) -> p j n", n=KR * 128)
    t_kT = [T("kT%d" % i) for i in range(KR)]
    Vp = A.bf16(KR * 16 * 65)
    Vpv = Vp.rearrange("p (s h d) -> p s h d", h=16, d=65)
    t_Vp = [T("Vp%d" % i) for i in range(KR)]
    vop("pool", "memset", [], t_Vp, Vpv[:, :, :, 64:65], 1.0)
    xs_ = Ring([(A.f32(1024), T("x%d" % i)) for i in range(6)])
    dtmp = A.f32(1024)
    t_dtmp = (T("dtmp_a"), T("dtmp_b"))
    hb = A.bf16(1024)
    t_hb = (T("hb_a"), T("hb_b"))
    hTs = Ring([(A.bf16(1024), T("hT%d" % i)) for i in range(2)])
    pfs = Ring([(A.f32(1024), (T("pf%d_a" % i), T("pf%d_b" % i))) for i in range(3)])
    qnb = A.bf16(1024)
    t_qnb = (T("qnb_a"), T("qnb_b"))
    knb = A.bf16(1024)
    t_knb = (T("knb_a"), T("knb_b"))
    qTs = Ring([(A.bf16(1024), T("qT%d" % i)) for i in range(2)])
    zss = Ring([(A.bf16(1024), T("zs%d" % i)) for i in range(3)])
    PTs = Ring([(A.bf16(512), [T("PT%d_%d" % (i, k)) for k in range(4)]) for i in range(3)])
    ogs = Ring([(A.bf16(1024), T("og%d" % i)) for i in range(2)])
    ogT = A.bf16(1024)
    t_ogT = T("ogT")
    bc3 = A.f32(2048)
    t_bc3 = T("bc3")
    gates = Ring([(A.f32(1024), T("gate%d" % i)) for i in range(2)])
    cur_gate = [None]
    G1 = A.f32(1024)
    t_G1 = T("G1")
    st_ss = A.f32(4)
    t_ss = T("ss")
    st_16 = A.f32(64)
    t_16 = (T("st16_a"), T("st16_b"))
    rden = A.f32(16)
    t_rden = T("rden")
    sringA = Ring([(psb[3][:, :], T("sa0")), (psb[4][:, :], T("sa1")), (psb[7][:, :], T("sa2"))])
    t_O = [T("O0"), T("O1")]

    def o_ap(h):
        if h < 7:
            return psb[5][:, h * 65:(h + 1) * 65], t_O[0]
        if h < 14:
            return psb[6][:, (h - 7) * 65:(h - 6) * 65], t_O[1]
        return psb[5][:, (h - 14) * 65:(h - 13) * 65], t_O[0]

    def load_mod(l, s):
        dma_in(bc3, bc(modd[l * NS + s:l * NS + s + 1, 0:2048], [128, 2048]), t_bc3, key="l_bc3")
        gap, gt = gates.next()
        dma_in(gap, bc(modd[l * NS + s:l * NS + s + 1, 2048:3072], [128, 1024]), gt)
        cur_gate[0] = (gap, gt)
        dma_in(G1, bc(I["norm_g"][l:l + 1, :], [128, 1024]), t_G1, key="l_G1")
        vop("dve", "scalar_tensor_tensor", [t_bc3, t_G1], [t_G1], out=G1, in0=bc3[:, 1024:2048], scalar=1.0, in1=G1,
            op0=ALU.add, op1=ALU.mult)

    def rstd_from(ss_ap, n, tss, scale, scr_ap):
        act(scr_ap, ss_ap, AF.Ln, tl_(tss), tl_(tss), scale=scale, bias=EPS)
        act(ss_ap, scr_ap, AF.Exp, tl_(tss), tl_(tss), scale=-0.5)

    def ada_norm_tile(xap, xt):
        act(junk, xap, AF.Square, [xt], [t_junk, t_ss], accum_out=st_ss[:, 0:1])
        rstd_from(st_ss[:, 0:1], 1, t_ss, 1.0 / D, st_ss[:, 1:2])
        vop("dve", "scalar_tensor_tensor", [xt, t_ss, t_G1], [t_dtmp], out=dtmp, in0=xap, scalar=st_ss[:, 0:1], in1=G1,
            op0=ALU.mult, op1=ALU.mult)
        vop("dve", "tensor_tensor", [t_dtmp, t_bc3], [t_hb], out=hb, in0=dtmp, in1=bc3[:, 0:1024], op=ALU.add)
        transposes(t_tb, [(tb_ap[:, k * 128:(k + 1) * 128], hb[:, k * 128:(k + 1) * 128]) for k in range(8)],
                   [t_hb, t_identb], identb)
        hT, hTt = hTs.next()
        evac_copy(hT, tb_ap, [t_tb], [hTt])
        return hT.rearrange("p (k n) -> p k n", n=128), hTt

    def head_rms(pf, pft, g_ap, g_t, out_bf, out_t, also_f32):
        gb = bc(g_ap.unsqueeze(1), [128, 8, 64])
        for hf in range(2):
            cs_ = slice(hf * 512, hf * 512 + 512)
            pv = pf[:, cs_].rearrange("p (h d) -> p h d", d=64)
            dv = dtmp[:, cs_].rearrange("p (h d) -> p h d", d=64)
            s16 = st_16[:, hf * 8:hf * 8 + 8]
            vop("dve", "tensor_tensor", [pft[hf]], [t_dtmp[hf]], out=dtmp[:, cs_], in0=pf[:, cs_], in1=pf[:, cs_], op=ALU.mult)
            vop("dve", "tensor_reduce", [t_dtmp[hf]], [t_16[hf]], out=s16, in_=dv, axis=AX.X, op=ALU.add)
            yield
        rstd_from(st_16[:, 0:16], 16, t_16, 1.0 / 64, st_16[:, 16:32])
        yield
        for hf in range(2):
            cs_ = slice(hf * 512, hf * 512 + 512)
            pv = pf[:, cs_].rearrange("p (h d) -> p h d", d=64)
            s16 = st_16[:, hf * 8:hf * 8 + 8]
            vop("dve", "tensor_tensor", [pft[hf], t_16[hf]], [pft[hf]], out=pv, in0=pv, in1=bc(s16.unsqueeze(2), [128, 8, 64]), op=ALU.mult)
            yield
            if also_f32:
                vop("dve", "tensor_tensor", [pft[hf], g_t], [pft[hf]], out=pv, in0=pv, in1=gb, op=ALU.mult)
                act(out_bf[:, cs_], pf[:, cs_], AF.Copy, [pft[hf]], [out_t[hf]])
            else:
                vop("dve", "tensor_tensor", [pft[hf], g_t], [out_t[hf]], out=out_bf[:, cs_].rearrange("p (h d) -> p h d", d=64), in0=pv, in1=gb,
                    op=ALU.mult)
            yield

    class Ctx:
        pass

    def sample_cache_tile(c):
        j = c.j
        slot = c.ti % KR
        pf, pft = pfs.next()
        dma_in(pf, I["ck"][j * 128:(j + 1) * 128, :], pft)
        vop("pool", "tensor_copy", list(pft), list(t_knb), out=knb, in_=pf)
        transposes(t_tb, [(tb_ap[:, q * 128:(q + 1) * 128], knb[:, q * 128:(q + 1) * 128]) for q in range(8)],
                   list(t_knb) + [t_identb], identb)
        evac_copy(kTv[:, :, slot * 128:(slot + 1) * 128], tb_ap.rearrange("p (j n) -> p j n", n=128), [t_tb], [t_kT[slot]])
        pf2, pft2 = pfs.next()
        dma_in(pf2, I["cv"][j * 128:(j + 1) * 128, :], pft2)
        vop("dve", "tensor_copy", list(pft2), [t_Vp[slot]], out=Vpv[:, slot, :, 0:64], in_=pf2.rearrange("p (h d) -> p h d", d=64))

    def load_x(c):
        c.x, c.xt = xs_.next()
        dma_in(c.x, c.xsrc, c.xt)

    def norm_part(c):
        if c.premod is not None:
            c.premod()
        c.gate = cur_gate[0]
        xap, xt = c.x, c.xt
        act(hb, xap, AF.Square, [xt], list(t_hb) + [t_ss], accum_out=st_ss[:, 0:1])
        rstd_from(st_ss[:, 0:1], 1, t_ss, 1.0 / D, st_ss[:, 1:2])
        for hf in range(2):
            cs_ = slice(hf * 512, hf * 512 + 512)
            vop("dve", "scalar_tensor_tensor", [xt, t_ss, t_G1], [t_dtmp[hf]], out=dtmp[:, cs_], in0=xap[:, cs_], scalar=st_ss[:, 0:1],
                in1=G1[:, cs_], op0=ALU.mult, op1=ALU.mult)
            vop("dve", "tensor_tensor", [t_dtmp[hf], t_bc3], [t_hb[hf]], out=hb[:, cs_], in0=dtmp[:, cs_], in1=bc3[:, cs_], op=ALU.add)
            yield

    def hT_part(c):
        transposes(t_tb, [(tb_ap[:, k * 128:(k + 1) * 128], hb[:, k * 128:(k + 1) * 128]) for k in range(8)],
                   list(t_hb) + [t_identb], identb)
        hT, hTt = hTs.next()
        evac_copy(hT, tb_ap, [t_tb], [hTt])
        c.hT, c.hTt = hT.rearrange("p (k n) -> p k n", n=128), hTt

    def frontA(c):
        for c2 in c.load_list:
            load_x(c2)
        if c.kind == "cache":
            sample_cache_tile(c)
            yield
            if c.norm_next is not None:
                yield from norm_part(c.norm_next)
                hT_part(c.norm_next)
            return
        hT, hTt = c.hT, c.hTt
        slot = c.ti % KR
        c.slot = slot
        pfq, pfqt = pfs.next()
        pfk, pfkt = pfs.next()
        zs, zst = zss.next()
        c.zs, c.zst = zs, zst
        pfv = pfvt = None
        if c.store_kv is not None:
            pfv, pfvt = pfs.next()
        for cg in range(8):
            pap, pt = mmb.next()
            mm(pt, pap, [(hT[:, k, :], wA_in_v[:, k, cg * 512:(cg + 1) * 512]) for k in range(8)],
               [hTt, t_wAin[cg]])
            half = slice((cg % 2) * 512, (cg % 2) * 512 + 512)
            if cg < 2:
                act(pfq[:, half], pap, AF.Copy, [pt], [pfqt[cg % 2]])
            elif cg < 4:
                act(pfk[:, half], pap, AF.Copy, [pt], [pfkt[cg % 2]])
            elif cg < 6:
                h0 = (cg - 4) * 8
                if pfv is not None:
                    act(pfv[:, half], pap, AF.Copy, [pt], [pfvt[cg % 2]])
                    vop("pool", "tensor_copy", [pfvt[cg % 2]], [t_Vp[slot]], out=Vpv[:, slot, h0:h0 + 8, 0:64],
                        in_=pfv[:, half].rearrange("p (h d) -> p h d", d=64))
                else:
                    vop("dve", "tensor_copy", [pt], [t_Vp[slot]], out=Vpv[:, slot, h0:h0 + 8, 0:64],
                        in_=pap.rearrange("p (h d) -> p h d", d=64))
                if cg == 5 and c.sample:
                    vop("pool", "memset", [], [t_Vp[slot]], Vpv[32:64, slot, :, :], 0.0)
                    vop("pool", "memset", [], [t_Vp[slot]], Vpv[64:128, slot, :, :], 0.0)
            elif cg == 6:
                z6 = (pap, pt)
            else:
                act(zs[:, 0:512], z6[0], AF.Silu, [z6[1]], [zst])
                act(zs[:, 512:1024], pap, AF.Silu, [pt], [zst])
            yield
            if cg == 0 and c.norm_next is not None:
                yield from norm_part(c.norm_next)
            if cg == 1:
                yield from head_rms(pfq, pfqt, gq, t_gq, qnb, t_qnb, False)
            if cg == 3:
                yield from head_rms(pfk, pfkt, gk, t_gk, knb, t_knb, True)
                if c.store_kv is not None:
                    dma_out(c.store_kv[0], pfk, pfkt)
                yield
            if cg == 5:
                transposes(t_tb, [(tb_ap[:, j * 128:(j + 1) * 128], qnb[:, j * 128:(j + 1) * 128]) for j in range(8)],
                           list(t_qnb) + [t_identb], identb)
                qT, qTt = qTs.next()
                c.qT, c.qTt = qT.rearrange("p (j n) -> p j n", n=128), qTt
                evac_copy(qT, tb_ap, [t_tb], [qTt])
                if pfv is not None:
                    dma_out(c.store_kv[1], pfv, pfvt)
                yield
            if cg == 7:
                transposes(t_tb, [(tb_ap[:, j * 128:(j + 1) * 128], knb[:, j * 128:(j + 1) * 128]) for j in range(8)],
                           list(t_knb) + [t_identb], identb)
                evac_copy(kTv[:, :, slot * 128:(slot + 1) * 128], tb_ap.rearrange("p (j n) -> p j n", n=128), [t_tb],
                          [t_kT[slot]])
                yield
        if c.norm_next is not None:
            hT_part(c.norm_next)

    def attnA(c):
        if c.kind == "cache":
            return
        ti = c.ti
        js = [j for j in range(5) if ti - 4 + j >= c.first_ti]
        og, ogt = ogs.next()
        c.og, c.ogt = og, ogt

        def norm_hook(h):
            def f():
                g0, g1, src = {6: (0, 7, psb[5][:, 0:455]), 13: (7, 14, psb[6][:, 0:455]), 15: (14, 16, psb[5][:, 0:130])}[h]
                ng = g1 - g0
                ov = src.rearrange("p (h d) -> p h d", d=65)
                tO = o_ap(g0)[1]
                vop("dve", "reciprocal", [tO], [t_rden], out=rden[:, g0:g1], in_=ov[:, :, 64])
                dv = dtmp[:, g0 * 64:g1 * 64].rearrange("p (h d) -> p h d", d=64)
                vop("dve", "tensor_tensor", [tO, t_rden], list(t_dtmp), out=dv, in0=ov[:, :, 0:64],
                    in1=bc(rden[:, g0:g1].unsqueeze(2), [128, ng, 64]), op=ALU.mult)
                vop("pool", "tensor_tensor", list(t_dtmp) + [c.zst], [ogt], out=og[:, g0 * 64:g1 * 64],
                    in0=dtmp[:, g0 * 64:g1 * 64], in1=c.zs[:, g0 * 64:g1 * 64], op=ALU.mult)
            return f

        tiles = []
        for h in range(16):
            rg, j8 = h // 8, h % 8
            ps = slice(rg * 64, rg * 64 + 64)
            oap, ot = o_ap(h)
            for j in js:
                sl = (ti - 4 + j) % KR
                emul = None
                if j >= 3:
                    emul = (Ebv[:, h, j - 3, :], t_E)
                elif j == 0 and not c.sample:
                    emul = (M0, t_M0)
                tiles.append(dict(l=kTv[ps, j8, sl * 128:(sl + 1) * 128], r=c.qT[ps, j8, :], rdq=[t_kT[sl], c.qTt],
                                  V=Vpv[:, sl, h, :], Vt=t_Vp[sl], o=oap, ot=ot, start=(j == js[0]), stop=(j == js[-1]),
                                  emul=emul, masks=[],
                                  after=(norm_hook(h) if (h in (6, 13, 15) and j == js[-1]) else None)))
                if j == 3 and 4 in js:
                    tiles[-1]["pair"] = Eb[:, h * 256:(h + 1) * 256]
                if j == 4 and 3 in js:
                    tiles[-1]["pair_of"] = tiles[-2]
        yield from attn_stream(tiles, 0.125, sringA, PTs)

    def backA(c, wout_v, t_wout):
        if c.kind == "cache":
            return
        transposes(t_tb, [(tb_ap[:, k * 128:(k + 1) * 128], c.og[:, k * 128:(k + 1) * 128]) for k in range(8)],
                   [c.ogt, t_identb], identb)
        evac_copy(ogT, tb_ap, [t_tb], [t_ogT])
        ogTv = ogT.rearrange("p (k n) -> p k n", n=128)
        yield
        for cg in range(2):
            pap, pt = mmb.next()
            mm(pt, pap, [(ogTv[:, k, :], wout_v[:, k, cg * 512:(cg + 1) * 512]) for k in range(8)], [t_ogT, t_wout[cg]])
            cs = slice(cg * 512, cg * 512 + 512)
            vop("dve", "tensor_tensor", [pt, c.gate[1]], [t_dtmp[cg]], out=dtmp[:, cs], in0=pap, in1=c.gate[0][:, cs], op=ALU.mult)
            vop("dve", "tensor_tensor", [t_dtmp[cg], c.xt], [c.xt], out=c.x[:, cs], in0=dtmp[:, cs], in1=c.x[:, cs], op=ALU.add)
            yield
        for dst in c.ydst:
            dma_out(dst, c.x, c.xt)

    def attn_stream(tiles, scale, sring, ptring, lookahead=2):
        items = [tiles[i:i + 4] for i in range(0, len(tiles), 4)]

        def issue_qk(item):
            sap, st_ = sring.next()
            mats = [(sap[:, i * 128:(i + 1) * 128], tl["l"], tl["r"]) for i, tl in enumerate(item)]

            def fn(e, mats=mats):
                for o, l, r in mats:
                    ins = e.matmul(o, lhsT=l, rhs=r, start=True, stop=True)
                return ins
            rd = []
            for tl in item:
                for t_ in tl["rdq"]:
                    if t_ not in rd:
                        rd.append(t_)
            sch.op("pe", fn, rd=rd, wr=[st_])
            return sap, st_

        def softmax_part(item, sap, st_):
            n = len(item)
            PT, subs = ptring.next()
            act(PT[:, 0:n * 128], sap[:, 0:n * 128], AF.Exp, [st_], list(subs), scale=scale)
            idx = 0
            while idx < n:
                tl = item[idx]
                if tl["emul"] is None:
                    idx += 1
                    continue
                eap, et = tl["emul"]
                if idx + 1 < n and tl.get("pair") is not None and item[idx + 1].get("pair_of") is tl:
                    vop("dve", "tensor_tensor", [subs[idx], subs[idx + 1], et], [subs[idx], subs[idx + 1]],
                        out=PT[:, idx * 128:(idx + 2) * 128], in0=PT[:, idx * 128:(idx + 2) * 128], in1=tl["pair"], op=ALU.mult)
                    idx += 2
                    continue
                vop("dve", "tensor_tensor", [subs[idx], et], [subs[idx]], out=PT[:, idx * 128:(idx + 1) * 128],
                    in0=PT[:, idx * 128:(idx + 1) * 128], in1=eap, op=ALU.mult)
                idx += 1
            for idx, tl in enumerate(item):
                for (ps_, c0, c1) in tl["masks"]:
                    vop("pool", "memset", [], [subs[idx]], PT[ps_, idx * 128 + c0:idx * 128 + c1], 0.0)
            return PT, subs

        pend = [issue_qk(it) for it in items[:lookahead]]
        soft = [softmax_part(items[0], *pend.pop(0))]
        yield
        for i, item in enumerate(items):
            if i + lookahead < len(items):
                pend.append(issue_qk(items[i + lookahead]))
            if i + 1 < len(items):
                soft.append(softmax_part(items[i + 1], *pend.pop(0)))
            PT, subs = soft.pop(0)
            mats = [(tl["o"], PT[:, idx * 128:(idx + 1) * 128], tl["V"], tl["start"], tl["stop"]) for idx, tl in enumerate(item)]

            def fnpv(e, mats=mats):
                for o, l, r, s0, s1 in mats:
                    ins = e.matmul(o, lhsT=l, rhs=r, start=s0, stop=s1)
                return ins
            rd = list(subs[:len(item)])
            wr = []
            for tl in item:
                if tl["Vt"] not in rd:
                    rd.append(tl["Vt"])
                if tl["ot"] not in wr:
                    wr.append(tl["ot"])
            sch.op("pe", fnpv, rd=rd, wr=wr)
            for tl in item:
                if tl["after"] is not None:
                    tl["after"]()
            yield

    def run_pipeline(tiles, stages, late_first=False):
        ns = len(stages)
        for step in range(len(tiles) + ns - 1):
            gens = []
            for k in (reversed(range(ns)) if late_first else range(ns)):
                n = step - k
                if 0 <= n < len(tiles):
                    if k == 0 and tiles[n].pre is not None:
                        tiles[n].pre()
                    gens.append(stages[k](tiles[n]))
            while gens:
                for g in list(gens):
                    try:
                        next(g)
                    except StopIteration:
                        gens.remove(g)

    tilesA = []
    gti = 0
    for s in range(nseq):
        for t in range(NT):
            c = Ctx()
            c.kind = "prompt"
            c.sample = False
            c.ti = gti + t
            c.first_ti = gti
            c.xsrc = I["xp"][s * S + t * 128:s * S + (t + 1) * 128, :]
            c.ydst = [y0p[s * S + t * 128:s * S + (t + 1) * 128, :]]
            if debug_layers == 1:
                c.ydst = [O["yp"][s * S + t * 128:s * S + (t + 1) * 128, :]]
            c.store_kv = None
            import os
            if t >= NT - NAO and not os.environ.get("DBG_NOSTORE"):
                r0 = s * AR + (t - (NT - NAO)) * 128
                c.store_kv = (O["akp"][r0:r0 + 128, :], O["avp"][r0:r0 + 128, :])
            c.pre = None
            c.premod = (lambda s=s: load_mod(0, s)) if t == 0 else None
            tilesA.append(c)
        gti += NT
    for j in range(4):
        c = Ctx()
        c.kind = "cache"
        c.ti = gti + j
        c.j = j
        c.pre = None
        tilesA.append(c)
    c = Ctx()
    c.kind = "sample"
    c.sample = True
    c.first_ti = gti
    c.ti = gti + 4
    c.xsrc = I["xs"]
    c.ydst = [y0s] if debug_layers != 1 else [O["ys"]]
    c.store_kv = (O["aks"], O["avs"])
    c.pre = None
    c.premod = lambda: load_mod(0, nseq)
    tilesA.append(c)
    import os
    if os.environ.get("DBG_MEM"):
        print("layer A arena used", A.off, "of", A.cap)
    if debug_layers == 0:
        tilesA = []
    if debug_layers < 0:
        tilesA = tilesA[:-debug_layers]
    for c in tilesA:
        c.load_list = []
        c.norm_next = None
    normal = [i for i, c in enumerate(tilesA) if c.kind != "cache"]
    prologue_load, prologue_norm = [], []
    for i in normal:
        (tilesA[i - 2].load_list if i >= 2 else prologue_load).append(tilesA[i])
        if i >= 1:
            tilesA[i - 1].norm_next = tilesA[i]
        else:
            prologue_norm.append(tilesA[i])
    for c in prologue_load:
        load_x(c)
    for c in prologue_norm:
        for _ in norm_part(c):
            pass
        hT_part(c)
    run_pipeline(tilesA, [frontA, attnA, lambda c: backA(c, wA_out_v, t_wAout)], late_first=True)
    do_barrier()

    if debug_layers >= 2:
        A.reset(persist_mark)
        wBin = A.bf16(8 * 1696)
        wBin_v = wBin.rearrange("p (k n) -> p k n", n=1696)
        wuq = A.bf16(3 * 1536)
        wuq_v = wuq.rearrange("p (k n) -> p k n", n=1536)
        wukv = A.bf16(2 * 2048)
        wukv_v = wukv.rearrange("p (k n) -> p k n", n=2048)
        wBout = A.bf16(8 * 1024)
        wBout_v = wBout.rearrange("p (k n) -> p k n", n=1024)
        t_wBin = [T("wBin%d" % i) for i in range(4)]
        t_wuq = [T("wuq%d" % i) for i in range(4)]
        t_wukv = [T("wukv%d" % i) for i in range(4)]
        t_wBout = [T("wBout%d" % i) for i in range(2)]
        wB_mark = A.mark()
        stg = Ring([(A.f32(2048), T("stgB%d" % i)) for i in range(4)])
        load_weight(I["b_w_in"], 8, 1696, wBin_v, t_wBin, 512)
        load_weight(I["b_w_uq"], 3, 1536, wuq_v, t_wuq, 384)
        load_weight(I["b_w_ukv"], 2, 2048, wukv_v, t_wukv, 512)
        load_weight(I["b_w_out"], 8, 1024, wBout_v, t_wBout, 512)
        do_barrier()
        A.reset(wB_mark)
        ZT = max(NT, 1)
        zsB = A.bf16(ZT * 1024)
        zsB_v = zsB.rearrange("p (t n) -> p t n", n=1024)
        t_zsB = [T("zsB%d" % i) for i in range(ZT)]
        cqnT = A.bf16(3 * ZT * 128)
        cqnT_v = cqnT.rearrange("p (k n) -> p k n", n=ZT * 128)
        t_cq = [T("cq%d" % i) for i in range(ZT)]
        ckvnT = A.bf16(2 * NTB * 128)
        ckvnT_v = ckvnT.rearrange("p (k n) -> p k n", n=NTB * 128)
        t_ckv = [T("ckv%d" % i) for i in range(NTB)]
        krb = A.bf16(NTB * 32)
        krb_v = krb.rearrange("p (t n) -> p t n", n=32)
        t_kr = [T("kr%d" % i) for i in range(NTB)]
        kTg = A.bf16(4 * NTB * 128)
        kTg_v = kTg.rearrange("p (h n) -> p h n", n=NTB * 128)
        t_kTg = [T("kTg%d" % i) for i in range(NTB)]
        vop("pool", "memset", [], t_kTg, kTg, 0.0)
        Vg = A.bf16(NTB * 4 * 65)
        Vg_v = Vg.rearrange("p (t h d) -> p t h d", h=4, d=65)
        t_Vg = [T("Vg%d" % i) for i in range(NTB)]
        vop("pool", "memset", [], t_Vg, Vg_v[:, :, :, 64:65], 1.0)
        cosb = A.f32((NT + 1) * 16)
        sinb = A.f32((NT + 1) * 16)
        cos_v = cosb.rearrange("p (t f) -> p t f", f=16)
        sin_v = sinb.rearrange("p (t f) -> p t f", f=16)
        t_cs = T("cossin")
        dma_in(cosb[:, 0:NT * 16], I["cosp"], t_cs, key="l_cos")
        dma_in(sinb[:, 0:NT * 16], I["sinp"], t_cs, key="l_cos")
        dma_in(cos_v[:, NT, :], I["coss"], t_cs, key="l_cos")
        dma_in(sin_v[:, NT, :], I["sins"], t_cs, key="l_cos")
        gateB = A.f32(1024)
        t_gateB = T("gateB")
        work_mark = A.mark()
        SCB = 96.0 ** -0.5
        sbanks = Ring([(psb[3][:, :], T("sB0")), (psb[4][:, :], T("sB1")), (psb[7][:, :], T("sB2"))])
        obanks = Ring([(psb[5][:, :], T("oB0")), (psb[6][:, :], T("oB1"))])

        def ring_f32(name, n, k, parts=128):
            return Ring([(A.f32(n, parts=parts), T("%s%d" % (name, i))) for i in range(k)])

        def ring_bf16(name, n, k, parts=128):
            return Ring([(A.bf16(n, parts=parts), T("%s%d" % (name, i))) for i in range(k)])

        def rope(src3, nh, csi, out1, out2, rd, wr, rts):
            x1, x2 = src3[:, :, 0:16], src3[:, :, 16:32]
            cosx = bc(cos_v[:, csi, :].unsqueeze(1), [128, nh, 16])
            sinx = bc(sin_v[:, csi, :].unsqueeze(1), [128, nh, 16])
            rt_, t_rt = rts.next()
            a1 = rt_[:, 0:nh * 16].rearrange("p (h f) -> p h f", f=16)
            a2 = rt_[:, 64:64 + nh * 16].rearrange("p (h f) -> p h f", f=16)
            b1 = rt_[:, 128:128 + nh * 16].rearrange("p (h f) -> p h f", f=16)
            b2 = rt_[:, 192:192 + nh * 16].rearrange("p (h f) -> p h f", f=16)
            vop("pool", "tensor_tensor", rd + [t_cs], [t_rt], out=a1, in0=x1, in1=cosx, op=ALU.mult)
            vop("pool", "tensor_tensor", rd + [t_cs], [t_rt], out=a2, in0=x2, in1=sinx, op=ALU.mult)
            vop("pool", "tensor_tensor", rd + [t_cs], [t_rt], out=b1, in0=x2, in1=cosx, op=ALU.mult)
            vop("pool", "tensor_tensor", rd + [t_cs], [t_rt], out=b2, in0=x1, in1=sinx, op=ALU.mult)
            vop("pool", "tensor_tensor", [t_rt], wr, out=out1, in0=a1, in1=a2, op=ALU.subtract)
            vop("pool", "tensor_tensor", [t_rt], wr, out=out2, in0=b1, in1=b2, op=ALU.add)

        W = Ctx()

        def sweep1_bufs(s):
            do_barrier()
            A.reset(work_mark)
            W.xs = ring_f32("x1_", 1024, 3)
            W.dt = ring_f32("dt1_", 1024, 2)
            W.hb = ring_bf16("hb1_", 1024, 2)
            W.hT = ring_bf16("hT1_", 1024, 2)
            W.pj = ring_f32("pj", 672, 3)
            W.cqb = ring_bf16("cqb", 384, 2)
            W.ckvf = ring_f32("ckvf", 256, 3)
            W.ckvb = ring_bf16("ckvb", 256, 2)
            W.krn = ring_f32("krn", 32, 2)
            W.krf = ring_f32("krf", 32, 3)
            W.rt = ring_f32("rt", 256, 2)
            W.ss = ring_f32("ss1_", 4, 3)
            W.st3 = ring_f32("st3_", 8, 3)
            W.bc3 = A.f32(2048)
            W.t_bc3 = T("bc3B")
            W.G1 = A.f32(1024)
            W.t_G1 = T("G1B")
            dma_in(W.bc3, bc(modd[NS + s:NS + s + 1, 0:2048], [128, 2048]), W.t_bc3)
            dma_in(gateB, bc(modd[NS + s:NS + s + 1, 2048:3072], [128, 1024]), t_gateB)
            dma_in(W.G1, bc(I["norm_g"][1:2, :], [128, 1024]), W.t_G1)
            vop("dve", "scalar_tensor_tensor", [W.t_bc3, W.t_G1], [W.t_G1], out=W.G1, in0=W.bc3[:, 1024:2048], scalar=1.0, in1=W.G1,
                op0=ALU.add, op1=ALU.mult)

        def s1_load(c):
            if c.kind == "cache":
                return
            c.x, c.xt = W.xs.next()
            dma_in(c.x, c.ysrc, c.xt)
            return
            yield

        def s1_norm(c):
            if c.kind == "cache":
                return
            xap, xt = c.x, c.xt
            ss, sst = W.ss.next()
            hb_, hbt = W.hb.next()
            dt_, dtt = W.dt.next()
            act(hb_, xap, AF.Square, [xt], [hbt, sst], accum_out=ss[:, 0:1])
            rstd_from(ss[:, 0:1], 1, sst, 1.0 / D, ss[:, 1:2])
            vop("dve", "scalar_tensor_tensor", [xt, sst, W.t_G1], [dtt], out=dt_, in0=xap, scalar=ss[:, 0:1], in1=W.G1,
                op0=ALU.mult, op1=ALU.mult)
            vop("dve", "tensor_tensor", [dtt, W.t_bc3], [hbt], out=hb_, in0=dt_, in1=W.bc3[:, 0:1024], op=ALU.add)
            c.hb_, c.hbt = hb_, hbt
            yield

        def s1_tr(c):
            if c.kind == "cache":
                return
            hb_, hbt = c.hb_, c.hbt
            transposes(t_tb, [(tb_ap[:, k * 128:(k + 1) * 128], hb_[:, k * 128:(k + 1) * 128]) for k in range(8)],
                       [hbt, t_identb], identb)
            hT, hTt = W.hT.next()
            evac_copy(hT, tb_ap, [t_tb], [hTt])
            c.hT, c.hTt = hT.rearrange("p (k n) -> p k n", n=128), hTt
            yield

        def s1_mm(c):
            if c.kind == "cache":
                return
            c.pj, c.pjt = W.pj.next()
            for cg in range(4):
                w_ = 512 if cg < 3 else 160
                pap, pt = mmb.next()
                mm(pt, pap[:, 0:w_], [(c.hT[:, k, :], wBin_v[:, k, cg * 512:cg * 512 + w_]) for k in range(8)], [c.hTt, t_wBin[cg]])
                if cg == 0:
                    z0 = (pap, pt)
                elif cg == 1:
                    act(zsB_v[:, c.zt, 0:512], z0[0], AF.Silu, [z0[1]], [t_zsB[c.zt]])
                    act(zsB_v[:, c.zt, 512:1024], pap, AF.Silu, [pt], [t_zsB[c.zt]])
                else:
                    act(c.pj[:, (cg - 2) * 512:(cg - 2) * 512 + w_], pap[:, 0:w_], AF.Copy, [pt], [c.pjt])
                yield

        def s1_e1(c):
            if c.kind == "cache":
                kt = c.kt
                c.cf, c.cft = W.ckvf.next()
                dma_in(c.cf, I["cckv"][kt * 128:(kt + 1) * 128, :], c.cft)
                c.kr_, c.krt = W.krf.next()
                dma_in(c.kr_, I["ckr"][kt * 128:(kt + 1) * 128, :], c.krt)
                c.ckvb, c.ckvbt = W.ckvb.next()
                vop("pool", "tensor_copy", [c.cft], [c.ckvbt], out=c.ckvb, in_=c.cf)
                vop("pool", "tensor_copy", [c.krt], [t_kr[c.kt]], out=krb_v[:, c.kt, :], in_=c.kr_)
                return
            pj, pjt = c.pj, c.pjt
            dt_, dtt = W.dt.next()
            st3, t_st3 = W.st3.next()
            vop("dve", "tensor_tensor", [pjt], [dtt], out=dt_[:, 0:672], in0=pj, in1=pj, op=ALU.mult)
            vop("dve", "tensor_reduce", [dtt], [t_st3], out=st3[:, 0:1], in_=dt_[:, 0:384], axis=AX.X, op=ALU.add)
            vop("dve", "tensor_reduce", [dtt], [t_st3], out=st3[:, 1:2], in_=dt_[:, 384:640], axis=AX.X, op=ALU.add)
            vop("dve", "tensor_reduce", [dtt], [t_st3], out=st3[:, 2:3], in_=dt_[:, 640:672], axis=AX.X, op=ALU.add)
            vop("dve", "tensor_tensor", [t_st3, t_invn], [t_st3], out=st3[:, 0:3], in0=st3[:, 0:3], in1=invn3, op=ALU.mult)
            rstd_from(st3[:, 0:3], 3, t_st3, 1.0, st3[:, 4:7])
            yield
            c.cqb, c.cqbt = W.cqb.next()
            vop("dve", "scalar_tensor_tensor", [pjt, t_st3, t_gcq], [c.cqbt], out=c.cqb, in0=pj[:, 0:384], scalar=st3[:, 0:1], in1=gcq,
                op0=ALU.mult, op1=ALU.mult)
            c.cf, c.cft = W.ckvf.next()
            vop("dve", "scalar_tensor_tensor", [pjt, t_st3, t_gckv], [c.cft], out=c.cf, in0=pj[:, 384:640], scalar=st3[:, 1:2], in1=gckv,
                op0=ALU.mult, op1=ALU.mult)
            dma_out(c.ckv_dst, c.cf, c.cft)
            c.ckvb, c.ckvbt = W.ckvb.next()
            vop("pool", "tensor_copy", [c.cft], [c.ckvbt], out=c.ckvb, in_=c.cf)
            c.krn, c.krnt = W.krn.next()
            vop("dve", "scalar_tensor_tensor", [pjt, t_st3, t_gkr], [c.krnt], out=c.krn, in0=pj[:, 640:672], scalar=st3[:, 2:3], in1=gkr,
                op0=ALU.mult, op1=ALU.mult)
            yield

        def s1_e2(c):
            kt, zt = c.kt, c.zt
            if c.kind != "cache":
                transposes(t_tb, [(tb_ap[:, k * 128:(k + 1) * 128], c.cqb[:, k * 128:(k + 1) * 128]) for k in range(3)],
                           [c.cqbt, t_identb], identb)
                evac_copy(cqnT_v[:, :, zt * 128:(zt + 1) * 128], tb_ap[:, 0:384].rearrange("p (k n) -> p k n", n=128), [t_tb], [t_cq[zt]])
                yield
            transposes(t_tb, [(tb_ap[:, k * 128:(k + 1) * 128], c.ckvb[:, k * 128:(k + 1) * 128]) for k in range(2)],
                       [c.ckvbt, t_identb], identb)
            evac_copy(ckvnT_v[:, :, kt * 128:(kt + 1) * 128], tb_ap[:, 0:256].rearrange("p (k n) -> p k n", n=128), [t_tb], [t_ckv[kt]])
            yield
            if c.kind != "cache":
                kr_, krt = W.krf.next()
                k3 = c.krn.rearrange("p (h f) -> p h f", h=1)
                o3 = kr_.rearrange("p (h f) -> p h f", h=1)
                rope(k3, 1, c.cs, o3[:, :, 0:16], o3[:, :, 16:32], [c.krnt], [krt], W.rt)
                dma_out(c.kr_dst, kr_, krt)
                vop("pool", "tensor_copy", [krt], [t_kr[kt]], out=krb_v[:, kt, :], in_=kr_)
                yield

        def sweep2_bufs():
            do_barrier()
            A.reset(work_mark)
            W.qg = ring_f32("qg", 384, 3)
            W.kf = ring_f32("kf", 256, 3)
            W.dsq = ring_f32("dsq", 384, 2)
            W.qnb = ring_bf16("qnb", 384, 3)
            W.knb = ring_bf16("knb", 384, 3)
            W.qT = ring_bf16("qTB", 512, 4)
            for qap_, qt_ in W.qT.items:
                vop("pool", "memset", [], [qt_], qap_, 0.0)
            W.PT = Ring([(A.bf16(512), [T("PTB%d_%d" % (i, k)) for k in range(4)]) for i in range(4)])
            W.don = ring_f32("don", 256, 2)
            W.rt = ring_f32("rt", 256, 2)
            W.st8 = ring_f32("st8_", 16, 3)
            W.st4 = ring_f32("st4_", 8, 3)
            W.rden = ring_f32("rden", 4, 2)

        def p2_mm(c):
            g, kt, zt = c.g, c.kt, c.zt
            if c.kind != "cache":
                pap, pt = mmb.next()
                mm(pt, pap[:, 0:384], [(cqnT_v[:, k, zt * 128:(zt + 1) * 128], wuq_v[:, k, g * 384:(g + 1) * 384]) for k in range(3)],
                   [t_cq[zt], t_wuq[g]])
                c.qg, c.qgt = W.qg.next()
                act(c.qg, pap[:, 0:384], AF.Copy, [pt], [c.qgt])
                yield
            pap, pt = mmb.next()
            mm(pt, pap, [(ckvnT_v[:, k, kt * 128:(kt + 1) * 128], wukv_v[:, k, g * 512:(g + 1) * 512]) for k in range(2)],
               [t_ckv[kt], t_wukv[g]])
            p3 = pap.rearrange("p (h d) -> p h d", d=128)
            c.kf, c.kft = W.kf.next()
            act(c.kf.rearrange("p (h d) -> p h d", d=64), p3[:, :, 0:64], AF.Copy, [pt], [c.kft])
            act(Vg_v[:, kt, :, 0:64], p3[:, :, 64:128], AF.Copy, [pt], [t_Vg[kt]])
            yield

        def p2_norm(c):
            kt = c.kt
            if c.kind != "cache":
                qg, qgt = c.qg, c.qgt
                q3 = qg.rearrange("p (h d) -> p h d", d=96)
                ds, dst_ = W.dsq.next()
                d3 = ds.rearrange("p (h d) -> p h d", d=96)
                st8, t_st8 = W.st8.next()
                vop("dve", "tensor_tensor", [qgt], [dst_], out=ds, in0=qg, in1=qg, op=ALU.mult)
                vop("dve", "tensor_reduce", [dst_], [t_st8], out=st8[:, 0:4], in_=d3[:, :, 0:64], axis=AX.X, op=ALU.add)
                vop("dve", "tensor_reduce", [dst_], [t_st8], out=st8[:, 4:8], in_=d3[:, :, 64:96], axis=AX.X, op=ALU.add)
                vop("dve", "tensor_tensor", [t_st8, t_invn], [t_st8], out=st8[:, 0:8], in0=st8[:, 0:8], in1=invn8, op=ALU.mult)
                rstd_from(st8[:, 0:8], 8, t_st8, 1.0, st8[:, 8:16])
                c.st8, c.t_st8 = st8, t_st8
            kf, kft = c.kf, c.kft
            ds, dst_ = W.dsq.next()
            st4, t_st4 = W.st4.next()
            vop("dve", "tensor_tensor", [kft], [dst_], out=ds[:, 0:256], in0=kf, in1=kf, op=ALU.mult)
            vop("dve", "tensor_reduce", [dst_], [t_st4], out=st4[:, 0:4], in_=ds[:, 0:256].rearrange("p (h d) -> p h d", d=64),
                axis=AX.X, op=ALU.add)
            rstd_from(st4[:, 0:4], 4, t_st4, 1.0 / 64, st4[:, 4:8])
            c.st4, c.t_st4 = st4, t_st4
            yield
            if c.kind != "cache":
                st8, t_st8 = c.st8, c.t_st8
                c.qnb, c.qnbt = W.qnb.next()
                qn3 = c.qnb.rearrange("p (h d) -> p h d", d=96)
                vop("dve", "tensor_tensor", [qgt, t_st8], [qgt], out=q3[:, :, 0:64], in0=q3[:, :, 0:64],
                    in1=bc(st8[:, 0:4].unsqueeze(2), [128, 4, 64]), op=ALU.mult)
                vop("dve", "tensor_tensor", [qgt, t_gqn], [c.qnbt], out=qn3[:, :, 0:64], in0=q3[:, :, 0:64],
                    in1=bc(gqn.unsqueeze(1), [128, 4, 64]), op=ALU.mult)
                vop("dve", "tensor_tensor", [qgt, t_st8], [qgt], out=q3[:, :, 64:96], in0=q3[:, :, 64:96],
                    in1=bc(st8[:, 4:8].unsqueeze(2), [128, 4, 32]), op=ALU.mult)
                vop("dve", "tensor_tensor", [qgt, t_gqr], [qgt], out=q3[:, :, 64:96], in0=q3[:, :, 64:96],
                    in1=bc(gqr.unsqueeze(1), [128, 4, 32]), op=ALU.mult)
                yield
                rope(q3[:, :, 64:96], 4, c.cs, qn3[:, :, 64:80], qn3[:, :, 80:96], [qgt], [c.qnbt], W.rt)
                yield
            k3 = kf.rearrange("p (h d) -> p h d", d=64)
            c.knb, c.knbt = W.knb.next()
            kn3 = c.knb.rearrange("p (h d) -> p h d", d=96)
            vop("dve", "tensor_tensor", [kft, c.t_st4], [kft], out=k3, in0=k3, in1=bc(c.st4[:, 0:4].unsqueeze(2), [128, 4, 64]), op=ALU.mult)
            vop("dve", "tensor_tensor", [kft, t_gkn], [c.knbt], out=kn3[:, :, 0:64], in0=k3, in1=bc(gkn.unsqueeze(1), [128, 4, 64]),
                op=ALU.mult)
            vop("pool", "tensor_copy", [t_kr[kt]], [c.knbt], out=kn3[:, :, 64:96], in_=bc(krb_v[:, kt, :].unsqueeze(1), [128, 4, 32]))
            yield

        def p2_tr(c):
            kt = c.kt
            if c.kind != "cache":
                qn3 = c.qnb.rearrange("p (h d) -> p h d", d=96)
                transposes(t_tb, [(tb_ap[0:96, h * 128:(h + 1) * 128], qn3[:, h, :]) for h in range(4)], [c.qnbt, t_identb], identb)
                qT, qTt = W.qT.next()
                c.qT, c.qTt = qT.rearrange("p (h n) -> p h n", n=128), qTt
                evac_copy(qT[0:96, :], tb_ap[0:96, 0:512], [t_tb], [qTt])
                yield
            kn3 = c.knb.rearrange("p (h d) -> p h d", d=96)
            transposes(t_tb, [(tb_ap[0:96, h * 128:(h + 1) * 128], kn3[:, h, :]) for h in range(4)], [c.knbt, t_identb], identb)
            evac_copy(kTg_v[0:96, :, kt * 128:(kt + 1) * 128], tb_ap[0:96, 0:512].rearrange("p (h n) -> p h n", n=128), [t_tb], [t_kTg[kt]])
            yield

        def a2(c):
            if c.kind == "cache":
                return
            g, kt, zt = c.g, c.kt, c.zt
            kts = list(range(c.kt0, kt + 1))
            oap, ot = obanks.next()

            def norm_hook():
                ov = oap[:, 0:260].rearrange("p (h d) -> p h d", d=65)
                rden, t_rden = W.rden.next()
                don, t_don = W.don.next()
                vop("dve", "reciprocal", [ot], [t_rden], out=rden[:, 0:4], in_=ov[:, :, 64])
                vop("dve", "tensor_tensor", [ot, t_rden], [t_don], out=don.rearrange("p (h d) -> p h d", d=64),
                    in0=ov[:, :, 0:64], in1=bc(rden[:, 0:4].unsqueeze(2), [128, 4, 64]), op=ALU.mult)
                vop("pool", "tensor_tensor", [t_don, t_zsB[zt]], [t_zsB[zt]], out=zsB_v[:, zt, g * 256:(g + 1) * 256],
                    in0=don, in1=zsB_v[:, zt, g * 256:(g + 1) * 256], op=ALU.mult)

            tiles = []
            for hh in range(4):
                for k in kts:
                    masks = []
                    if k == kt:
                        masks = ([(slice(32, 64), 0, 128), (slice(64, 128), 0, 128)] if c.kind == "sample"
                                 else [(slice(64, 128), 0, 64)])
                    tiles.append(dict(l=kTg_v[:, hh, k * 128:(k + 1) * 128], r=c.qT[:, hh, :], rdq=[t_kTg[k], c.qTt],
                                      V=Vg_v[:, k, hh, :], Vt=t_Vg[k], o=oap[:, hh * 65:(hh + 1) * 65], ot=ot,
                                      start=(k == kts[0]), stop=(k == kts[-1]), emul=None, masks=masks,
                                      after=(norm_hook if (hh == 3 and k == kts[-1]) else None)))
            yield from attn_stream(tiles, SCB, sbanks, W.PT)

        def sweep3_bufs():
            do_barrier()
            A.reset(work_mark)
            W.xs = ring_f32("x3_", 1024, 4)
            W.ogT = ring_bf16("ogT", 1024, 3)
            W.dt = ring_f32("dt3_", 1024, 2)

        def b3_load(c):
            if c.kind == "cache":
                return
            c.x, c.xt = W.xs.next()
            dma_in(c.x, c.ysrc, c.xt)
            return
            yield

        def b3a(c):
            if c.kind == "cache":
                return
            zt = c.zt
            transposes(t_tb, [(tb_ap[:, k * 128:(k + 1) * 128], zsB_v[:, zt, k * 128:(k + 1) * 128]) for k in range(8)],
                       [t_zsB[zt], t_identb], identb)
            ogT, ogTt = W.ogT.next()
            c.ogT, c.ogTt = ogT.rearrange("p (k n) -> p k n", n=128), ogTt
            evac_copy(ogT, tb_ap, [t_tb], [ogTt])
            yield

        def b3b(c):
            if c.kind == "cache":
                return
            dt_, dtt = W.dt.next()
            for cg in range(2):
                pap, pt = mmb.next()
                mm(pt, pap, [(c.ogT[:, k, :], wBout_v[:, k, cg * 512:(cg + 1) * 512]) for k in range(8)], [c.ogTt, t_wBout[cg]])
                cs_ = slice(cg * 512, cg * 512 + 512)
                vop("dve", "tensor_tensor", [pt, t_gateB], [dtt], out=dt_[:, cs_], in0=pap, in1=gateB[:, cs_], op=ALU.mult)
                vop("dve", "tensor_tensor", [dtt, c.xt], [c.xt], out=c.x[:, cs_], in0=dt_[:, cs_], in1=c.x[:, cs_], op=ALU.add)
                yield
            dma_out(c.y_dst, c.x, c.xt)

        import os
        DBG_B = os.environ.get("DBG_B", "")

        def run_seqB(s, tiles):
            if DBG_B == "w" or (DBG_B and s > 0):
                return
            sweep1_bufs(s)
            run_pipeline(tiles, [s1_load, s1_norm, s1_tr, s1_mm, s1_e1, s1_e2])
            if DBG_B == "s1":
                return
            sweep2_bufs()
            for g in range(4):
                for c in tiles:
                    c.g = g
                run_pipeline(tiles, [p2_mm, p2_norm, p2_tr, a2])
                if DBG_B == "s2":
                    return
            sweep3_bufs()
            run_pipeline(tiles, [b3_load, b3a, b3b])

        for s in range(nseq):
            tiles = []
            for t in range(NT):
                c = Ctx()
                c.kind = "prompt"
                c.pre = None
                c.kt, c.zt, c.cs, c.kt0 = t, t, t, 0
                r = slice(s * S + t * 128, s * S + (t + 1) * 128)
                c.ysrc = y0p[r, :]
                c.ckv_dst, c.kr_dst, c.y_dst = O["bcp"][r, :], O["brp"][r, :], O["yp"][r, :]
                tiles.append(c)
            run_seqB(s, tiles)
        tiles = []
        for t in range(NTC):
            c = Ctx()
            c.kind = "cache"
            c.pre = None
            c.kt = t
            c.zt = 0
            tiles.append(c)
        c = Ctx()
        c.kind = "sample"
        c.pre = None
        c.kt, c.zt, c.cs, c.kt0 = NTC, 0, NT, 0
        c.ysrc = y0s
        c.ckv_dst, c.kr_dst, c.y_dst = O["bcs"], O["brs"], O["ys"]
        tiles.append(c)
        run_seqB(nseq, tiles)

    sch.emit(nc, stack)
    stack.close()
    return nc


def build_layer_B(env):
    raise NotImplementedError


ROPE_THETA = 10000.0
B_COLS = np.concatenate([np.arange(672, 1696), np.arange(0, 672)])


def rope_tables(pos):
    inv = (np.float32(ROPE_THETA) ** (-np.arange(16, dtype=np.float32) / np.float32(16))).astype(np.float32)
    ang = pos.astype(np.float32)[:, None] * inv[None, :]
    return np.cos(ang).astype(np.float32), np.sin(ang).astype(np.float32)


def shared_inputs(inp, S, PAST):
    f = lambda a: np.ascontiguousarray(np.asarray(a, dtype=np.float32))
    sh = {}
    sh["norm_g"] = f(inp["norm_g"])
    sh["ada_w"] = f(inp["ada_w"]).reshape(2 * D, 3 * D)
    sh["ada_b"] = f(inp["ada_b"])
    w = f(inp["a_w_in"])[0]
    w = np.concatenate([w[:, 0:1024][:, PERM_COLS], w[:, 1024:2048][:, PERM_COLS], w[:, 2048:]], axis=1)
    sh["a_w_in"] = f(w)
    sh["a_g_q"] = f(inp["a_g_q"])
    sh["a_g_k"] = f(inp["a_g_k"])
    tab = f(inp["a_rel_bias"])[0]
    ki = np.arange(128)[:, None, None]
    jj = np.arange(2)[None, :, None]
    qi = np.arange(128)[None, None, :]
    rel = np.clip((4 - (3 + jj)) * 128 + qi - ki, -128, 128) + 128
    sh["erel"] = f(np.transpose(tab[:, rel], (1, 0, 2, 3)).reshape(128, 16 * 2 * 128))
    sh["cbias"] = f(tab[:, 256][None, :])
    sh["a_w_out"] = f(inp["a_w_out"])[0]
    sh["b_w_in"] = f(f(inp["b_w_in"])[0][:, B_COLS])
    for k in ("b_g_cq", "b_g_ckv", "b_g_qn", "b_g_qr", "b_g_kn", "b_g_kr"):
        sh[k] = f(inp[k])
    sh["b_w_uq"] = f(inp["b_w_uq"])[0]
    sh["b_w_ukv"] = f(inp["b_w_ukv"])[0]
    sh["b_w_out"] = f(inp["b_w_out"])[0]
    cp, sp_ = rope_tables(np.arange(S))
    NT = S // 128
    sh["cosp"] = f(cp.reshape(NT, 128, 16).transpose(1, 0, 2).reshape(128, NT * 16))
    sh["sinp"] = f(sp_.reshape(NT, 128, 16).transpose(1, 0, 2).reshape(128, NT * 16))
    sh["coss"], sh["sins"] = rope_tables(PAST + np.arange(128))
    sh["ident"] = np.eye(128, dtype=np.float32)
    return sh


def core_inputs(inp, sh, core, nseq, S, PAST):
    f = lambda a: np.ascontiguousarray(np.asarray(a, dtype=np.float32))
    m = dict(sh)
    m["xp"] = f(inp["x_prompt"][core * nseq:(core + 1) * nseq]).reshape(nseq * S, D)
    xs = np.zeros((128, D), np.float32)
    xs[:32] = np.asarray(inp["x_sample"][core])
    m["xs"] = xs
    m["ck"] = f(np.asarray(inp["cache_a_k"])[0, core].reshape(512, D)[:, PERM_COLS])
    m["cv"] = f(np.asarray(inp["cache_a_v"])[0, core].reshape(512, D))
    m["cckv"] = f(np.asarray(inp["cache_mla_ckv"])[0, core])
    m["ckr"] = f(np.asarray(inp["cache_mla_krope"])[0, core])
    c = np.concatenate([np.asarray(inp["c_prompt"])[core * nseq:(core + 1) * nseq], np.asarray(inp["c_sample"])[core:core + 1]], 0)
    NS = nseq + 1
    m["scT"] = f(c.T.reshape(8, 128, NS).transpose(1, 0, 2).reshape(128, 8 * NS))
    return m


_NC_CACHE = {}


def run_cores(inp, ncores, nseq, S, PAST, debug_layers=2):
    key = (nseq, S, PAST, debug_layers)
    if key not in _NC_CACHE:
        _NC_CACHE[key] = build(nseq, S, PAST, debug_layers)
    nc = _NC_CACHE[key]
    sh = shared_inputs(inp, S, PAST)
    in_maps = [core_inputs(inp, sh, c, nseq, S, PAST) for c in range(ncores)]
    res = run_bass_kernel_spmd(nc, in_maps, core_ids=list(range(ncores)))
    return res.results


def assemble(results, ncores, nseq, S):
    AR = min(512, S)
    inv = np.empty(1024, np.int64)
    inv[PERM_COLS] = np.arange(1024)
    cat = lambda k: np.concatenate([r[k] for r in results], 0)
    y_p = cat("yp").reshape(ncores * nseq, S, D)
    y_s = np.stack([r["ys"][:32] for r in results], 0)
    akp = cat("akp")[:, inv].reshape(1, ncores * nseq, AR, 16, 64)
    avp = cat("avp").reshape(1, ncores * nseq, AR, 16, 64)
    aks = np.stack([r["aks"][:32][:, inv] for r in results], 0).reshape(1, ncores, 32, 16, 64)
    avs = np.stack([r["avs"][:32] for r in results], 0).reshape(1, ncores, 32, 16, 64)
    bcp = cat("bcp").reshape(1, ncores * nseq, S, 256)
    brp = cat("brp").reshape(1, ncores * nseq, S, 32)
    bcs = np.stack([r["bcs"][:32] for r in results], 0).reshape(1, ncores, 32, 256)
    brs = np.stack([r["brs"][:32] for r in results], 0).reshape(1, ncores, 32, 32)
    return tuple(np.ascontiguousarray(a, dtype=np.float32) for a in (y_p, y_s, akp, avp, aks, avs, bcp, brp, bcs, brs))


def kernel(**inputs):
    res = run_cores(inputs, NCORES, 4, 2048, 2048)
    return assemble(res, NCORES, 4, 2048)
```

```python
import numpy as np
import concourse.bass as bass
import concourse.mybir as mybir
from concourse.bass_utils import run_bass_kernel_spmd
from contextlib import ExitStack

F32 = mybir.dt.float32
BF16 = mybir.dt.bfloat16
AF = mybir.ActivationFunctionType
ALU = mybir.AluOpType
AX = mybir.AxisListType

D = 1024
EPS = 1e-6
NCORES = 8


class T:
    __slots__ = ("name", "w", "r")

    def __init__(self, name=""):
        self.name = name
        self.w = None
        self.r = {}


class Op:
    __slots__ = ("eng", "fn", "deps", "dma", "val", "needed", "kind")


class Sched:
    ENG = ("pe", "act", "dve", "pool", "sp")

    def __init__(self):
        self.q = {e: [] for e in self.ENG}
        self.dmacnt = {}
        self.lastdma = {}
        self.nbar = 0

    def op(self, eng, fn, rd=(), wr=(), dma=None, kind="c", extra_deps=()):
        o = Op()
        o.eng, o.fn, o.dma, o.kind = eng, fn, dma, kind
        o.needed = False
        o.val = None
        deps = list(extra_deps)
        for t in rd:
            if t.w is not None:
                deps.append((t.w, 0))
        for t in wr:
            if t.w is not None:
                deps.append((t.w, 1))
            for r in t.r.values():
                if r is not o:
                    deps.append((r, 2))
        o.deps = deps
        for d, k in deps:
            if d.dma is None:
                if d.eng != eng or eng != "pe":
                    d.needed = True
        wrs = set(id(t) for t in wr)
        for t in wr:
            t.w = o
            t.r = {}
        key = dma if dma is not None else eng
        for t in rd:
            if id(t) not in wrs:
                t.r[key] = o
        if dma is not None:
            c = self.dmacnt.get(dma, 0) + 16
            self.dmacnt[dma] = c
            o.val = c
            self.lastdma[dma] = o
        self.q[eng].append(o)
        return o

    def barrier(self, markers, pe_wr=()):
        bt = [T("bar%d_%s" % (self.nbar, e)) for e in self.ENG]
        self.nbar += 1
        alld = [(o, 0) for o in self.lastdma.values()]
        for e, t in zip(self.ENG, bt):
            if e == "sp":
                self.op("sp", None, wr=[t], kind="seminc", extra_deps=alld)
            elif e == "pool":
                self.op("pool", markers[e], wr=[t], extra_deps=alld)
            elif e == "pe":
                self.op(e, markers[e], wr=[t] + list(pe_wr))
            else:
                self.op(e, markers[e], wr=[t])
        for e in self.ENG:
            self.op(e, None, rd=bt, kind="wait")

    def emit(self, nc, stack):
        for e in self.ENG:
            cnt = 0
            for o in self.q[e]:
                if o.dma is None:
                    if o.needed:
                        cnt += 1
                    o.val = cnt
        sems = {}
        for e in self.ENG:
            sems[e] = stack.enter_context(nc.semaphore("s_" + e))
        for k in self.dmacnt:
            sems[k] = stack.enter_context(nc.semaphore("d_" + k))
        self.sems = sems
        block = stack.enter_context(nc.Block())
        final = [(k, v) for k, v in self.dmacnt.items()]

        def run(eng, ename):
            waited = {}
            for o in self.q[ename]:
                for d, kind in o.deps:
                    key = d.dma if d.dma is not None else d.eng
                    if d.dma is None and d.eng == ename and ename == "pe":
                        continue
                    if waited.get(key, 0) >= d.val:
                        continue
                    eng.wait_ge(sems[key], d.val)
                    waited[key] = d.val
                if o.kind == "wait":
                    continue
                if o.kind == "seminc":
                    if o.needed:
                        eng.sem_inc(sems[ename], 1)
                    continue
                ins = o.fn(eng)
                if o.dma is not None:
                    ins.then_inc(sems[o.dma], 16)
                elif o.needed:
                    ins.then_inc(sems[ename], 1)
            if ename in ("sp", "pool"):
                for k, v in final:
                    if waited.get(k, 0) < v:
                        eng.wait_ge(sems[k], v)

        @block.sync
        def _(e):
            run(e, "sp")

        @block.scalar
        def _(e):
            run(e, "act")

        @block.vector
        def _(e):
            run(e, "dve")

        @block.gpsimd
        def _(e):
            run(e, "pool")

        @block.tensor
        def _(e):
            run(e, "pe")


class Ring:
    def __init__(self, items):
        self.items = items
        self.i = 0

    def next(self):
        it = self.items[self.i % len(self.items)]
        self.i += 1
        return it


class Arena:
    def __init__(self, ar, nbytes):
        self.ar = ar
        self.cap = nbytes
        self.off = 0
        self.marks = []

    def _take(self, nbytes):
        nbytes = (nbytes + 63) // 64 * 64
        o = self.off
        self.off += nbytes
        assert self.off <= self.cap, "SBUF arena overflow %d > %d" % (self.off, self.cap)
        return o

    def f32(self, n, parts=128):
        o = self._take(n * 4)
        return self.ar[0:parts, o // 4:o // 4 + n]

    def bf16(self, n, parts=128):
        o = self._take(n * 2)
        return self.ar[0:parts, o // 4:o // 4 + (n + 1) // 2].bitcast(BF16)[:, 0:n]

    def mark(self):
        return self.off

    def reset(self, m):
        self.off = m


PERM_COLS = np.array([(j8 + 8 * rg) * 64 + d for j8 in range(8) for rg in range(2) for d in range(64)])


def build(nseq, S, PAST, debug_layers=2):
    NT = S // 128
    NS = nseq + 1
    NTC = PAST // 128
    AR = min(512, S)
    NAO = AR // 128
    NTB = max(NT, NTC + 1)
    nc = bass.Bass("TRN2", target_bir_lowering=False)

    def din(name, shape):
        return nc.dram_tensor(name, list(shape), F32, kind="ExternalInput").ap()

    def dout(name, shape):
        return nc.dram_tensor(name, list(shape), F32, kind="ExternalOutput").ap()

    def dscr(name, shape):
        return nc.dram_tensor(name, list(shape), F32).ap()

    I = {}
    for name, shape in [
        ("xp", (nseq * S, D)), ("xs", (128, D)), ("ck", (512, D)), ("cv", (512, D)),
        ("cckv", (PAST, 256)), ("ckr", (PAST, 32)), ("scT", (128, 8 * NS)),
        ("norm_g", (2, D)), ("ada_w", (2 * D, 3 * D)), ("ada_b", (2, 3 * D)),
        ("a_w_in", (D, 4 * D)), ("a_g_q", (1, 64)), ("a_g_k", (1, 64)),
        ("erel", (128, 16 * 2 * 128)), ("cbias", (1, 16)), ("a_w_out", (D, D)),
        ("b_w_in", (D, 1696)), ("b_g_cq", (1, 384)), ("b_w_uq", (384, 1536)), ("b_g_ckv", (1, 256)),
        ("b_w_ukv", (256, 2048)), ("b_g_qn", (1, 64)), ("b_g_qr", (1, 32)), ("b_g_kn", (1, 64)),
        ("b_g_kr", (1, 32)), ("b_w_out", (D, D)),
        ("cosp", (128, NT * 16)), ("sinp", (128, NT * 16)), ("coss", (128, 16)), ("sins", (128, 16)),
        ("ident", (128, 128)),
    ]:
        I[name] = din(name, shape)
    O = {}
    for name, shape in [
        ("yp", (nseq * S, D)), ("ys", (128, D)), ("akp", (nseq * AR, D)), ("avp", (nseq * AR, D)),
        ("aks", (128, D)), ("avs", (128, D)), ("bcp", (nseq * S, 256)), ("brp", (nseq * S, 32)),
        ("bcs", (128, 256)), ("brs", (128, 32)),
    ]:
        O[name] = dout(name, shape)
    y0p = dscr("y0p", (nseq * S, D))
    y0s = dscr("y0s", (128, D))
    modd = dscr("modd", (2 * NS, 3 * D))

    sch = Sched()
    stack = ExitStack()
    ARENA_BYTES = 207 * 1024
    arena_t = stack.enter_context(nc.sbuf_tensor("arena", [128, ARENA_BYTES // 4], F32))
    A = Arena(arena_t, ARENA_BYTES)
    psb = [stack.enter_context(nc.psum_tensor("psb%d" % i, [128, 512], F32)) for i in range(8)]

    def tl_(t):
        return list(t) if isinstance(t, (list, tuple)) else [t]

    def dma_in(out_ap, in_ap, t, rd=(), key=None):
        sch.op("sp", lambda e, o=out_ap, i=in_ap: e.dma_start(out=o, in_=i), rd=rd, wr=tl_(t), dma=key or ("l_" + tl_(t)[0].name))

    def dma_out(out_ap, in_ap, t, wr=(), key=None):
        sch.op("pool", lambda e, o=out_ap, i=in_ap: e.dma_start(out=o, in_=i), rd=tl_(t), wr=wr, dma=key or ("s_" + tl_(t)[0].name))

    def mm(pst, out_ap, pairs, rd, start=True, stop=True):
        def fn(e, out_ap=out_ap, pairs=pairs, start=start, stop=stop):
            n = len(pairs)
            for i, (l, r) in enumerate(pairs):
                ins = e.matmul(out_ap, lhsT=l, rhs=r, start=(start and i == 0), stop=(stop and i == n - 1))
            return ins
        sch.op("pe", fn, rd=rd, wr=[pst])

    def transposes(pst, items, rd, ident):
        def fn(e, items=items, ident=ident):
            for o, i in items:
                ins = e.transpose(out=o, in_=i, identity=ident)
            return ins
        sch.op("pe", fn, rd=rd, wr=[pst])

    def act(out, in_, func, rd, wr, **kw):
        sch.op("act", lambda e, o=out, i=in_, f=func, kw=kw: e.activation(out=o, in_=i, func=f, **kw), rd=rd, wr=wr)

    def vop(eng, name, rd, wr, *args, **kw):
        sch.op(eng, lambda e, name=name, args=args, kw=kw: getattr(e, name)(*args, **kw), rd=rd, wr=wr)

    cp_rr = [0]

    def evac_copy(out, in_, rd, wr):
        cp_rr[0] += 1
        if cp_rr[0] % 2:
            act(out, in_, AF.Copy, rd, wr)
        else:
            vop("dve", "tensor_copy", rd, wr, out=out, in_=in_)

    def bc(ap, shape):
        return ap.to_broadcast(list(shape))

    identf = A.f32(128)
    identb = A.bf16(128)
    t_ident = T("ident")
    dma_in(identf, I["ident"], t_ident)
    t_identb = T("identb")
    act(identb, identf, AF.Copy, [t_ident], [t_identb])
    bar_scr = {e: A.f32(16) for e in ("act", "dve", "pool")}
    t_scr = T("barscr")

    def do_barrier():
        sch.barrier({
            "act": lambda e: e.activation(out=bar_scr["act"], in_=identf[:, 0:16], func=AF.Copy),
            "dve": lambda e: e.tensor_copy(out=bar_scr["dve"], in_=identf[:, 0:16]),
            "pool": lambda e: e.tensor_copy(out=bar_scr["pool"], in_=identf[:, 0:16]),
            "pe": lambda e: e.transpose(out=psb[2][:, :].bitcast(BF16)[:, 0:128], in_=identb, identity=identb),
        }, pe_wr=[t_tb])

    mmb = Ring([(psb[0][:, :], T("mm0")), (psb[1][:, :], T("mm1"))])
    tb_ap = psb[2][:, :].bitcast(BF16)
    t_tb = T("tb")

    def load_bc(name, n):
        ap = A.f32(n)
        t = T("g_" + name)
        dma_in(ap, bc(I[name][0:1, :], [128, n]), t)
        return ap, t

    gq, t_gq = load_bc("a_g_q", 64)
    gk, t_gk = load_bc("a_g_k", 64)
    cb, t_cb = load_bc("cbias", 16)
    gcq, t_gcq = load_bc("b_g_cq", 384)
    gckv, t_gckv = load_bc("b_g_ckv", 256)
    gqn, t_gqn = load_bc("b_g_qn", 64)
    gqr, t_gqr = load_bc("b_g_qr", 32)
    gkn, t_gkn = load_bc("b_g_kn", 64)
    gkr, t_gkr = load_bc("b_g_kr", 32)
    ncb = A.f32(16)
    t_ncb = T("ncb")
    vop("dve", "tensor_scalar", [t_cb], [t_ncb], out=ncb, in0=cb, scalar1=-1.0, scalar2=None, op0=ALU.mult)
    invn3 = A.f32(3)
    invn8 = A.f32(8)
    t_invn = T("invn")
    for ap_, v in ((invn3[:, 0:1], 1.0 / 384), (invn3[:, 1:2], 1.0 / 256), (invn3[:, 2:3], 1.0 / 32),
                   (invn8[:, 0:4], 1.0 / 64), (invn8[:, 4:8], 1.0 / 32)):
        vop("pool", "memset", [], [t_invn], ap_, v)

    def stats_tile(n):
        return A.f32(n)

    persist_mark = A.mark()

    wA_in = A.bf16(8 * 4096)
    wA_in_v = wA_in.rearrange("p (k n) -> p k n", n=4096)
    wA_out = A.bf16(8 * 1024)
    wA_out_v = wA_out.rearrange("p (k n) -> p k n", n=1024)
    t_wAin = [T("wAin%d" % i) for i in range(8)]
    t_wAout = [T("wAout%d" % i) for i in range(2)]
    Eb = A.bf16(16 * 2 * 128)
    Ebv = Eb.rearrange("p (h j q) -> p h j q", j=2, q=128)
    t_E = T("E")
    M0 = A.bf16(128)
    t_M0 = T("M0")
    vop("pool", "memset", [], [t_M0], M0, 1.0)
    vop("pool", "memset", [], [t_M0], M0[0:64, 64:128], 0.0)
    wA_mark = A.mark()

    stg = Ring([(A.f32(2048), T("stg%d" % i)) for i in range(4)])
    cast_rr = [0]

    def cast(out, in_, rd, wr):
        cast_rr[0] += 1
        k = cast_rr[0] % 3
        if k == 0:
            act(out, in_, AF.Copy, rd, wr)
        elif k == 1:
            vop("dve", "tensor_copy", rd, wr, out=out, in_=in_)
        else:
            vop("pool", "tensor_copy", rd, wr, out=out, in_=in_)

    def load_weight(src, K, N, dst_v, tlist, tcols):
        cw = 2048 // K
        for c0 in range(0, N, cw):
            w_ = min(cw, N - c0)
            sap, st = stg.next()
            sv = sap[:, 0:K * w_].rearrange("p (k n) -> p k n", n=w_)
            dma_in(sv, src[:, c0:c0 + w_].rearrange("(k p) n -> p k n", p=128), st)
            cast(dst_v[:, :, c0:c0 + w_], sv, [st], [tlist[c0 // tcols]])

    sc = A.f32(8 * NS)
    t_sc = T("sc")
    dma_in(sc, I["scT"], t_sc)
    act(sc, sc, AF.Silu, [t_sc], [t_sc])
    ones1 = A.f32(8, parts=1)
    t_ones = T("ones1")
    vop("dve", "memset", [], [t_ones], ones1, 1.0)
    adab = Ring([(A.f32(256, parts=1), T("adab%d" % i)) for i in range(4)])
    modc = Ring([(A.f32(256, parts=NS), T("modc%d" % i)) for i in range(4)])
    scv = sc.rearrange("p (k s) -> p k s", s=NS)
    for l in range(2):
        for c in range(12):
            c0 = c * 256
            sap, st = stg.next()
            sv = sap.rearrange("p (k n) -> p k n", n=256)
            dma_in(sv, I["ada_w"][l * D:(l + 1) * D, c0:c0 + 256].rearrange("(k p) n -> p k n", p=128), st)
            bap, bt = adab.next()
            dma_in(bap, I["ada_b"][l:l + 1, c0:c0 + 256], bt)
            pap, pt = mmb.next()
            pairs = [(scv[:, k, :], sv[:, k, :]) for k in range(8)] + [(ones1[0:1, 0:NS], bap)]
            mm(pt, pap[0:NS, 0:256], pairs, [t_sc, st, bt, t_ones])
            map_, mt = modc.next()
            act(map_, pap[0:NS, 0:256], AF.Copy, [pt], [mt])
            dma_out(modd[l * NS:(l + 1) * NS, c0:c0 + 256], map_, mt)

    load_weight(I["a_w_in"], 8, 4096, wA_in_v, t_wAin, 512)
    load_weight(I["a_w_out"], 8, 1024, wA_out_v, t_wAout, 512)
    for hh in range(0, 16, 8):
        sap, st = stg.next()
        dma_in(sap, I["erel"][:, hh * 256:(hh + 8) * 256], st)
        for h in range(hh, hh + 8):
            act(Eb[:, h * 256:(h + 1) * 256], sap[:, (h - hh) * 256:(h - hh + 1) * 256], AF.Exp, [st, t_ncb], [t_E],
                bias=ncb[:, h:h + 1])
    vop("pool", "memset", [], [t_E], Ebv[64:128, :, 1, 0:64], 0.0)
    do_barrier()

    A.reset(wA_mark)
    KR = 6
    kT = A.bf16(8 * KR * 128)
    kTv = kT.rearrange("p (j n) -> p j n", n=KR * 128)
    t_kT = [T("kT%d" % i) for i in range(KR)]
    Vp = A.bf16(KR * 16 * 65)
    Vpv = Vp.rearrange("p (s h d) -> p s h d", h=16, d=65)
    t_Vp = [T("Vp%d" % i) for i in range(KR)]
    vop("pool", "memset", [], t_Vp, Vpv[:, :, :, 64:65], 1.0)
    xs_ = Ring([(A.f32(1024), T("x%d" % i)) for i in range(6)])
    dtmp = A.f32(1024)
    t_dtmp = (T("dtmp_a"), T("dtmp_b"))
    hb = A.bf16(1024)
    t_hb = (T("hb_a"), T("hb_b"))
    hTs = Ring([(A.bf16(1024), T("hT%d" % i)) for i in range(2)])
    pfs = Ring([(A.f32(1024), (T("pf%d_a" % i), T("pf%d_b" % i))) for i in range(3)])
    qnb = A.bf16(1024)
    t_qnb = (T("qnb_a"), T("qnb_b"))
    knb = A.bf16(1024)
    t_knb = (T("knb_a"), T("knb_b"))
    qTs = Ring([(A.bf16(1024), T("qT%d" % i)) for i in range(2)])
    zss = Ring([(A.bf16(1024), T("zs%d" % i)) for i in range(3)])
    PTs = Ring([(A.bf16(512), [T("PT%d_%d" % (i, k)) for k in range(4)]) for i in range(3)])
    ogs = Ring([(A.bf16(1024), T("og%d" % i)) for i in range(2)])
    ogT = A.bf16(1024)
    t_ogT = T("ogT")
    bc3 = A.f32(2048)
    t_bc3 = T("bc3")
    gates = Ring([(A.f32(1024), T("gate%d" % i)) for i in range(2)])
    cur_gate = [None]
    G1 = A.f32(1024)
    t_G1 = T("G1")
    st_ss = A.f32(4)
    t_ss = T("ss")
    st_16 = A.f32(64)
    t_16 = (T("st16_a"), T("st16_b"))
    rden = A.f32(16)
    t_rden = T("rden")
    sringA = Ring([(psb[3][:, :], T("sa0")), (psb[4][:, :], T("sa1")), (psb[7][:, :], T("sa2"))])
    t_O = [T("O0"), T("O1")]

    def o_ap(h):
        if h < 7:
            return psb[5][:, h * 65:(h + 1) * 65], t_O[0]
        if h < 14:
            return psb[6][:, (h - 7) * 65:(h - 6) * 65], t_O[1]
        return psb[5][:, (h - 14) * 65:(h - 13) * 65], t_O[0]

    def load_mod(l, s):
        dma_in(bc3, bc(modd[l * NS + s:l * NS + s + 1, 0:2048], [128, 2048]), t_bc3, key="l_bc3")
        gap, gt = gates.next()
        dma_in(gap, bc(modd[l * NS + s:l * NS + s + 1, 2048:3072], [128, 1024]), gt)
        cur_gate[0] = (gap, gt)
        dma_in(G1, bc(I["norm_g"][l:l + 1, :], [128, 1024]), t_G1, key="l_G1")
        vop("dve", "scalar_tensor_tensor", [t_bc3, t_G1], [t_G1], out=G1, in0=bc3[:, 1024:2048], scalar=1.0, in1=G1,
            op0=ALU.add, op1=ALU.mult)

    def rstd_from(ss_ap, n, tss, scale, scr_ap):
        act(scr_ap, ss_ap, AF.Ln, tl_(tss), tl_(tss), scale=scale, bias=EPS)
        act(ss_ap, scr_ap, AF.Exp, tl_(tss), tl_(tss), scale=-0.5)

    def ada_norm_tile(xap, xt):
        act(junk, xap, AF.Square, [xt], [t_junk, t_ss], accum_out=st_ss[:, 0:1])
        rstd_from(st_ss[:, 0:1], 1, t_ss, 1.0 / D, st_ss[:, 1:2])
        vop("dve", "scalar_tensor_tensor", [xt, t_ss, t_G1], [t_dtmp], out=dtmp, in0=xap, scalar=st_ss[:, 0:1], in1=G1,
            op0=ALU.mult, op1=ALU.mult)
        vop("dve", "tensor_tensor", [t_dtmp, t_bc3], [t_hb], out=hb, in0=dtmp, in1=bc3[:, 0:1024], op=ALU.add)
        transposes(t_tb, [(tb_ap[:, k * 128:(k + 1) * 128], hb[:, k * 128:(k + 1) * 128]) for k in range(8)],
                   [t_hb, t_identb], identb)
        hT, hTt = hTs.next()
        evac_copy(hT, tb_ap, [t_tb], [hTt])
        return hT.rearrange("p (k n) -> p k n", n=128), hTt

    def head_rms(pf, pft, g_ap, g_t, out_bf, out_t, also_f32):
        gb = bc(g_ap.unsqueeze(1), [128, 8, 64])
        for hf in range(2):
            cs_ = slice(hf * 512, hf * 512 + 512)
            pv = pf[:, cs_].rearrange("p (h d) -> p h d", d=64)
            dv = dtmp[:, cs_].rearrange("p (h d) -> p h d", d=64)
            s16 = st_16[:, hf * 8:hf * 8 + 8]
            vop("dve", "tensor_tensor", [pft[hf]], [t_dtmp[hf]], out=dtmp[:, cs_], in0=pf[:, cs_], in1=pf[:, cs_], op=ALU.mult)
            vop("dve", "tensor_reduce", [t_dtmp[hf]], [t_16[hf]], out=s16, in_=dv, axis=AX.X, op=ALU.add)
            yield
        rstd_from(st_16[:, 0:16], 16, t_16, 1.0 / 64, st_16[:, 16:32])
        yield
        for hf in range(2):
            cs_ = slice(hf * 512, hf * 512 + 512)
            pv = pf[:, cs_].rearrange("p (h d) -> p h d", d=64)
            s16 = st_16[:, hf * 8:hf * 8 + 8]
            vop("dve", "tensor_tensor", [pft[hf], t_16[hf]], [pft[hf]], out=pv, in0=pv, in1=bc(s16.unsqueeze(2), [128, 8, 64]), op=ALU.mult)
            yield
            if also_f32:
                vop("dve", "tensor_tensor", [pft[hf], g_t], [pft[hf]], out=pv, in0=pv, in1=gb, op=ALU.mult)
                act(out_bf[:, cs_], pf[:, cs_], AF.Copy, [pft[hf]], [out_t[hf]])
            else:
                vop("dve", "tensor_tensor", [pft[hf], g_t], [out_t[hf]], out=out_bf[:, cs_].rearrange("p (h d) -> p h d", d=64), in0=pv, in1=gb,
                    op=ALU.mult)
            yield

    class Ctx:
        pass

    def sample_cache_tile(c):
        j = c.j
        slot = c.ti % KR
        pf, pft = pfs.next()
        dma_in(pf, I["ck"][j * 128:(j + 1) * 128, :], pft)
        vop("pool", "tensor_copy", list(pft), list(t_knb), out=knb, in_=pf)
        transposes(t_tb, [(tb_ap[:, q * 128:(q + 1) * 128], knb[:, q * 128:(q + 1) * 128]) for q in range(8)],
                   list(t_knb) + [t_identb], identb)
        evac_copy(kTv[:, :, slot * 128:(slot + 1) * 128], tb_ap.rearrange("p (j n) -> p j n", n=128), [t_tb], [t_kT[slot]])
        pf2, pft2 = pfs.next()
        dma_in(pf2, I["cv"][j * 128:(j + 1) * 128, :], pft2)
        vop("dve", "tensor_copy", list(pft2), [t_Vp[slot]], out=Vpv[:, slot, :, 0:64], in_=pf2.rearrange("p (h d) -> p h d", d=64))

    def load_x(c):
        c.x, c.xt = xs_.next()
        dma_in(c.x, c.xsrc, c.xt)

    def norm_part(c):
        if c.premod is not None:
            c.premod()
        c.gate = cur_gate[0]
        xap, xt = c.x, c.xt
        act(hb, xap, AF.Square, [xt], list(t_hb) + [t_ss], accum_out=st_ss[:, 0:1])
        rstd_from(st_ss[:, 0:1], 1, t_ss, 1.0 / D, st_ss[:, 1:2])
        for hf in range(2):
            cs_ = slice(hf * 512, hf * 512 + 512)
            vop("dve", "scalar_tensor_tensor", [xt, t_ss, t_G1], [t_dtmp[hf]], out=dtmp[:, cs_], in0=xap[:, cs_], scalar=st_ss[:, 0:1],
                in1=G1[:, cs_], op0=ALU.mult, op1=ALU.mult)
            vop("dve", "tensor_tensor", [t_dtmp[hf], t_bc3], [t_hb[hf]], out=hb[:, cs_], in0=dtmp[:, cs_], in1=bc3[:, cs_], op=ALU.add)
            yield

    def hT_part(c):
        transposes(t_tb, [(tb_ap[:, k * 128:(k + 1) * 128], hb[:, k * 128:(k + 1) * 128]) for k in range(8)],
                   list(t_hb) + [t_identb], identb)
        hT, hTt = hTs.next()
        evac_copy(hT, tb_ap, [t_tb], [hTt])
        c.hT, c.hTt = hT.rearrange("p (k n) -> p k n", n=128), hTt

    def frontA(c):
        for c2 in c.load_list:
            load_x(c2)
        if c.kind == "cache":
            sample_cache_tile(c)
            yield
            if c.norm_next is not None:
                yield from norm_part(c.norm_next)
                hT_part(c.norm_next)
            return
        hT, hTt = c.hT, c.hTt
        slot = c.ti % KR
        c.slot = slot
        pfq, pfqt = pfs.next()
        pfk, pfkt = pfs.next()
        zs, zst = zss.next()
        c.zs, c.zst = zs, zst
        pfv = pfvt = None
        if c.store_kv is not None:
            pfv, pfvt = pfs.next()
        for cg in range(8):
            pap, pt = mmb.next()
            mm(pt, pap, [(hT[:, k, :], wA_in_v[:, k, cg * 512:(cg + 1) * 512]) for k in range(8)],
               [hTt, t_wAin[cg]])
            half = slice((cg % 2) * 512, (cg % 2) * 512 + 512)
            if cg < 2:
                act(pfq[:, half], pap, AF.Copy, [pt], [pfqt[cg % 2]])
            elif cg < 4:
                act(pfk[:, half], pap, AF.Copy, [pt], [pfkt[cg % 2]])
            elif cg < 6:
                h0 = (cg - 4) * 8
                if pfv is not None:
                    act(pfv[:, half], pap, AF.Copy, [pt], [pfvt[cg % 2]])
                    vop("pool", "tensor_copy", [pfvt[cg % 2]], [t_Vp[slot]], out=Vpv[:, slot, h0:h0 + 8, 0:64],
                        in_=pfv[:, half].rearrange("p (h d) -> p h d", d=64))
                else:
                    vop("dve", "tensor_copy", [pt], [t_Vp[slot]], out=Vpv[:, slot, h0:h0 + 8, 0:64],
                        in_=pap.rearrange("p (h d) -> p h d", d=64))
                if cg == 5 and c.sample:
                    vop("pool", "memset", [], [t_Vp[slot]], Vpv[32:64, slot, :, :], 0.0)
                    vop("pool", "memset", [], [t_Vp[slot]], Vpv[64:128, slot, :, :], 0.0)
            elif cg == 6:
                z6 = (pap, pt)
            else:
                act(zs[:, 0:512], z6[0], AF.Silu, [z6[1]], [zst])
                act(zs[:, 512:1024], pap, AF.Silu, [pt], [zst])
            yield
            if cg == 0 and c.norm_next is not None:
                yield from norm_part(c.norm_next)
            if cg == 1:
                yield from head_rms(pfq, pfqt, gq, t_gq, qnb, t_qnb, False)
            if cg == 3:
                yield from head_rms(pfk, pfkt, gk, t_gk, knb, t_knb, True)
                if c.store_kv is not None:
                    dma_out(c.store_kv[0], pfk, pfkt)
                yield
            if cg == 5:
                transposes(t_tb, [(tb_ap[:, j * 128:(j + 1) * 128], qnb[:, j * 128:(j + 1) * 128]) for j in range(8)],
                           list(t_qnb) + [t_identb], identb)
                qT, qTt = qTs.next()
                c.qT, c.qTt = qT.rearrange("p (j n) -> p j n", n=128), qTt
                evac_copy(qT, tb_ap, [t_tb], [qTt])
                if pfv is not None:
                    dma_out(c.store_kv[1], pfv, pfvt)
                yield
            if cg == 7:
                transposes(t_tb, [(tb_ap[:, j * 128:(j + 1) * 128], knb[:, j * 128:(j + 1) * 128]) for j in range(8)],
                           list(t_knb) + [t_identb], identb)
                evac_copy(kTv[:, :, slot * 128:(slot + 1) * 128], tb_ap.rearrange("p (j n) -> p j n", n=128), [t_tb],
                          [t_kT[slot]])
                yield
        if c.norm_next is not None:
            hT_part(c.norm_next)

    def attnA(c):
        if c.kind == "cache":
            return
        ti = c.ti
        js = [j for j in range(5) if ti - 4 + j >= c.first_ti]
        og, ogt = ogs.next()
        c.og, c.ogt = og, ogt

        def norm_hook(h):
            def f():
                g0, g1, src = {6: (0, 7, psb[5][:, 0:455]), 13: (7, 14, psb[6][:, 0:455]), 15: (14, 16, psb[5][:, 0:130])}[h]
                ng = g1 - g0
                ov = src.rearrange("p (h d) -> p h d", d=65)
                tO = o_ap(g0)[1]
                vop("dve", "reciprocal", [tO], [t_rden], out=rden[:, g0:g1], in_=ov[:, :, 64])
                dv = dtmp[:, g0 * 64:g1 * 64].rearrange("p (h d) -> p h d", d=64)
                vop("dve", "tensor_tensor", [tO, t_rden], list(t_dtmp), out=dv, in0=ov[:, :, 0:64],
                    in1=bc(rden[:, g0:g1].unsqueeze(2), [128, ng, 64]), op=ALU.mult)
                vop("pool", "tensor_tensor", list(t_dtmp) + [c.zst], [ogt], out=og[:, g0 * 64:g1 * 64],
                    in0=dtmp[:, g0 * 64:g1 * 64], in1=c.zs[:, g0 * 64:g1 * 64], op=ALU.mult)
            return f

        tiles = []
        for h in range(16):
            rg, j8 = h // 8, h % 8
            ps = slice(rg * 64, rg * 64 + 64)
            oap, ot = o_ap(h)
            for j in js:
                sl = (ti - 4 + j) % KR
                emul = None
                if j >= 3:
                    emul = (Ebv[:, h, j - 3, :], t_E)
                elif j == 0 and not c.sample:
                    emul = (M0, t_M0)
                tiles.append(dict(l=kTv[ps, j8, sl * 128:(sl + 1) * 128], r=c.qT[ps, j8, :], rdq=[t_kT[sl], c.qTt],
                                  V=Vpv[:, sl, h, :], Vt=t_Vp[sl], o=oap, ot=ot, start=(j == js[0]), stop=(j == js[-1]),
                                  emul=emul, masks=[],
                                  after=(norm_hook(h) if (h in (6, 13, 15) and j == js[-1]) else None)))
                if j == 3 and 4 in js:
                    tiles[-1]["pair"] = Eb[:, h * 256:(h + 1) * 256]
                if j == 4 and 3 in js:
                    tiles[-1]["pair_of"] = tiles[-2]
        yield from attn_stream(tiles, 0.125, sringA, PTs)

    def backA(c, wout_v, t_wout):
        if c.kind == "cache":
            return
        transposes(t_tb, [(tb_ap[:, k * 128:(k + 1) * 128], c.og[:, k * 128:(k + 1) * 128]) for k in range(8)],
                   [c.ogt, t_identb], identb)
        evac_copy(ogT, tb_ap, [t_tb], [t_ogT])
        ogTv = ogT.rearrange("p (k n) -> p k n", n=128)
        yield
        for cg in range(2):
            pap, pt = mmb.next()
            mm(pt, pap, [(ogTv[:, k, :], wout_v[:, k, cg * 512:(cg + 1) * 512]) for k in range(8)], [t_ogT, t_wout[cg]])
            cs = slice(cg * 512, cg * 512 + 512)
            vop("dve", "tensor_tensor", [pt, c.gate[1]], [t_dtmp[cg]], out=dtmp[:, cs], in0=pap, in1=c.gate[0][:, cs], op=ALU.mult)
            vop("dve", "tensor_tensor", [t_dtmp[cg], c.xt], [c.xt], out=c.x[:, cs], in0=dtmp[:, cs], in1=c.x[:, cs], op=ALU.add)
            yield
        for dst in c.ydst:
            dma_out(dst, c.x, c.xt)

    def attn_stream(tiles, scale, sring, ptring, lookahead=2):
        items = [tiles[i:i + 4] for i in range(0, len(tiles), 4)]

        def issue_qk(item):
            sap, st_ = sring.next()
            mats = [(sap[:, i * 128:(i + 1) * 128], tl["l"], tl["r"]) for i, tl in enumerate(item)]

            def fn(e, mats=mats):
                for o, l, r in mats:
                    ins = e.matmul(o, lhsT=l, rhs=r, start=True, stop=True)
                return ins
            rd = []
            for tl in item:
                for t_ in tl["rdq"]:
                    if t_ not in rd:
                        rd.append(t_)
            sch.op("pe", fn, rd=rd, wr=[st_])
            return sap, st_

        def softmax_part(item, sap, st_):
            n = len(item)
            PT, subs = ptring.next()
            act(PT[:, 0:n * 128], sap[:, 0:n * 128], AF.Exp, [st_], list(subs), scale=scale)
            idx = 0
            while idx < n:
                tl = item[idx]
                if tl["emul"] is None:
                    idx += 1
                    continue
                eap, et = tl["emul"]
                if idx + 1 < n and tl.get("pair") is not None and item[idx + 1].get("pair_of") is tl:
                    vop("dve", "tensor_tensor", [subs[idx], subs[idx + 1], et], [subs[idx], subs[idx + 1]],
                        out=PT[:, idx * 128:(idx + 2) * 128], in0=PT[:, idx * 128:(idx + 2) * 128], in1=tl["pair"], op=ALU.mult)
                    idx += 2
                    continue
                vop("dve", "tensor_tensor", [subs[idx], et], [subs[idx]], out=PT[:, idx * 128:(idx + 1) * 128],
                    in0=PT[:, idx * 128:(idx + 1) * 128], in1=eap, op=ALU.mult)
                idx += 1
            for idx, tl in enumerate(item):
                for (ps_, c0, c1) in tl["masks"]:
                    vop("pool", "memset", [], [subs[idx]], PT[ps_, idx * 128 + c0:idx * 128 + c1], 0.0)
            return PT, subs

        pend = [issue_qk(it) for it in items[:lookahead]]
        soft = [softmax_part(items[0], *pend.pop(0))]
        yield
        for i, item in enumerate(items):
            if i + lookahead < len(items):
                pend.append(issue_qk(items[i + lookahead]))
            if i + 1 < len(items):
                soft.append(softmax_part(items[i + 1], *pend.pop(0)))
            PT, subs = soft.pop(0)
            mats = [(tl["o"], PT[:, idx * 128:(idx + 1) * 128], tl["V"], tl["start"], tl["stop"]) for idx, tl in enumerate(item)]

            def fnpv(e, mats=mats):
                for o, l, r, s0, s1 in mats:
                    ins = e.matmul(o, lhsT=l, rhs=r, start=s0, stop=s1)
                return ins
            rd = list(subs[:len(item)])
            wr = []
            for tl in item:
                if tl["Vt"] not in rd:
                    rd.append(tl["Vt"])
                if tl["ot"] not in wr:
                    wr.append(tl["ot"])
            sch.op("pe", fnpv, rd=rd, wr=wr)
            for tl in item:
                if tl["after"] is not None:
                    tl["after"]()
            yield

    def run_pipeline(tiles, stages, late_first=False, order=None):
        ns = len(stages)
        for step in range(len(tiles) + ns - 1):
            gens = []
            for k in (order if order is not None else (reversed(range(ns)) if late_first else range(ns))):
                n = step - k
                if 0 <= n < len(tiles):
                    if k == 0 and tiles[n].pre is not None:
                        tiles[n].pre()
                    gens.append(stages[k](tiles[n]))
            while gens:
                for g in list(gens):
                    try:
                        next(g)
                    except StopIteration:
                        gens.remove(g)

    tilesA = []
    gti = 0
    for s in range(nseq):
        for t in range(NT):
            c = Ctx()
            c.kind = "prompt"
            c.sample = False
            c.ti = gti + t
            c.first_ti = gti
            c.xsrc = I["xp"][s * S + t * 128:s * S + (t + 1) * 128, :]
            c.ydst = [y0p[s * S + t * 128:s * S + (t + 1) * 128, :]]
            if debug_layers == 1:
                c.ydst = [O["yp"][s * S + t * 128:s * S + (t + 1) * 128, :]]
            c.store_kv = None
            import os
            if t >= NT - NAO and not os.environ.get("DBG_NOSTORE"):
                r0 = s * AR + (t - (NT - NAO)) * 128
                c.store_kv = (O["akp"][r0:r0 + 128, :], O["avp"][r0:r0 + 128, :])
            c.pre = None
            c.premod = (lambda s=s: load_mod(0, s)) if t == 0 else None
            tilesA.append(c)
        gti += NT
    for j in range(4):
        c = Ctx()
        c.kind = "cache"
        c.ti = gti + j
        c.j = j
        c.pre = None
        tilesA.append(c)
    c = Ctx()
    c.kind = "sample"
    c.sample = True
    c.first_ti = gti
    c.ti = gti + 4
    c.xsrc = I["xs"]
    c.ydst = [y0s] if debug_layers != 1 else [O["ys"]]
    c.store_kv = (O["aks"], O["avs"])
    c.pre = None
    c.premod = lambda: load_mod(0, nseq)
    tilesA.append(c)
    import os
    if os.environ.get("DBG_MEM"):
        print("layer A arena used", A.off, "of", A.cap)
    if debug_layers == 0:
        tilesA = []
    if debug_layers < 0:
        tilesA = tilesA[:-debug_layers]
    for c in tilesA:
        c.load_list = []
        c.norm_next = None
    normal = [i for i, c in enumerate(tilesA) if c.kind != "cache"]
    prologue_load, prologue_norm = [], []
    for i in normal:
        (tilesA[i - 2].load_list if i >= 2 else prologue_load).append(tilesA[i])
        if i >= 1:
            tilesA[i - 1].norm_next = tilesA[i]
        else:
            prologue_norm.append(tilesA[i])
    for c in prologue_load:
        load_x(c)
    for c in prologue_norm:
        for _ in norm_part(c):
            pass
        hT_part(c)
    run_pipeline(tilesA, [frontA, attnA, lambda c: backA(c, wA_out_v, t_wAout)], late_first=True)
    do_barrier()

    if debug_layers >= 2:
        A.reset(persist_mark)
        wBin = A.bf16(8 * 1696)
        wBin_v = wBin.rearrange("p (k n) -> p k n", n=1696)
        wuq = A.bf16(3 * 1536)
        wuq_v = wuq.rearrange("p (k n) -> p k n", n=1536)
        wukv = A.bf16(2 * 2048)
        wukv_v = wukv.rearrange("p (k n) -> p k n", n=2048)
        wBout = A.bf16(8 * 1024)
        wBout_v = wBout.rearrange("p (k n) -> p k n", n=1024)
        t_wBin = [T("wBin%d" % i) for i in range(4)]
        t_wuq = [T("wuq%d" % i) for i in range(4)]
        t_wukv = [T("wukv%d" % i) for i in range(4)]
        t_wBout = [T("wBout%d" % i) for i in range(2)]
        wB_mark = A.mark()
        stg = Ring([(A.f32(2048), T("stgB%d" % i)) for i in range(4)])
        load_weight(I["b_w_in"], 8, 1696, wBin_v, t_wBin, 512)
        load_weight(I["b_w_uq"], 3, 1536, wuq_v, t_wuq, 384)
        load_weight(I["b_w_ukv"], 2, 2048, wukv_v, t_wukv, 512)
        load_weight(I["b_w_out"], 8, 1024, wBout_v, t_wBout, 512)
        do_barrier()
        A.reset(wB_mark)
        ZT = max(NT, 1)
        zsB = A.bf16(ZT * 1024)
        zsB_v = zsB.rearrange("p (t n) -> p t n", n=1024)
        t_zsB = [T("zsB%d" % i) for i in range(ZT)]
        cqnT = A.bf16(3 * ZT * 128)
        cqnT_v = cqnT.rearrange("p (k n) -> p k n", n=ZT * 128)
        t_cq = [T("cq%d" % i) for i in range(ZT)]
        ckvnT = A.bf16(2 * NTB * 128)
        ckvnT_v = ckvnT.rearrange("p (k n) -> p k n", n=NTB * 128)
        t_ckv = [T("ckv%d" % i) for i in range(NTB)]
        krb = A.bf16(NTB * 32)
        krb_v = krb.rearrange("p (t n) -> p t n", n=32)
        t_kr = [T("kr%d" % i) for i in range(NTB)]
        kTg = A.bf16(4 * NTB * 128)
        kTg_v = kTg.rearrange("p (h n) -> p h n", n=NTB * 128)
        t_kTg = [T("kTg%d" % i) for i in range(NTB)]
        vop("pool", "memset", [], t_kTg, kTg, 0.0)
        Vg = A.bf16(NTB * 4 * 65)
        Vg_v = Vg.rearrange("p (t h d) -> p t h d", h=4, d=65)
        t_Vg = [T("Vg%d" % i) for i in range(NTB)]
        vop("pool", "memset", [], t_Vg, Vg_v[:, :, :, 64:65], 1.0)
        cosb = A.f32((NT + 1) * 16)
        sinb = A.f32((NT + 1) * 16)
        cos_v = cosb.rearrange("p (t f) -> p t f", f=16)
        sin_v = sinb.rearrange("p (t f) -> p t f", f=16)
        t_cs = T("cossin")
        dma_in(cosb[:, 0:NT * 16], I["cosp"], t_cs, key="l_cos")
        dma_in(sinb[:, 0:NT * 16], I["sinp"], t_cs, key="l_cos")
        dma_in(cos_v[:, NT, :], I["coss"], t_cs, key="l_cos")
        dma_in(sin_v[:, NT, :], I["sins"], t_cs, key="l_cos")
        gateB = A.f32(1024)
        t_gateB = T("gateB")
        work_mark = A.mark()
        SCB = 96.0 ** -0.5
        sbanks = Ring([(psb[3][:, :], T("sB0")), (psb[4][:, :], T("sB1")), (psb[7][:, :], T("sB2"))])
        obanks = Ring([(psb[5][:, :], T("oB0")), (psb[6][:, :], T("oB1"))])

        def ring_f32(name, n, k, parts=128):
            return Ring([(A.f32(n, parts=parts), T("%s%d" % (name, i))) for i in range(k)])

        def ring_bf16(name, n, k, parts=128):
            return Ring([(A.bf16(n, parts=parts), T("%s%d" % (name, i))) for i in range(k)])

        def rope(src3, nh, csi, out1, out2, rd, wr, rts):
            x1, x2 = src3[:, :, 0:16], src3[:, :, 16:32]
            cosx = bc(cos_v[:, csi, :].unsqueeze(1), [128, nh, 16])
            sinx = bc(sin_v[:, csi, :].unsqueeze(1), [128, nh, 16])
            rt_, t_rt = rts.next()
            a1 = rt_[:, 0:nh * 16].rearrange("p (h f) -> p h f", f=16)
            a2 = rt_[:, 64:64 + nh * 16].rearrange("p (h f) -> p h f", f=16)
            b1 = rt_[:, 128:128 + nh * 16].rearrange("p (h f) -> p h f", f=16)
            b2 = rt_[:, 192:192 + nh * 16].rearrange("p (h f) -> p h f", f=16)
            vop("pool", "tensor_tensor", rd + [t_cs], [t_rt], out=a1, in0=x1, in1=cosx, op=ALU.mult)
            vop("pool", "tensor_tensor", rd + [t_cs], [t_rt], out=a2, in0=x2, in1=sinx, op=ALU.mult)
            vop("pool", "tensor_tensor", rd + [t_cs], [t_rt], out=b1, in0=x2, in1=cosx, op=ALU.mult)
            vop("pool", "tensor_tensor", rd + [t_cs], [t_rt], out=b2, in0=x1, in1=sinx, op=ALU.mult)
            vop("pool", "tensor_tensor", [t_rt], wr, out=out1, in0=a1, in1=a2, op=ALU.subtract)
            vop("pool", "tensor_tensor", [t_rt], wr, out=out2, in0=b1, in1=b2, op=ALU.add)

        W = Ctx()

        def sweep1_bufs(s):
            do_barrier()
            A.reset(work_mark)
            W.xs = ring_f32("x1_", 1024, 3)
            W.dt = ring_f32("dt1_", 1024, 2)
            W.hb = ring_bf16("hb1_", 1024, 2)
            W.hT = ring_bf16("hT1_", 1024, 2)
            W.pj = ring_f32("pj", 672, 3)
            W.cqb = ring_bf16("cqb", 384, 2)
            W.ckvf = ring_f32("ckvf", 256, 3)
            W.ckvb = ring_bf16("ckvb", 256, 2)
            W.krn = ring_f32("krn", 32, 2)
            W.krf = ring_f32("krf", 32, 3)
            W.rt = ring_f32("rt", 256, 2)
            W.ss = ring_f32("ss1_", 4, 3)
            W.st3 = ring_f32("st3_", 8, 3)
            W.bc3 = A.f32(2048)
            W.t_bc3 = T("bc3B")
            W.G1 = A.f32(1024)
            W.t_G1 = T("G1B")
            dma_in(W.bc3, bc(modd[NS + s:NS + s + 1, 0:2048], [128, 2048]), W.t_bc3)
            dma_in(gateB, bc(modd[NS + s:NS + s + 1, 2048:3072], [128, 1024]), t_gateB)
            dma_in(W.G1, bc(I["norm_g"][1:2, :], [128, 1024]), W.t_G1)
            vop("dve", "scalar_tensor_tensor", [W.t_bc3, W.t_G1], [W.t_G1], out=W.G1, in0=W.bc3[:, 1024:2048], scalar=1.0, in1=W.G1,
                op0=ALU.add, op1=ALU.mult)

        def s1_load(c):
            if c.kind == "cache":
                return
            c.x, c.xt = W.xs.next()
            dma_in(c.x, c.ysrc, c.xt)
            return
            yield

        def s1_norm(c):
            if c.kind == "cache":
                return
            xap, xt = c.x, c.xt
            ss, sst = W.ss.next()
            hb_, hbt = W.hb.next()
            dt_, dtt = W.dt.next()
            act(hb_, xap, AF.Square, [xt], [hbt, sst], accum_out=ss[:, 0:1])
            rstd_from(ss[:, 0:1], 1, sst, 1.0 / D, ss[:, 1:2])
            vop("dve", "scalar_tensor_tensor", [xt, sst, W.t_G1], [dtt], out=dt_, in0=xap, scalar=ss[:, 0:1], in1=W.G1,
                op0=ALU.mult, op1=ALU.mult)
            vop("dve", "tensor_tensor", [dtt, W.t_bc3], [hbt], out=hb_, in0=dt_, in1=W.bc3[:, 0:1024], op=ALU.add)
            c.hb_, c.hbt = hb_, hbt
            yield

        def s1_tr(c):
            if c.kind == "cache":
                return
            hb_, hbt = c.hb_, c.hbt
            transposes(t_tb, [(tb_ap[:, k * 128:(k + 1) * 128], hb_[:, k * 128:(k + 1) * 128]) for k in range(8)],
                       [hbt, t_identb], identb)
            hT, hTt = W.hT.next()
            evac_copy(hT, tb_ap, [t_tb], [hTt])
            c.hT, c.hTt = hT.rearrange("p (k n) -> p k n", n=128), hTt
            yield

        def s1_mm(c):
            if c.kind == "cache":
                return
            c.pj, c.pjt = W.pj.next()
            for cg in range(4):
                w_ = 512 if cg < 3 else 160
                pap, pt = mmb.next()
                mm(pt, pap[:, 0:w_], [(c.hT[:, k, :], wBin_v[:, k, cg * 512:cg * 512 + w_]) for k in range(8)], [c.hTt, t_wBin[cg]])
                if cg == 0:
                    z0 = (pap, pt)
                elif cg == 1:
                    act(zsB_v[:, c.zt, 0:512], z0[0], AF.Silu, [z0[1]], [t_zsB[c.zt]])
                    act(zsB_v[:, c.zt, 512:1024], pap, AF.Silu, [pt], [t_zsB[c.zt]])
                else:
                    act(c.pj[:, (cg - 2) * 512:(cg - 2) * 512 + w_], pap[:, 0:w_], AF.Copy, [pt], [c.pjt])
                yield

        def s1_e1(c):
            if c.kind == "cache":
                kt = c.kt
                c.cf, c.cft = W.ckvf.next()
                dma_in(c.cf, I["cckv"][kt * 128:(kt + 1) * 128, :], c.cft)
                c.kr_, c.krt = W.krf.next()
                dma_in(c.kr_, I["ckr"][kt * 128:(kt + 1) * 128, :], c.krt)
                c.ckvb, c.ckvbt = W.ckvb.next()
                vop("pool", "tensor_copy", [c.cft], [c.ckvbt], out=c.ckvb, in_=c.cf)
                vop("pool", "tensor_copy", [c.krt], [t_kr[c.kt]], out=krb_v[:, c.kt, :], in_=c.kr_)
                return
            pj, pjt = c.pj, c.pjt
            dt_, dtt = W.dt.next()
            st3, t_st3 = W.st3.next()
            vop("dve", "tensor_tensor", [pjt], [dtt], out=dt_[:, 0:672], in0=pj, in1=pj, op=ALU.mult)
            vop("dve", "tensor_reduce", [dtt], [t_st3], out=st3[:, 0:1], in_=dt_[:, 0:384], axis=AX.X, op=ALU.add)
            vop("dve", "tensor_reduce", [dtt], [t_st3], out=st3[:, 1:2], in_=dt_[:, 384:640], axis=AX.X, op=ALU.add)
            vop("dve", "tensor_reduce", [dtt], [t_st3], out=st3[:, 2:3], in_=dt_[:, 640:672], axis=AX.X, op=ALU.add)
            vop("dve", "tensor_tensor", [t_st3, t_invn], [t_st3], out=st3[:, 0:3], in0=st3[:, 0:3], in1=invn3, op=ALU.mult)
            rstd_from(st3[:, 0:3], 3, t_st3, 1.0, st3[:, 4:7])
            yield
            c.cqb, c.cqbt = W.cqb.next()
            vop("dve", "scalar_tensor_tensor", [pjt, t_st3, t_gcq], [c.cqbt], out=c.cqb, in0=pj[:, 0:384], scalar=st3[:, 0:1], in1=gcq,
                op0=ALU.mult, op1=ALU.mult)
            c.cf, c.cft = W.ckvf.next()
            vop("dve", "scalar_tensor_tensor", [pjt, t_st3, t_gckv], [c.cft], out=c.cf, in0=pj[:, 384:640], scalar=st3[:, 1:2], in1=gckv,
                op0=ALU.mult, op1=ALU.mult)
            dma_out(c.ckv_dst, c.cf, c.cft)
            c.ckvb, c.ckvbt = W.ckvb.next()
            vop("pool", "tensor_copy", [c.cft], [c.ckvbt], out=c.ckvb, in_=c.cf)
            c.krn, c.krnt = W.krn.next()
            vop("dve", "scalar_tensor_tensor", [pjt, t_st3, t_gkr], [c.krnt], out=c.krn, in0=pj[:, 640:672], scalar=st3[:, 2:3], in1=gkr,
                op0=ALU.mult, op1=ALU.mult)
            yield

        def s1_e2(c):
            kt, zt = c.kt, c.zt
            if c.kind != "cache":
                transposes(t_tb, [(tb_ap[:, k * 128:(k + 1) * 128], c.cqb[:, k * 128:(k + 1) * 128]) for k in range(3)],
                           [c.cqbt, t_identb], identb)
                evac_copy(cqnT_v[:, :, zt * 128:(zt + 1) * 128], tb_ap[:, 0:384].rearrange("p (k n) -> p k n", n=128), [t_tb], [t_cq[zt]])
                yield
            transposes(t_tb, [(tb_ap[:, k * 128:(k + 1) * 128], c.ckvb[:, k * 128:(k + 1) * 128]) for k in range(2)],
                       [c.ckvbt, t_identb], identb)
            evac_copy(ckvnT_v[:, :, kt * 128:(kt + 1) * 128], tb_ap[:, 0:256].rearrange("p (k n) -> p k n", n=128), [t_tb], [t_ckv[kt]])
            yield
            if c.kind != "cache":
                kr_, krt = W.krf.next()
                k3 = c.krn.rearrange("p (h f) -> p h f", h=1)
                o3 = kr_.rearrange("p (h f) -> p h f", h=1)
                rope(k3, 1, c.cs, o3[:, :, 0:16], o3[:, :, 16:32], [c.krnt], [krt], W.rt)
                dma_out(c.kr_dst, kr_, krt)
                vop("pool", "tensor_copy", [krt], [t_kr[kt]], out=krb_v[:, kt, :], in_=kr_)
                yield

        def sweep2_bufs():
            do_barrier()
            A.reset(work_mark)
            W.qg = ring_f32("qg", 384, 3)
            W.kf = ring_f32("kf", 256, 3)
            W.dsq = ring_f32("dsq", 384, 2)
            W.qnb = ring_bf16("qnb", 384, 3)
            W.knb = ring_bf16("knb", 384, 3)
            W.qT = ring_bf16("qTB", 512, 4)
            for qap_, qt_ in W.qT.items:
                vop("pool", "memset", [], [qt_], qap_, 0.0)
            W.PT = Ring([(A.bf16(512), [T("PTB%d_%d" % (i, k)) for k in range(4)]) for i in range(4)])
            W.don = ring_f32("don", 256, 2)
            W.rt = ring_f32("rt", 256, 2)
            W.st8 = ring_f32("st8_", 16, 3)
            W.st4 = ring_f32("st4_", 8, 3)
            W.rden = ring_f32("rden", 4, 2)

        def p2_mm(c):
            g, kt, zt = c.g, c.kt, c.zt
            if c.kind != "cache":
                pap, pt = mmb.next()
                mm(pt, pap[:, 0:384], [(cqnT_v[:, k, zt * 128:(zt + 1) * 128], wuq_v[:, k, g * 384:(g + 1) * 384]) for k in range(3)],
                   [t_cq[zt], t_wuq[g]])
                c.qg, c.qgt = W.qg.next()
                act(c.qg, pap[:, 0:384], AF.Copy, [pt], [c.qgt])
                yield
            pap, pt = mmb.next()
            mm(pt, pap, [(ckvnT_v[:, k, kt * 128:(kt + 1) * 128], wukv_v[:, k, g * 512:(g + 1) * 512]) for k in range(2)],
               [t_ckv[kt], t_wukv[g]])
            p3 = pap.rearrange("p (h d) -> p h d", d=128)
            c.kf, c.kft = W.kf.next()
            act(c.kf.rearrange("p (h d) -> p h d", d=64), p3[:, :, 0:64], AF.Copy, [pt], [c.kft])
            act(Vg_v[:, kt, :, 0:64], p3[:, :, 64:128], AF.Copy, [pt], [t_Vg[kt]])
            yield

        def p2_norm(c):
            kt = c.kt
            if c.kind != "cache":
                qg, qgt = c.qg, c.qgt
                q3 = qg.rearrange("p (h d) -> p h d", d=96)
                ds, dst_ = W.dsq.next()
                d3 = ds.rearrange("p (h d) -> p h d", d=96)
                st8, t_st8 = W.st8.next()
                vop("dve", "tensor_tensor", [qgt], [dst_], out=ds, in0=qg, in1=qg, op=ALU.mult)
                vop("dve", "tensor_reduce", [dst_], [t_st8], out=st8[:, 0:4], in_=d3[:, :, 0:64], axis=AX.X, op=ALU.add)
                vop("dve", "tensor_reduce", [dst_], [t_st8], out=st8[:, 4:8], in_=d3[:, :, 64:96], axis=AX.X, op=ALU.add)
                vop("dve", "tensor_tensor", [t_st8, t_invn], [t_st8], out=st8[:, 0:8], in0=st8[:, 0:8], in1=invn8, op=ALU.mult)
                rstd_from(st8[:, 0:8], 8, t_st8, 1.0, st8[:, 8:16])
                c.st8, c.t_st8 = st8, t_st8
            kf, kft = c.kf, c.kft
            ds, dst_ = W.dsq.next()
            st4, t_st4 = W.st4.next()
            vop("dve", "tensor_tensor", [kft], [dst_], out=ds[:, 0:256], in0=kf, in1=kf, op=ALU.mult)
            vop("dve", "tensor_reduce", [dst_], [t_st4], out=st4[:, 0:4], in_=ds[:, 0:256].rearrange("p (h d) -> p h d", d=64),
                axis=AX.X, op=ALU.add)
            rstd_from(st4[:, 0:4], 4, t_st4, 1.0 / 64, st4[:, 4:8])
            c.st4, c.t_st4 = st4, t_st4
            yield
            if c.kind != "cache":
                st8, t_st8 = c.st8, c.t_st8
                c.qnb, c.qnbt = W.qnb.next()
                qn3 = c.qnb.rearrange("p (h d) -> p h d", d=96)
                vop("dve", "tensor_tensor", [qgt, t_st8], [qgt], out=q3[:, :, 0:64], in0=q3[:, :, 0:64],
                    in1=bc(st8[:, 0:4].unsqueeze(2), [128, 4, 64]), op=ALU.mult)
                vop("dve", "tensor_tensor", [qgt, t_gqn], [c.qnbt], out=qn3[:, :, 0:64], in0=q3[:, :, 0:64],
                    in1=bc(gqn.unsqueeze(1), [128, 4, 64]), op=ALU.mult)
                vop("dve", "tensor_tensor", [qgt, t_st8], [qgt], out=q3[:, :, 64:96], in0=q3[:, :, 64:96],
                    in1=bc(st8[:, 4:8].unsqueeze(2), [128, 4, 32]), op=ALU.mult)
                vop("dve", "tensor_tensor", [qgt, t_gqr], [qgt], out=q3[:, :, 64:96], in0=q3[:, :, 64:96],
                    in1=bc(gqr.unsqueeze(1), [128, 4, 32]), op=ALU.mult)
                yield
                rope(q3[:, :, 64:96], 4, c.cs, qn3[:, :, 64:80], qn3[:, :, 80:96], [qgt], [c.qnbt], W.rt)
                yield
            k3 = kf.rearrange("p (h d) -> p h d", d=64)
            c.knb, c.knbt = W.knb.next()
            kn3 = c.knb.rearrange("p (h d) -> p h d", d=96)
            vop("dve", "tensor_tensor", [kft, c.t_st4], [kft], out=k3, in0=k3, in1=bc(c.st4[:, 0:4].unsqueeze(2), [128, 4, 64]), op=ALU.mult)
            vop("dve", "tensor_tensor", [kft, t_gkn], [c.knbt], out=kn3[:, :, 0:64], in0=k3, in1=bc(gkn.unsqueeze(1), [128, 4, 64]),
                op=ALU.mult)
            vop("pool", "tensor_copy", [t_kr[kt]], [c.knbt], out=kn3[:, :, 64:96], in_=bc(krb_v[:, kt, :].unsqueeze(1), [128, 4, 32]))
            yield

        def p2_tr(c):
            kt = c.kt
            if c.kind != "cache":
                qn3 = c.qnb.rearrange("p (h d) -> p h d", d=96)
                transposes(t_tb, [(tb_ap[0:96, h * 128:(h + 1) * 128], qn3[:, h, :]) for h in range(4)], [c.qnbt, t_identb], identb)
                qT, qTt = W.qT.next()
                c.qT, c.qTt = qT.rearrange("p (h n) -> p h n", n=128), qTt
                evac_copy(qT[0:96, :], tb_ap[0:96, 0:512], [t_tb], [qTt])
                yield
            kn3 = c.knb.rearrange("p (h d) -> p h d", d=96)
            transposes(t_tb, [(tb_ap[0:96, h * 128:(h + 1) * 128], kn3[:, h, :]) for h in range(4)], [c.knbt, t_identb], identb)
            evac_copy(kTg_v[0:96, :, kt * 128:(kt + 1) * 128], tb_ap[0:96, 0:512].rearrange("p (h n) -> p h n", n=128), [t_tb], [t_kTg[kt]])
            yield

        def a2(c):
            if c.kind == "cache":
                return
            g, kt, zt = c.g, c.kt, c.zt
            kts = list(range(c.kt0, kt + 1))
            oap, ot = obanks.next()

            def norm_hook():
                ov = oap[:, 0:260].rearrange("p (h d) -> p h d", d=65)
                rden, t_rden = W.rden.next()
                don, t_don = W.don.next()
                vop("dve", "reciprocal", [ot], [t_rden], out=rden[:, 0:4], in_=ov[:, :, 64])
                vop("dve", "tensor_tensor", [ot, t_rden], [t_don], out=don.rearrange("p (h d) -> p h d", d=64),
                    in0=ov[:, :, 0:64], in1=bc(rden[:, 0:4].unsqueeze(2), [128, 4, 64]), op=ALU.mult)
                vop("pool", "tensor_tensor", [t_don, t_zsB[zt]], [t_zsB[zt]], out=zsB_v[:, zt, g * 256:(g + 1) * 256],
                    in0=don, in1=zsB_v[:, zt, g * 256:(g + 1) * 256], op=ALU.mult)

            tiles = []
            for hh in range(4):
                for k in kts:
                    masks = []
                    if k == kt:
                        masks = ([(slice(32, 64), 0, 128), (slice(64, 128), 0, 128)] if c.kind == "sample"
                                 else [(slice(64, 128), 0, 64)])
                    tiles.append(dict(l=kTg_v[:, hh, k * 128:(k + 1) * 128], r=c.qT[:, hh, :], rdq=[t_kTg[k], c.qTt],
                                      V=Vg_v[:, k, hh, :], Vt=t_Vg[k], o=oap[:, hh * 65:(hh + 1) * 65], ot=ot,
                                      start=(k == kts[0]), stop=(k == kts[-1]), emul=None, masks=masks,
                                      after=(norm_hook if (hh == 3 and k == kts[-1]) else None)))
            yield from attn_stream(tiles, SCB, sbanks, W.PT)

        def sweep3_bufs():
            do_barrier()
            A.reset(work_mark)
            W.xs = ring_f32("x3_", 1024, 4)
            W.ogT = ring_bf16("ogT", 1024, 3)
            W.dt = ring_f32("dt3_", 1024, 2)

        def b3_load(c):
            if c.kind == "cache":
                return
            c.x, c.xt = W.xs.next()
            dma_in(c.x, c.ysrc, c.xt)
            return
            yield

        def b3a(c):
            if c.kind == "cache":
                return
            zt = c.zt
            transposes(t_tb, [(tb_ap[:, k * 128:(k + 1) * 128], zsB_v[:, zt, k * 128:(k + 1) * 128]) for k in range(8)],
                       [t_zsB[zt], t_identb], identb)
            ogT, ogTt = W.ogT.next()
            c.ogT, c.ogTt = ogT.rearrange("p (k n) -> p k n", n=128), ogTt
            evac_copy(ogT, tb_ap, [t_tb], [ogTt])
            yield

        def b3b(c):
            if c.kind == "cache":
                return
            dt_, dtt = W.dt.next()
            for cg in range(2):
                pap, pt = mmb.next()
                mm(pt, pap, [(c.ogT[:, k, :], wBout_v[:, k, cg * 512:(cg + 1) * 512]) for k in range(8)], [c.ogTt, t_wBout[cg]])
                cs_ = slice(cg * 512, cg * 512 + 512)
                vop("dve", "tensor_tensor", [pt, t_gateB], [dtt], out=dt_[:, cs_], in0=pap, in1=gateB[:, cs_], op=ALU.mult)
                vop("dve", "tensor_tensor", [dtt, c.xt], [c.xt], out=c.x[:, cs_], in0=dt_[:, cs_], in1=c.x[:, cs_], op=ALU.add)
                yield
            dma_out(c.y_dst, c.x, c.xt)

        import os
        DBG_B = os.environ.get("DBG_B", "")

        def run_seqB(s, tiles):
            if DBG_B == "w" or (DBG_B and s > 0):
                return
            sweep1_bufs(s)
            run_pipeline(tiles, [s1_load, s1_norm, s1_tr, s1_mm, s1_e1, s1_e2])
            if DBG_B == "s1":
                return
            sweep2_bufs()
            for g in range(4):
                for c in tiles:
                    c.g = g
                run_pipeline(tiles, [p2_mm, p2_norm, p2_tr, a2], order=[3, 0, 1, 2])
                if DBG_B == "s2":
                    return
            sweep3_bufs()
            run_pipeline(tiles, [b3_load, b3a, b3b])

        for s in range(nseq):
            tiles = []
            for t in range(NT):
                c = Ctx()
                c.kind = "prompt"
                c.pre = None
                c.kt, c.zt, c.cs, c.kt0 = t, t, t, 0
                r = slice(s * S + t * 128, s * S + (t + 1) * 128)
                c.ysrc = y0p[r, :]
                c.ckv_dst, c.kr_dst, c.y_dst = O["bcp"][r, :], O["brp"][r, :], O["yp"][r, :]
                tiles.append(c)
            run_seqB(s, tiles)
        tiles = []
        for t in range(NTC):
            c = Ctx()
            c.kind = "cache"
            c.pre = None
            c.kt = t
            c.zt = 0
            tiles.append(c)
        c = Ctx()
        c.kind = "sample"
        c.pre = None
        c.kt, c.zt, c.cs, c.kt0 = NTC, 0, NT, 0
        c.ysrc = y0s
        c.ckv_dst, c.kr_dst, c.y_dst = O["bcs"], O["brs"], O["ys"]
        tiles.append(c)
        run_seqB(nseq, tiles)

    sch.emit(nc, stack)
    stack.close()
    return nc


def build_layer_B(env):
    raise NotImplementedError


ROPE_THETA = 10000.0
B_COLS = np.concatenate([np.arange(672, 1696), np.arange(0, 672)])


def rope_tables(pos):
    inv = (np.float32(ROPE_THETA) ** (-np.arange(16, dtype=np.float32) / np.float32(16))).astype(np.float32)
    ang = pos.astype(np.float32)[:, None] * inv[None, :]
    return np.cos(ang).astype(np.float32), np.sin(ang).astype(np.float32)


def shared_inputs(inp, S, PAST):
    f = lambda a: np.ascontiguousarray(np.asarray(a, dtype=np.float32))
    sh = {}
    sh["norm_g"] = f(inp["norm_g"])
    sh["ada_w"] = f(inp["ada_w"]).reshape(2 * D, 3 * D)
    sh["ada_b"] = f(inp["ada_b"])
    w = f(inp["a_w_in"])[0]
    w = np.concatenate([w[:, 0:1024][:, PERM_COLS], w[:, 1024:2048][:, PERM_COLS], w[:, 2048:]], axis=1)
    sh["a_w_in"] = f(w)
    sh["a_g_q"] = f(inp["a_g_q"])
    sh["a_g_k"] = f(inp["a_g_k"])
    tab = f(inp["a_rel_bias"])[0]
    ki = np.arange(128)[:, None, None]
    jj = np.arange(2)[None, :, None]
    qi = np.arange(128)[None, None, :]
    rel = np.clip((4 - (3 + jj)) * 128 + qi - ki, -128, 128) + 128
    sh["erel"] = f(np.transpose(tab[:, rel], (1, 0, 2, 3)).reshape(128, 16 * 2 * 128))
    sh["cbias"] = f(tab[:, 256][None, :])
    sh["a_w_out"] = f(inp["a_w_out"])[0]
    sh["b_w_in"] = f(f(inp["b_w_in"])[0][:, B_COLS])
    for k in ("b_g_cq", "b_g_ckv", "b_g_qn", "b_g_qr", "b_g_kn", "b_g_kr"):
        sh[k] = f(inp[k])
    sh["b_w_uq"] = f(inp["b_w_uq"])[0]
    sh["b_w_ukv"] = f(inp["b_w_ukv"])[0]
    sh["b_w_out"] = f(inp["b_w_out"])[0]
    cp, sp_ = rope_tables(np.arange(S))
    NT = S // 128
    sh["cosp"] = f(cp.reshape(NT, 128, 16).transpose(1, 0, 2).reshape(128, NT * 16))
    sh["sinp"] = f(sp_.reshape(NT, 128, 16).transpose(1, 0, 2).reshape(128, NT * 16))
    sh["coss"], sh["sins"] = rope_tables(PAST + np.arange(128))
    sh["ident"] = np.eye(128, dtype=np.float32)
    return sh


def core_inputs(inp, sh, core, nseq, S, PAST):
    f = lambda a: np.ascontiguousarray(np.asarray(a, dtype=np.float32))
    m = dict(sh)
    m["xp"] = f(inp["x_prompt"][core * nseq:(core + 1) * nseq]).reshape(nseq * S, D)
    xs = np.zeros((128, D), np.float32)
    xs[:32] = np.asarray(inp["x_sample"][core])
    m["xs"] = xs
    m["ck"] = f(np.asarray(inp["cache_a_k"])[0, core].reshape(512, D)[:, PERM_COLS])
    m["cv"] = f(np.asarray(inp["cache_a_v"])[0, core].reshape(512, D))
    m["cckv"] = f(np.asarray(inp["cache_mla_ckv"])[0, core])
    m["ckr"] = f(np.asarray(inp["cache_mla_krope"])[0, core])
    c = np.concatenate([np.asarray(inp["c_prompt"])[core * nseq:(core + 1) * nseq], np.asarray(inp["c_sample"])[core:core + 1]], 0)
    NS = nseq + 1
    m["scT"] = f(c.T.reshape(8, 128, NS).transpose(1, 0, 2).reshape(128, 8 * NS))
    return m


_NC_CACHE = {}


def run_cores(inp, ncores, nseq, S, PAST, debug_layers=2):
    key = (nseq, S, PAST, debug_layers)
    if key not in _NC_CACHE:
        _NC_CACHE[key] = build(nseq, S, PAST, debug_layers)
    nc = _NC_CACHE[key]
    sh = shared_inputs(inp, S, PAST)
    in_maps = [core_inputs(inp, sh, c, nseq, S, PAST) for c in range(ncores)]
    res = run_bass_kernel_spmd(nc, in_maps, core_ids=list(range(ncores)))
    return res.results


def assemble(results, ncores, nseq, S):
    AR = min(512, S)
    inv = np.empty(1024, np.int64)
    inv[PERM_COLS] = np.arange(1024)
    cat = lambda k: np.concatenate([r[k] for r in results], 0)
    y_p = cat("yp").reshape(ncores * nseq, S, D)
    y_s = np.stack([r["ys"][:32] for r in results], 0)
    akp = cat("akp")[:, inv].reshape(1, ncores * nseq, AR, 16, 64)
    avp = cat("avp").reshape(1, ncores * nseq, AR, 16, 64)
    aks = np.stack([r["aks"][:32][:, inv] for r in results], 0).reshape(1, ncores, 32, 16, 64)
    avs = np.stack([r["avs"][:32] for r in results], 0).reshape(1, ncores, 32, 16, 64)
    bcp = cat("bcp").reshape(1, ncores * nseq, S, 256)
    brp = cat("brp").reshape(1, ncores * nseq, S, 32)
    bcs = np.stack([r["bcs"][:32] for r in results], 0).reshape(1, ncores, 32, 256)
    brs = np.stack([r["brs"][:32] for r in results], 0).reshape(1, ncores, 32, 32)
    return tuple(np.ascontiguousarray(a, dtype=np.float32) for a in (y_p, y_s, akp, avp, aks, avs, bcp, brp, bcs, brs))


def kernel(**inputs):
    res = run_cores(inputs, NCORES, 4, 2048, 2048)
    return assemble(res, NCORES, 4, 2048)
```

```python
import numpy as np
import concourse.bass as bass
import concourse.mybir as mybir
from concourse.bass_utils import run_bass_kernel_spmd
from contextlib import ExitStack

F32 = mybir.dt.float32
BF16 = mybir.dt.bfloat16
AF = mybir.ActivationFunctionType
ALU = mybir.AluOpType
AX = mybir.AxisListType

D = 1024
EPS = 1e-6
NCORES = 8


class T:
    __slots__ = ("name", "w", "r")

    def __init__(self, name=""):
        self.name = name
        self.w = None
        self.r = {}


class Op:
    __slots__ = ("eng", "fn", "deps", "dma", "val", "needed", "kind")


class Sched:
    ENG = ("pe", "act", "dve", "pool", "sp")

    def __init__(self):
        self.q = {e: [] for e in self.ENG}
        self.dmacnt = {}
        self.lastdma = {}
        self.nbar = 0

    def op(self, eng, fn, rd=(), wr=(), dma=None, kind="c", extra_deps=()):
        o = Op()
        o.eng, o.fn, o.dma, o.kind = eng, fn, dma, kind
        o.needed = False
        o.val = None
        deps = list(extra_deps)
        for t in rd:
            if t.w is not None:
                deps.append((t.w, 0))
        for t in wr:
            if t.w is not None:
                deps.append((t.w, 1))
            for r in t.r.values():
                if r is not o:
                    deps.append((r, 2))
        o.deps = deps
        for d, k in deps:
            if d.dma is None:
                if d.eng != eng or eng != "pe":
                    d.needed = True
        wrs = set(id(t) for t in wr)
        for t in wr:
            t.w = o
            t.r = {}
        key = dma if dma is not None else eng
        for t in rd:
            if id(t) not in wrs:
                t.r[key] = o
        if dma is not None:
            c = self.dmacnt.get(dma, 0) + 16
            self.dmacnt[dma] = c
            o.val = c
            self.lastdma[dma] = o
        self.q[eng].append(o)
        return o

    def barrier(self, markers, pe_wr=()):
        bt = [T("bar%d_%s" % (self.nbar, e)) for e in self.ENG]
        self.nbar += 1
        alld = [(o, 0) for o in self.lastdma.values()]
        for e, t in zip(self.ENG, bt):
            if e == "sp":
                self.op("sp", None, wr=[t], kind="seminc", extra_deps=alld)
            elif e == "pool":
                self.op("pool", markers[e], wr=[t], extra_deps=alld)
            elif e == "pe":
                self.op(e, markers[e], wr=[t] + list(pe_wr))
            else:
                self.op(e, markers[e], wr=[t])
        for e in self.ENG:
            self.op(e, None, rd=bt, kind="wait")

    def emit(self, nc, stack):
        for e in self.ENG:
            cnt = 0
            for o in self.q[e]:
                if o.dma is None:
                    if o.needed:
                        cnt += 1
                    o.val = cnt
        sems = {}
        for e in self.ENG:
            sems[e] = stack.enter_context(nc.semaphore("s_" + e))
        for k in self.dmacnt:
            sems[k] = stack.enter_context(nc.semaphore("d_" + k))
        self.sems = sems
        block = stack.enter_context(nc.Block())
        final = [(k, v) for k, v in self.dmacnt.items()]

        def run(eng, ename):
            waited = {}
            for o in self.q[ename]:
                for d, kind in o.deps:
                    key = d.dma if d.dma is not None else d.eng
                    if d.dma is None and d.eng == ename and ename == "pe":
                        continue
                    if waited.get(key, 0) >= d.val:
                        continue
                    eng.wait_ge(sems[key], d.val)
                    waited[key] = d.val
                if o.kind == "wait":
                    continue
                if o.kind == "seminc":
                    if o.needed:
                        eng.sem_inc(sems[ename], 1)
                    continue
                ins = o.fn(eng)
                if o.dma is not None:
                    ins.then_inc(sems[o.dma], 16)
                elif o.needed:
                    ins.then_inc(sems[ename], 1)
            if ename in ("sp", "pool"):
                for k, v in final:
                    if waited.get(k, 0) < v:
                        eng.wait_ge(sems[k], v)

        @block.sync
        def _(e):
            run(e, "sp")

        @block.scalar
        def _(e):
            run(e, "act")

        @block.vector
        def _(e):
            run(e, "dve")

        @block.gpsimd
        def _(e):
            run(e, "pool")

        @block.tensor
        def _(e):
            run(e, "pe")


class Ring:
    def __init__(self, items):
        self.items = items
        self.i = 0

    def next(self):
        it = self.items[self.i % len(self.items)]
        self.i += 1
        return it


class Arena:
    def __init__(self, ar, nbytes):
        self.ar = ar
        self.cap = nbytes
        self.off = 0
        self.marks = []

    def _take(self, nbytes):
        nbytes = (nbytes + 63) // 64 * 64
        o = self.off
        self.off += nbytes
        assert self.off <= self.cap, "SBUF arena overflow %d > %d" % (self.off, self.cap)
        return o

    def f32(self, n, parts=128):
        o = self._take(n * 4)
        return self.ar[0:parts, o // 4:o // 4 + n]

    def bf16(self, n, parts=128):
        o = self._take(n * 2)
        return self.ar[0:parts, o // 4:o // 4 + (n + 1) // 2].bitcast(BF16)[:, 0:n]

    def mark(self):
        return self.off

    def reset(self, m):
        self.off = m


PERM_COLS = np.array([(j8 + 8 * rg) * 64 + d for j8 in range(8) for rg in range(2) for d in range(64)])


def build(nseq, S, PAST, debug_layers=2):
    NT = S // 128
    NS = nseq + 1
    NTC = PAST // 128
    AR = min(512, S)
    NAO = AR // 128
    NTB = max(NT, NTC + 1)
    nc = bass.Bass("TRN2", target_bir_lowering=False)

    def din(name, shape):
        return nc.dram_tensor(name, list(shape), F32, kind="ExternalInput").ap()

    def dout(name, shape):
        return nc.dram_tensor(name, list(shape), F32, kind="ExternalOutput").ap()

    def dscr(name, shape):
        return nc.dram_tensor(name, list(shape), F32).ap()

    I = {}
    for name, shape in [
        ("xp", (nseq * S, D)), ("xs", (128, D)), ("ck", (512, D)), ("cv", (512, D)),
        ("cckv", (PAST, 256)), ("ckr", (PAST, 32)), ("scT", (128, 8 * NS)),
        ("norm_g", (2, D)), ("ada_w", (2 * D, 3 * D)), ("ada_b", (2, 3 * D)),
        ("a_w_in", (D, 4 * D)), ("a_g_q", (1, 64)), ("a_g_k", (1, 64)),
        ("erel", (128, 16 * 2 * 128)), ("cbias", (1, 16)), ("a_w_out", (D, D)),
        ("b_w_in", (D, 1696)), ("b_g_cq", (1, 384)), ("b_w_uq", (384, 1536)), ("b_g_ckv", (1, 256)),
        ("b_w_ukv", (256, 2048)), ("b_g_qn", (1, 64)), ("b_g_qr", (1, 32)), ("b_g_kn", (1, 64)),
        ("b_g_kr", (1, 32)), ("b_w_out", (D, D)),
        ("cosp", (128, NT * 16)), ("sinp", (128, NT * 16)), ("coss", (128, 16)), ("sins", (128, 16)),
        ("ident", (128, 128)),
    ]:
        I[name] = din(name, shape)
    O = {}
    for name, shape in [
        ("yp", (nseq * S, D)), ("ys", (128, D)), ("akp", (nseq * AR, D)), ("avp", (nseq * AR, D)),
        ("aks", (128, D)), ("avs", (128, D)), ("bcp", (nseq * S, 256)), ("brp", (nseq * S, 32)),
        ("bcs", (128, 256)), ("brs", (128, 32)),
    ]:
        O[name] = dout(name, shape)
    y0p = dscr("y0p", (nseq * S, D))
    y0s = dscr("y0s", (128, D))
    modd = dscr("modd", (2 * NS, 3 * D))

    sch = Sched()
    stack = ExitStack()
    ARENA_BYTES = 207 * 1024
    arena_t = stack.enter_context(nc.sbuf_tensor("arena", [128, ARENA_BYTES // 4], F32))
    A = Arena(arena_t, ARENA_BYTES)
    psb = [stack.enter_context(nc.psum_tensor("psb%d" % i, [128, 512], F32)) for i in range(8)]

    def tl_(t):
        return list(t) if isinstance(t, (list, tuple)) else [t]

    def dma_in(out_ap, in_ap, t, rd=(), key=None):
        sch.op("sp", lambda e, o=out_ap, i=in_ap: e.dma_start(out=o, in_=i), rd=rd, wr=tl_(t), dma=key or ("l_" + tl_(t)[0].name))

    def dma_out(out_ap, in_ap, t, wr=(), key=None):
        sch.op("pool", lambda e, o=out_ap, i=in_ap: e.dma_start(out=o, in_=i), rd=tl_(t), wr=wr, dma=key or ("s_" + tl_(t)[0].name))

    def mm(pst, out_ap, pairs, rd, start=True, stop=True):
        def fn(e, out_ap=out_ap, pairs=pairs, start=start, stop=stop):
            n = len(pairs)
            for i, (l, r) in enumerate(pairs):
                ins = e.matmul(out_ap, lhsT=l, rhs=r, start=(start and i == 0), stop=(stop and i == n - 1))
            return ins
        sch.op("pe", fn, rd=rd, wr=[pst])

    def transposes(pst, items, rd, ident):
        def fn(e, items=items, ident=ident):
            for o, i in items:
                ins = e.transpose(out=o, in_=i, identity=ident)
            return ins
        sch.op("pe", fn, rd=rd, wr=[pst])

    def act(out, in_, func, rd, wr, **kw):
        sch.op("act", lambda e, o=out, i=in_, f=func, kw=kw: e.activation(out=o, in_=i, func=f, **kw), rd=rd, wr=wr)

    def vop(eng, name, rd, wr, *args, **kw):
        sch.op(eng, lambda e, name=name, args=args, kw=kw: getattr(e, name)(*args, **kw), rd=rd, wr=wr)

    cp_rr = [0]

    def evac_copy(out, in_, rd, wr):
        cp_rr[0] += 1
        if cp_rr[0] % 2:
            act(out, in_, AF.Copy, rd, wr)
        else:
            vop("dve", "tensor_copy", rd, wr, out=out, in_=in_)

    def bc(ap, shape):
        return ap.to_broadcast(list(shape))

    identf = A.f32(128)
    identb = A.bf16(128)
    t_ident = T("ident")
    dma_in(identf, I["ident"], t_ident)
    t_identb = T("identb")
    act(identb, identf, AF.Copy, [t_ident], [t_identb])
    bar_scr = {e: A.f32(16) for e in ("act", "dve", "pool")}
    t_scr = T("barscr")

    def do_barrier():
        sch.barrier({
            "act": lambda e: e.activation(out=bar_scr["act"], in_=identf[:, 0:16], func=AF.Copy),
            "dve": lambda e: e.tensor_copy(out=bar_scr["dve"], in_=identf[:, 0:16]),
            "pool": lambda e: e.tensor_copy(out=bar_scr["pool"], in_=identf[:, 0:16]),
            "pe": lambda e: e.transpose(out=psb[2][:, :].bitcast(BF16)[:, 0:128], in_=identb, identity=identb),
        }, pe_wr=[t_tb])

    mmb = Ring([(psb[0][:, :], T("mm0")), (psb[1][:, :], T("mm1"))])
    tb_ap = psb[2][:, :].bitcast(BF16)
    t_tb = T("tb")

    def load_bc(name, n):
        ap = A.f32(n)
        t = T("g_" + name)
        dma_in(ap, bc(I[name][0:1, :], [128, n]), t)
        return ap, t

    gq, t_gq = load_bc("a_g_q", 64)
    gk, t_gk = load_bc("a_g_k", 64)
    cb, t_cb = load_bc("cbias", 16)
    gcq, t_gcq = load_bc("b_g_cq", 384)
    gckv, t_gckv = load_bc("b_g_ckv", 256)
    gqn, t_gqn = load_bc("b_g_qn", 64)
    gqr, t_gqr = load_bc("b_g_qr", 32)
    gkn, t_gkn = load_bc("b_g_kn", 64)
    gkr, t_gkr = load_bc("b_g_kr", 32)
    ncb = A.f32(16)
    t_ncb = T("ncb")
    vop("dve", "tensor_scalar", [t_cb], [t_ncb], out=ncb, in0=cb, scalar1=-1.0, scalar2=None, op0=ALU.mult)
    invn3 = A.f32(3)
    invn8 = A.f32(8)
    t_invn = T("invn")
    for ap_, v in ((invn3[:, 0:1], 1.0 / 384), (invn3[:, 1:2], 1.0 / 256), (invn3[:, 2:3], 1.0 / 32),
                   (invn8[:, 0:4], 1.0 / 64), (invn8[:, 4:8], 1.0 / 32)):
        vop("pool", "memset", [], [t_invn], ap_, v)

    def stats_tile(n):
        return A.f32(n)

    persist_mark = A.mark()

    wA_in = A.bf16(8 * 4096)
    wA_in_v = wA_in.rearrange("p (k n) -> p k n", n=4096)
    wA_out = A.bf16(8 * 1024)
    wA_out_v = wA_out.rearrange("p (k n) -> p k n", n=1024)
    t_wAin = [T("wAin%d" % i) for i in range(8)]
    t_wAout = [T("wAout%d" % i) for i in range(2)]
    Eb = A.bf16(16 * 2 * 128)
    Ebv = Eb.rearrange("p (h j q) -> p h j q", j=2, q=128)
    t_E = T("E")
    M0 = A.bf16(128)
    t_M0 = T("M0")
    vop("pool", "memset", [], [t_M0], M0, 1.0)
    vop("pool", "memset", [], [t_M0], M0[0:64, 64:128], 0.0)
    wA_mark = A.mark()

    stg = Ring([(A.f32(2048), T("stg%d" % i)) for i in range(4)])
    cast_rr = [0]

    def cast(out, in_, rd, wr):
        cast_rr[0] += 1
        k = cast_rr[0] % 3
        if k == 0:
            act(out, in_, AF.Copy, rd, wr)
        elif k == 1:
            vop("dve", "tensor_copy", rd, wr, out=out, in_=in_)
        else:
            vop("pool", "tensor_copy", rd, wr, out=out, in_=in_)

    def load_weight(src, K, N, dst_v, tlist, tcols):
        cw = 2048 // K
        for c0 in range(0, N, cw):
            w_ = min(cw, N - c0)
            sap, st = stg.next()
            sv = sap[:, 0:K * w_].rearrange("p (k n) -> p k n", n=w_)
            dma_in(sv, src[:, c0:c0 + w_].rearrange("(k p) n -> p k n", p=128), st)
            cast(dst_v[:, :, c0:c0 + w_], sv, [st], [tlist[c0 // tcols]])

    sc = A.f32(8 * NS)
    t_sc = T("sc")
    dma_in(sc, I["scT"], t_sc)
    act(sc, sc, AF.Silu, [t_sc], [t_sc])
    ones1 = A.f32(8, parts=1)
    t_ones = T("ones1")
    vop("dve", "memset", [], [t_ones], ones1, 1.0)
    adab = Ring([(A.f32(256, parts=1), T("adab%d" % i)) for i in range(4)])
    modc = Ring([(A.f32(256, parts=NS), T("modc%d" % i)) for i in range(4)])
    scv = sc.rearrange("p (k s) -> p k s", s=NS)
    for l in range(2):
        for c in range(12):
            c0 = c * 256
            sap, st = stg.next()
            sv = sap.rearrange("p (k n) -> p k n", n=256)
            dma_in(sv, I["ada_w"][l * D:(l + 1) * D, c0:c0 + 256].rearrange("(k p) n -> p k n", p=128), st)
            bap, bt = adab.next()
            dma_in(bap, I["ada_b"][l:l + 1, c0:c0 + 256], bt)
            pap, pt = mmb.next()
            pairs = [(scv[:, k, :], sv[:, k, :]) for k in range(8)] + [(ones1[0:1, 0:NS], bap)]
            mm(pt, pap[0:NS, 0:256], pairs, [t_sc, st, bt, t_ones])
            map_, mt = modc.next()
            act(map_, pap[0:NS, 0:256], AF.Copy, [pt], [mt])
            dma_out(modd[l * NS:(l + 1) * NS, c0:c0 + 256], map_, mt)

    load_weight(I["a_w_in"], 8, 4096, wA_in_v, t_wAin, 512)
    load_weight(I["a_w_out"], 8, 1024, wA_out_v, t_wAout, 512)
    for hh in range(0, 16, 8):
        sap, st = stg.next()
        dma_in(sap, I["erel"][:, hh * 256:(hh + 8) * 256], st)
        for h in range(hh, hh + 8):
            act(Eb[:, h * 256:(h + 1) * 256], sap[:, (h - hh) * 256:(h - hh + 1) * 256], AF.Exp, [st, t_ncb], [t_E],
                bias=ncb[:, h:h + 1])
    vop("pool", "memset", [], [t_E], Ebv[64:128, :, 1, 0:64], 0.0)
    do_barrier()

    A.reset(wA_mark)
    KR = 6
    kT = A.bf16(8 * KR * 128)
    kTv = kT.rearrange("p (j n) -> p j n", n=KR * 128)
    t_kT = [T("kT%d" % i) for i in range(KR)]
    Vp = A.bf16(KR * 16 * 65)
    Vpv = Vp.rearrange("p (s h d) -> p s h d", h=16, d=65)
    t_Vp = [T("Vp%d" % i) for i in range(KR)]
    vop("pool", "memset", [], t_Vp, Vpv[:, :, :, 64:65], 1.0)
    xs_ = Ring([(A.f32(1024), T("x%d" % i)) for i in range(6)])
    dtmp = A.f32(1024)
    t_dtmp = (T("dtmp_a"), T("dtmp_b"))
    hb = A.bf16(1024)
    t_hb = (T("hb_a"), T("hb_b"))
    hTs = Ring([(A.bf16(1024), T("hT%d" % i)) for i in range(2)])
    pfs = Ring([(A.f32(1024), (T("pf%d_a" % i), T("pf%d_b" % i))) for i in range(3)])
    qnb = A.bf16(1024)
    t_qnb = (T("qnb_a"), T("qnb_b"))
    knb = A.bf16(1024)
    t_knb = (T("knb_a"), T("knb_b"))
    qTs = Ring([(A.bf16(1024), T("qT%d" % i)) for i in range(2)])
    zss = Ring([(A.bf16(1024), T("zs%d" % i)) for i in range(3)])
    PTs = Ring([(A.bf16(512), [T("PT%d_%d" % (i, k)) for k in range(4)]) for i in range(3)])
    ogs = Ring([(A.bf16(1024), T("og%d" % i)) for i in range(2)])
    ogT = A.bf16(1024)
    t_ogT = T("ogT")
    bc3 = A.f32(2048)
    t_bc3 = T("bc3")
    gates = Ring([(A.f32(1024), T("gate%d" % i)) for i in range(2)])
    cur_gate = [None]
    G1 = A.f32(1024)
    t_G1 = T("G1")
    st_ss = A.f32(4)
    t_ss = T("ss")
    st_16 = A.f32(64)
    t_16 = (T("st16_a"), T("st16_b"))
    rden = A.f32(16)
    t_rden = T("rden")
    sringA = Ring([(psb[3][:, :], T("sa0")), (psb[4][:, :], T("sa1")), (psb[7][:, :], T("sa2"))])
    t_O = [T("O0"), T("O1")]

    def o_ap(h):
        if h < 7:
            return psb[5][:, h * 65:(h + 1) * 65], t_O[0]
        if h < 14:
            return psb[6][:, (h - 7) * 65:(h - 6) * 65], t_O[1]
        return psb[5][:, (h - 14) * 65:(h - 13) * 65], t_O[0]

    def load_mod(l, s):
        dma_in(bc3, bc(modd[l * NS + s:l * NS + s + 1, 0:2048], [128, 2048]), t_bc3, key="l_bc3")
        gap, gt = gates.next()
        dma_in(gap, bc(modd[l * NS + s:l * NS + s + 1, 2048:3072], [128, 1024]), gt)
        cur_gate[0] = (gap, gt)
        dma_in(G1, bc(I["norm_g"][l:l + 1, :], [128, 1024]), t_G1, key="l_G1")
        vop("dve", "scalar_tensor_tensor", [t_bc3, t_G1], [t_G1], out=G1, in0=bc3[:, 1024:2048], scalar=1.0, in1=G1,
            op0=ALU.add, op1=ALU.mult)

    def rstd_from(ss_ap, n, tss, scale, scr_ap):
        act(scr_ap, ss_ap, AF.Ln, tl_(tss), tl_(tss), scale=scale, bias=EPS)
        act(ss_ap, scr_ap, AF.Exp, tl_(tss), tl_(tss), scale=-0.5)

    def ada_norm_tile(xap, xt):
        act(junk, xap, AF.Square, [xt], [t_junk, t_ss], accum_out=st_ss[:, 0:1])
        rstd_from(st_ss[:, 0:1], 1, t_ss, 1.0 / D, st_ss[:, 1:2])
        vop("dve", "scalar_tensor_tensor", [xt, t_ss, t_G1], [t_dtmp], out=dtmp, in0=xap, scalar=st_ss[:, 0:1], in1=G1,
            op0=ALU.mult, op1=ALU.mult)
        vop("dve", "tensor_tensor", [t_dtmp, t_bc3], [t_hb], out=hb, in0=dtmp, in1=bc3[:, 0:1024], op=ALU.add)
        transposes(t_tb, [(tb_ap[:, k * 128:(k + 1) * 128], hb[:, k * 128:(k + 1) * 128]) for k in range(8)],
                   [t_hb, t_identb], identb)
        hT, hTt = hTs.next()
        evac_copy(hT, tb_ap, [t_tb], [hTt])
        return hT.rearrange("p (k n) -> p k n", n=128), hTt

    def head_rms(pf, pft, g_ap, g_t, out_bf, out_t, also_f32):
        gb = bc(g_ap.unsqueeze(1), [128, 8, 64])
        for hf in range(2):
            cs_ = slice(hf * 512, hf * 512 + 512)
            pv = pf[:, cs_].rearrange("p (h d) -> p h d", d=64)
            dv = dtmp[:, cs_].rearrange("p (h d) -> p h d", d=64)
            s16 = st_16[:, hf * 8:hf * 8 + 8]
            vop("dve", "tensor_tensor", [pft[hf]], [t_dtmp[hf]], out=dtmp[:, cs_], in0=pf[:, cs_], in1=pf[:, cs_], op=ALU.mult)
            vop("dve", "tensor_reduce", [t_dtmp[hf]], [t_16[hf]], out=s16, in_=dv, axis=AX.X, op=ALU.add)
            yield
        rstd_from(st_16[:, 0:16], 16, t_16, 1.0 / 64, st_16[:, 16:32])
        yield
        for hf in range(2):
            cs_ = slice(hf * 512, hf * 512 + 512)
            pv = pf[:, cs_].rearrange("p (h d) -> p h d", d=64)
            s16 = st_16[:, hf * 8:hf * 8 + 8]
            vop("dve", "tensor_tensor", [pft[hf], t_16[hf]], [pft[hf]], out=pv, in0=pv, in1=bc(s16.unsqueeze(2), [128, 8, 64]), op=ALU.mult)
            yield
            if also_f32:
                vop("dve", "tensor_tensor", [pft[hf], g_t], [pft[hf]], out=pv, in0=pv, in1=gb, op=ALU.mult)
                act(out_bf[:, cs_], pf[:, cs_], AF.Copy, [pft[hf]], [out_t[hf]])
            else:
                vop("dve", "tensor_tensor", [pft[hf], g_t], [out_t[hf]], out=out_bf[:, cs_].rearrange("p (h d) -> p h d", d=64), in0=pv, in1=gb,
                    op=ALU.mult)
            yield

    class Ctx:
        pass

    def sample_cache_tile(c):
        j = c.j
        slot = c.ti % KR
        pf, pft = pfs.next()
        dma_in(pf, I["ck"][j * 128:(j + 1) * 128, :], pft)
        vop("pool", "tensor_copy", list(pft), list(t_knb), out=knb, in_=pf)
        transposes(t_tb, [(tb_ap[:, q * 128:(q + 1) * 128], knb[:, q * 128:(q + 1) * 128]) for q in range(8)],
                   list(t_knb) + [t_identb], identb)
        evac_copy(kTv[:, :, slot * 128:(slot + 1) * 128], tb_ap.rearrange("p (j n) -> p j n", n=128), [t_tb], [t_kT[slot]])
        pf2, pft2 = pfs.next()
        dma_in(pf2, I["cv"][j * 128:(j + 1) * 128, :], pft2)
        vop("dve", "tensor_copy", list(pft2), [t_Vp[slot]], out=Vpv[:, slot, :, 0:64], in_=pf2.rearrange("p (h d) -> p h d", d=64))

    def load_x(c):
        c.x, c.xt = xs_.next()
        dma_in(c.x, c.xsrc, c.xt)

    def norm_part(c):
        if c.premod is not None:
            c.premod()
        c.gate = cur_gate[0]
        xap, xt = c.x, c.xt
        act(hb, xap, AF.Square, [xt], list(t_hb) + [t_ss], accum_out=st_ss[:, 0:1])
        rstd_from(st_ss[:, 0:1], 1, t_ss, 1.0 / D, st_ss[:, 1:2])
        for hf in range(2):
            cs_ = slice(hf * 512, hf * 512 + 512)
            vop("dve", "scalar_tensor_tensor", [xt, t_ss, t_G1], [t_dtmp[hf]], out=dtmp[:, cs_], in0=xap[:, cs_], scalar=st_ss[:, 0:1],
                in1=G1[:, cs_], op0=ALU.mult, op1=ALU.mult)
            vop("dve", "tensor_tensor", [t_dtmp[hf], t_bc3], [t_hb[hf]], out=hb[:, cs_], in0=dtmp[:, cs_], in1=bc3[:, cs_], op=ALU.add)
            yield

    def hT_part(c):
        transposes(t_tb, [(tb_ap[:, k * 128:(k + 1) * 128], hb[:, k * 128:(k + 1) * 128]) for k in range(8)],
                   list(t_hb) + [t_identb], identb)
        hT, hTt = hTs.next()
        evac_copy(hT, tb_ap, [t_tb], [hTt])
        c.hT, c.hTt = hT.rearrange("p (k n) -> p k n", n=128), hTt

    def frontA(c):
        for c2 in c.load_list:
            load_x(c2)
        if c.kind == "cache":
            sample_cache_tile(c)
            yield
            if c.norm_next is not None:
                yield from norm_part(c.norm_next)
                hT_part(c.norm_next)
            return
        hT, hTt = c.hT, c.hTt
        slot = c.ti % KR
        c.slot = slot
        pfq, pfqt = pfs.next()
        pfk, pfkt = pfs.next()
        zs, zst = zss.next()
        c.zs, c.zst = zs, zst
        pfv = pfvt = None
        if c.store_kv is not None:
            pfv, pfvt = pfs.next()
        for cg in range(8):
            pap, pt = mmb.next()
            mm(pt, pap, [(hT[:, k, :], wA_in_v[:, k, cg * 512:(cg + 1) * 512]) for k in range(8)],
               [hTt, t_wAin[cg]])
            half = slice((cg % 2) * 512, (cg % 2) * 512 + 512)
            if cg < 2:
                act(pfq[:, half], pap, AF.Copy, [pt], [pfqt[cg % 2]])
            elif cg < 4:
                act(pfk[:, half], pap, AF.Copy, [pt], [pfkt[cg % 2]])
            elif cg < 6:
                h0 = (cg - 4) * 8
                if pfv is not None:
                    act(pfv[:, half], pap, AF.Copy, [pt], [pfvt[cg % 2]])
                    vop("pool", "tensor_copy", [pfvt[cg % 2]], [t_Vp[slot]], out=Vpv[:, slot, h0:h0 + 8, 0:64],
                        in_=pfv[:, half].rearrange("p (h d) -> p h d", d=64))
                else:
                    vop("dve", "tensor_copy", [pt], [t_Vp[slot]], out=Vpv[:, slot, h0:h0 + 8, 0:64],
                        in_=pap.rearrange("p (h d) -> p h d", d=64))
                if cg == 5 and c.sample:
                    vop("pool", "memset", [], [t_Vp[slot]], Vpv[32:64, slot, :, :], 0.0)
                    vop("pool", "memset", [], [t_Vp[slot]], Vpv[64:128, slot, :, :], 0.0)
            elif cg == 6:
                z6 = (pap, pt)
            else:
                act(zs[:, 0:512], z6[0], AF.Silu, [z6[1]], [zst])
                act(zs[:, 512:1024], pap, AF.Silu, [pt], [zst])
            yield
            if cg == 0 and c.norm_next is not None:
                yield from norm_part(c.norm_next)
            if cg == 1:
                yield from head_rms(pfq, pfqt, gq, t_gq, qnb, t_qnb, False)
            if cg == 3:
                yield from head_rms(pfk, pfkt, gk, t_gk, knb, t_knb, True)
                if c.store_kv is not None:
                    dma_out(c.store_kv[0], pfk, pfkt)
                yield
            if cg == 5:
                transposes(t_tb, [(tb_ap[:, j * 128:(j + 1) * 128], qnb[:, j * 128:(j + 1) * 128]) for j in range(8)],
                           list(t_qnb) + [t_identb], identb)
                qT, qTt = qTs.next()
                c.qT, c.qTt = qT.rearrange("p (j n) -> p j n", n=128), qTt
                evac_copy(qT, tb_ap, [t_tb], [qTt])
                if pfv is not None:
                    dma_out(c.store_kv[1], pfv, pfvt)
                yield
            if cg == 7:
                transposes(t_tb, [(tb_ap[:, j * 128:(j + 1) * 128], knb[:, j * 128:(j + 1) * 128]) for j in range(8)],
                           list(t_knb) + [t_identb], identb)
                evac_copy(kTv[:, :, slot * 128:(slot + 1) * 128], tb_ap.rearrange("p (j n) -> p j n", n=128), [t_tb],
                          [t_kT[slot]])
                yield
        if c.norm_next is not None:
            hT_part(c.norm_next)

    def attnA(c):
        if c.kind == "cache":
            return
        ti = c.ti
        js = [j for j in range(5) if ti - 4 + j >= c.first_ti]
        og, ogt = ogs.next()
        c.og, c.ogt = og, ogt

        def norm_hook(h):
            def f():
                g0, g1, src = {6: (0, 7, psb[5][:, 0:455]), 13: (7, 14, psb[6][:, 0:455]), 15: (14, 16, psb[5][:, 0:130])}[h]
                ng = g1 - g0
                ov = src.rearrange("p (h d) -> p h d", d=65)
                tO = o_ap(g0)[1]
                vop("dve", "reciprocal", [tO], [t_rden], out=rden[:, g0:g1], in_=ov[:, :, 64])
                dv = dtmp[:, g0 * 64:g1 * 64].rearrange("p (h d) -> p h d", d=64)
                vop("dve", "tensor_tensor", [tO, t_rden], list(t_dtmp), out=dv, in0=ov[:, :, 0:64],
                    in1=bc(rden[:, g0:g1].unsqueeze(2), [128, ng, 64]), op=ALU.mult)
                vop("pool", "tensor_tensor", list(t_dtmp) + [c.zst], [ogt], out=og[:, g0 * 64:g1 * 64],
                    in0=dtmp[:, g0 * 64:g1 * 64], in1=c.zs[:, g0 * 64:g1 * 64], op=ALU.mult)
            return f

        tiles = []
        for h in range(16):
            rg, j8 = h // 8, h % 8
            ps = slice(rg * 64, rg * 64 + 64)
            oap, ot = o_ap(h)
            for j in js:
                sl = (ti - 4 + j) % KR
                emul = None
                if j >= 3:
                    emul = (Ebv[:, h, j - 3, :], t_E)
                elif j == 0 and not c.sample:
                    emul = (M0, t_M0)
                tiles.append(dict(l=kTv[ps, j8, sl * 128:(sl + 1) * 128], r=c.qT[ps, j8, :], rdq=[t_kT[sl], c.qTt],
                                  V=Vpv[:, sl, h, :], Vt=t_Vp[sl], o=oap, ot=ot, start=(j == js[0]), stop=(j == js[-1]),
                                  emul=emul, masks=[],
                                  after=(norm_hook(h) if (h in (6, 13, 15) and j == js[-1]) else None)))
                if j == 3 and 4 in js:
                    tiles[-1]["pair"] = Eb[:, h * 256:(h + 1) * 256]
                if j == 4 and 3 in js:
                    tiles[-1]["pair_of"] = tiles[-2]
        yield from attn_stream(tiles, 0.125, sringA, PTs)

    def backA(c, wout_v, t_wout):
        if c.kind == "cache":
            return
        transposes(t_tb, [(tb_ap[:, k * 128:(k + 1) * 128], c.og[:, k * 128:(k + 1) * 128]) for k in range(8)],
                   [c.ogt, t_identb], identb)
        evac_copy(ogT, tb_ap, [t_tb], [t_ogT])
        ogTv = ogT.rearrange("p (k n) -> p k n", n=128)
        yield
        for cg in range(2):
            pap, pt = mmb.next()
            mm(pt, pap, [(ogTv[:, k, :], wout_v[:, k, cg * 512:(cg + 1) * 512]) for k in range(8)], [t_ogT, t_wout[cg]])
            cs = slice(cg * 512, cg * 512 + 512)
            vop("dve", "tensor_tensor", [pt, c.gate[1]], [t_dtmp[cg]], out=dtmp[:, cs], in0=pap, in1=c.gate[0][:, cs], op=ALU.mult)
            vop("dve", "tensor_tensor", [t_dtmp[cg], c.xt], [c.xt], out=c.x[:, cs], in0=dtmp[:, cs], in1=c.x[:, cs], op=ALU.add)
            yield
        for dst in c.ydst:
            dma_out(dst, c.x, c.xt)

    def attn_stream(tiles, scale, sring, ptring, lookahead=2):
        items = [tiles[i:i + 4] for i in range(0, len(tiles), 4)]

        def issue_qk(item):
            sap, st_ = sring.next()
            mats = [(sap[:, i * 128:(i + 1) * 128], tl["l"], tl["r"]) for i, tl in enumerate(item)]

            def fn(e, mats=mats):
                for o, l, r in mats:
                    ins = e.matmul(o, lhsT=l, rhs=r, start=True, stop=True)
                return ins
            rd = []
            for tl in item:
                for t_ in tl["rdq"]:
                    if t_ not in rd:
                        rd.append(t_)
            sch.op("pe", fn, rd=rd, wr=[st_])
            return sap, st_

        def softmax_part(item, sap, st_):
            n = len(item)
            PT, subs = ptring.next()
            act(PT[:, 0:n * 128], sap[:, 0:n * 128], AF.Exp, [st_], list(subs), scale=scale)
            idx = 0
            while idx < n:
                tl = item[idx]
                if tl["emul"] is None:
                    idx += 1
                    continue
                eap, et = tl["emul"]
                if idx + 1 < n and tl.get("pair") is not None and item[idx + 1].get("pair_of") is tl:
                    vop("dve", "tensor_tensor", [subs[idx], subs[idx + 1], et], [subs[idx], subs[idx + 1]],
                        out=PT[:, idx * 128:(idx + 2) * 128], in0=PT[:, idx * 128:(idx + 2) * 128], in1=tl["pair"], op=ALU.mult)
                    idx += 2
                    continue
                vop("dve", "tensor_tensor", [subs[idx], et], [subs[idx]], out=PT[:, idx * 128:(idx + 1) * 128],
                    in0=PT[:, idx * 128:(idx + 1) * 128], in1=eap, op=ALU.mult)
                idx += 1
            for idx, tl in enumerate(item):
                for (ps_, c0, c1) in tl["masks"]:
                    vop("pool", "memset", [], [subs[idx]], PT[ps_, idx * 128 + c0:idx * 128 + c1], 0.0)
            return PT, subs

        pend = [issue_qk(it) for it in items[:lookahead]]
        soft = [softmax_part(items[0], *pend.pop(0))]
        yield
        for i, item in enumerate(items):
            if i + lookahead < len(items):
                pend.append(issue_qk(items[i + lookahead]))
            if i + 1 < len(items):
                soft.append(softmax_part(items[i + 1], *pend.pop(0)))
            PT, subs = soft.pop(0)
            mats = [(tl["o"], PT[:, idx * 128:(idx + 1) * 128], tl["V"], tl["start"], tl["stop"]) for idx, tl in enumerate(item)]

            def fnpv(e, mats=mats):
                for o, l, r, s0, s1 in mats:
                    ins = e.matmul(o, lhsT=l, rhs=r, start=s0, stop=s1)
                return ins
            rd = list(subs[:len(item)])
            wr = []
            for tl in item:
                if tl["Vt"] not in rd:
                    rd.append(tl["Vt"])
                if tl["ot"] not in wr:
                    wr.append(tl["ot"])
            sch.op("pe", fnpv, rd=rd, wr=wr)
            for tl in item:
                if tl["after"] is not None:
                    tl["after"]()
            yield

    def run_pipeline(tiles, stages, late_first=False, order=None):
        ns = len(stages)
        for step in range(len(tiles) + ns - 1):
            gens = []
            for k in (order if order is not None else (reversed(range(ns)) if late_first else range(ns))):
                n = step - k
                if 0 <= n < len(tiles):
                    if k == 0 and tiles[n].pre is not None:
                        tiles[n].pre()
                    gens.append(stages[k](tiles[n]))
            while gens:
                for g in list(gens):
                    try:
                        next(g)
                    except StopIteration:
                        gens.remove(g)

    tilesA = []
    gti = 0
    for s in range(nseq):
        for t in range(NT):
            c = Ctx()
            c.kind = "prompt"
            c.sample = False
            c.ti = gti + t
            c.first_ti = gti
            c.xsrc = I["xp"][s * S + t * 128:s * S + (t + 1) * 128, :]
            c.ydst = [y0p[s * S + t * 128:s * S + (t + 1) * 128, :]]
            if debug_layers == 1:
                c.ydst = [O["yp"][s * S + t * 128:s * S + (t + 1) * 128, :]]
            c.store_kv = None
            import os
            if t >= NT - NAO and not os.environ.get("DBG_NOSTORE"):
                r0 = s * AR + (t - (NT - NAO)) * 128
                c.store_kv = (O["akp"][r0:r0 + 128, :], O["avp"][r0:r0 + 128, :])
            c.pre = None
            c.premod = (lambda s=s: load_mod(0, s)) if t == 0 else None
            tilesA.append(c)
        gti += NT
    for j in range(4):
        c = Ctx()
        c.kind = "cache"
        c.ti = gti + j
        c.j = j
        c.pre = None
        tilesA.append(c)
    c = Ctx()
    c.kind = "sample"
    c.sample = True
    c.first_ti = gti
    c.ti = gti + 4
    c.xsrc = I["xs"]
    c.ydst = [y0s] if debug_layers != 1 else [O["ys"]]
    c.store_kv = (O["aks"], O["avs"])
    c.pre = None
    c.premod = lambda: load_mod(0, nseq)
    tilesA.append(c)
    import os
    if os.environ.get("DBG_MEM"):
        print("layer A arena used", A.off, "of", A.cap)
    if debug_layers == 0:
        tilesA = []
    if debug_layers < 0:
        tilesA = tilesA[:-debug_layers]
    for c in tilesA:
        c.load_list = []
        c.norm_next = None
    normal = [i for i, c in enumerate(tilesA) if c.kind != "cache"]
    prologue_load, prologue_norm = [], []
    for i in normal:
        (tilesA[i - 2].load_list if i >= 2 else prologue_load).append(tilesA[i])
        if i >= 1:
            tilesA[i - 1].norm_next = tilesA[i]
        else:
            prologue_norm.append(tilesA[i])
    for c in prologue_load:
        load_x(c)
    for c in prologue_norm:
        for _ in norm_part(c):
            pass
        hT_part(c)
    run_pipeline(tilesA, [frontA, attnA, lambda c: backA(c, wA_out_v, t_wAout)], late_first=True)
    do_barrier()

    if debug_layers >= 2:
        A.reset(persist_mark)
        wBin = A.bf16(8 * 1696)
        wBin_v = wBin.rearrange("p (k n) -> p k n", n=1696)
        wuq = A.bf16(3 * 1536)
        wuq_v = wuq.rearrange("p (k n) -> p k n", n=1536)
        wukv = A.bf16(2 * 2048)
        wukv_v = wukv.rearrange("p (k n) -> p k n", n=2048)
        wBout = A.bf16(8 * 1024)
        wBout_v = wBout.rearrange("p (k n) -> p k n", n=1024)
        t_wBin = [T("wBin%d" % i) for i in range(4)]
        t_wuq = [T("wuq%d" % i) for i in range(4)]
        t_wukv = [T("wukv%d" % i) for i in range(4)]
        t_wBout = [T("wBout%d" % i) for i in range(2)]
        wB_mark = A.mark()
        stg = Ring([(A.f32(2048), T("stgB%d" % i)) for i in range(4)])
        load_weight(I["b_w_in"], 8, 1696, wBin_v, t_wBin, 512)
        load_weight(I["b_w_uq"], 3, 1536, wuq_v, t_wuq, 384)
        load_weight(I["b_w_ukv"], 2, 2048, wukv_v, t_wukv, 512)
        load_weight(I["b_w_out"], 8, 1024, wBout_v, t_wBout, 512)
        do_barrier()
        A.reset(wB_mark)
        ZT = max(NT, 1)
        zsB = A.bf16(ZT * 1024)
        zsB_v = zsB.rearrange("p (t n) -> p t n", n=1024)
        t_zsB = [T("zsB%d" % i) for i in range(ZT)]
        cqnT = A.bf16(3 * ZT * 128)
        cqnT_v = cqnT.rearrange("p (k n) -> p k n", n=ZT * 128)
        t_cq = [T("cq%d" % i) for i in range(ZT)]
        ckvnT = A.bf16(2 * NTB * 128)
        ckvnT_v = ckvnT.rearrange("p (k n) -> p k n", n=NTB * 128)
        t_ckv = [T("ckv%d" % i) for i in range(NTB)]
        krb = A.bf16(NTB * 32)
        krb_v = krb.rearrange("p (t n) -> p t n", n=32)
        t_kr = [T("kr%d" % i) for i in range(NTB)]
        kTg = A.bf16(4 * NTB * 128)
        kTg_v = kTg.rearrange("p (h n) -> p h n", n=NTB * 128)
        t_kTg = [T("kTg%d" % i) for i in range(NTB)]
        vop("pool", "memset", [], t_kTg, kTg, 0.0)
        Vg = A.bf16(NTB * 4 * 65)
        Vg_v = Vg.rearrange("p (t h d) -> p t h d", h=4, d=65)
        t_Vg = [T("Vg%d" % i) for i in range(NTB)]
        vop("pool", "memset", [], t_Vg, Vg_v[:, :, :, 64:65], 1.0)
        cosb = A.f32((NT + 1) * 16)
        sinb = A.f32((NT + 1) * 16)
        cos_v = cosb.rearrange("p (t f) -> p t f", f=16)
        sin_v = sinb.rearrange("p (t f) -> p t f", f=16)
        t_cs = T("cossin")
        dma_in(cosb[:, 0:NT * 16], I["cosp"], t_cs, key="l_cos")
        dma_in(sinb[:, 0:NT * 16], I["sinp"], t_cs, key="l_cos")
        dma_in(cos_v[:, NT, :], I["coss"], t_cs, key="l_cos")
        dma_in(sin_v[:, NT, :], I["sins"], t_cs, key="l_cos")
        gateB = A.f32(1024)
        t_gateB = T("gateB")
        work_mark = A.mark()
        SCB = 96.0 ** -0.5
        sbanks = Ring([(psb[3][:, :], T("sB0")), (psb[4][:, :], T("sB1")), (psb[7][:, :], T("sB2"))])
        obanks = Ring([(psb[5][:, :], T("oB0")), (psb[6][:, :], T("oB1"))])

        def ring_f32(name, n, k, parts=128):
            return Ring([(A.f32(n, parts=parts), T("%s%d" % (name, i))) for i in range(k)])

        def ring_bf16(name, n, k, parts=128):
            return Ring([(A.bf16(n, parts=parts), T("%s%d" % (name, i))) for i in range(k)])

        def rope(src3, nh, csi, out1, out2, rd, wr, rts):
            x1, x2 = src3[:, :, 0:16], src3[:, :, 16:32]
            cosx = bc(cos_v[:, csi, :].unsqueeze(1), [128, nh, 16])
            sinx = bc(sin_v[:, csi, :].unsqueeze(1), [128, nh, 16])
            rt_, t_rt = rts.next()
            a1 = rt_[:, 0:nh * 16].rearrange("p (h f) -> p h f", f=16)
            a2 = rt_[:, 64:64 + nh * 16].rearrange("p (h f) -> p h f", f=16)
            b1 = rt_[:, 128:128 + nh * 16].rearrange("p (h f) -> p h f", f=16)
            b2 = rt_[:, 192:192 + nh * 16].rearrange("p (h f) -> p h f", f=16)
            vop("pool", "tensor_tensor", rd + [t_cs], [t_rt], out=a1, in0=x1, in1=cosx, op=ALU.mult)
            vop("pool", "tensor_tensor", rd + [t_cs], [t_rt], out=a2, in0=x2, in1=sinx, op=ALU.mult)
            vop("pool", "tensor_tensor", rd + [t_cs], [t_rt], out=b1, in0=x2, in1=cosx, op=ALU.mult)
            vop("pool", "tensor_tensor", rd + [t_cs], [t_rt], out=b2, in0=x1, in1=sinx, op=ALU.mult)
            vop("pool", "tensor_tensor", [t_rt], wr, out=out1, in0=a1, in1=a2, op=ALU.subtract)
            vop("pool", "tensor_tensor", [t_rt], wr, out=out2, in0=b1, in1=b2, op=ALU.add)

        W = Ctx()

        def sweep1_bufs(s):
            do_barrier()
            A.reset(work_mark)
            W.xs = ring_f32("x1_", 1024, 3)
            W.dt = ring_f32("dt1_", 1024, 2)
            W.hb = ring_bf16("hb1_", 1024, 2)
            W.hT = ring_bf16("hT1_", 1024, 2)
            W.pj = ring_f32("pj", 672, 3)
            W.cqb = ring_bf16("cqb", 384, 2)
            W.ckvf = ring_f32("ckvf", 256, 3)
            W.ckvb = ring_bf16("ckvb", 256, 2)
            W.krn = ring_f32("krn", 32, 2)
            W.krf = ring_f32("krf", 32, 3)
            W.rt = ring_f32("rt", 256, 2)
            W.ss = ring_f32("ss1_", 4, 3)
            W.st3 = ring_f32("st3_", 8, 3)
            W.bc3 = A.f32(2048)
            W.t_bc3 = T("bc3B")
            W.G1 = A.f32(1024)
            W.t_G1 = T("G1B")
            dma_in(W.bc3, bc(modd[NS + s:NS + s + 1, 0:2048], [128, 2048]), W.t_bc3)
            dma_in(gateB, bc(modd[NS + s:NS + s + 1, 2048:3072], [128, 1024]), t_gateB)
            dma_in(W.G1, bc(I["norm_g"][1:2, :], [128, 1024]), W.t_G1)
            vop("dve", "scalar_tensor_tensor", [W.t_bc3, W.t_G1], [W.t_G1], out=W.G1, in0=W.bc3[:, 1024:2048], scalar=1.0, in1=W.G1,
                op0=ALU.add, op1=ALU.mult)

        def s1_load(c):
            if c.kind == "cache":
                return
            c.x, c.xt = W.xs.next()
            dma_in(c.x, c.ysrc, c.xt)
            return
            yield

        def s1_norm(c):
            if c.kind == "cache":
                return
            xap, xt = c.x, c.xt
            ss, sst = W.ss.next()
            hb_, hbt = W.hb.next()
            dt_, dtt = W.dt.next()
            act(hb_, xap, AF.Square, [xt], [hbt, sst], accum_out=ss[:, 0:1])
            rstd_from(ss[:, 0:1], 1, sst, 1.0 / D, ss[:, 1:2])
            vop("dve", "scalar_tensor_tensor", [xt, sst, W.t_G1], [dtt], out=dt_, in0=xap, scalar=ss[:, 0:1], in1=W.G1,
                op0=ALU.mult, op1=ALU.mult)
            vop("dve", "tensor_tensor", [dtt, W.t_bc3], [hbt], out=hb_, in0=dt_, in1=W.bc3[:, 0:1024], op=ALU.add)
            c.hb_, c.hbt = hb_, hbt
            yield

        def s1_tr(c):
            if c.kind == "cache":
                return
            hb_, hbt = c.hb_, c.hbt
            transposes(t_tb, [(tb_ap[:, k * 128:(k + 1) * 128], hb_[:, k * 128:(k + 1) * 128]) for k in range(8)],
                       [hbt, t_identb], identb)
            hT, hTt = W.hT.next()
            evac_copy(hT, tb_ap, [t_tb], [hTt])
            c.hT, c.hTt = hT.rearrange("p (k n) -> p k n", n=128), hTt
            yield

        def s1_mm(c):
            if c.kind == "cache":
                return
            c.pj, c.pjt = W.pj.next()
            for cg in range(4):
                w_ = 512 if cg < 3 else 160
                pap, pt = mmb.next()
                mm(pt, pap[:, 0:w_], [(c.hT[:, k, :], wBin_v[:, k, cg * 512:cg * 512 + w_]) for k in range(8)], [c.hTt, t_wBin[cg]])
                if cg == 0:
                    z0 = (pap, pt)
                elif cg == 1:
                    act(zsB_v[:, c.zt, 0:512], z0[0], AF.Silu, [z0[1]], [t_zsB[c.zt]])
                    act(zsB_v[:, c.zt, 512:1024], pap, AF.Silu, [pt], [t_zsB[c.zt]])
                else:
                    act(c.pj[:, (cg - 2) * 512:(cg - 2) * 512 + w_], pap[:, 0:w_], AF.Copy, [pt], [c.pjt])
                yield

        def s1_e1(c):
            if c.kind == "cache":
                kt = c.kt
                c.cf, c.cft = W.ckvf.next()
                dma_in(c.cf, I["cckv"][kt * 128:(kt + 1) * 128, :], c.cft)
                c.kr_, c.krt = W.krf.next()
                dma_in(c.kr_, I["ckr"][kt * 128:(kt + 1) * 128, :], c.krt)
                c.ckvb, c.ckvbt = W.ckvb.next()
                vop("pool", "tensor_copy", [c.cft], [c.ckvbt], out=c.ckvb, in_=c.cf)
                vop("pool", "tensor_copy", [c.krt], [t_kr[c.kt]], out=krb_v[:, c.kt, :], in_=c.kr_)
                return
            pj, pjt = c.pj, c.pjt
            dt_, dtt = W.dt.next()
            st3, t_st3 = W.st3.next()
            vop("dve", "tensor_tensor", [pjt], [dtt], out=dt_[:, 0:672], in0=pj, in1=pj, op=ALU.mult)
            vop("dve", "tensor_reduce", [dtt], [t_st3], out=st3[:, 0:1], in_=dt_[:, 0:384], axis=AX.X, op=ALU.add)
            vop("dve", "tensor_reduce", [dtt], [t_st3], out=st3[:, 1:2], in_=dt_[:, 384:640], axis=AX.X, op=ALU.add)
            vop("dve", "tensor_reduce", [dtt], [t_st3], out=st3[:, 2:3], in_=dt_[:, 640:672], axis=AX.X, op=ALU.add)
            vop("dve", "tensor_tensor", [t_st3, t_invn], [t_st3], out=st3[:, 0:3], in0=st3[:, 0:3], in1=invn3, op=ALU.mult)
            rstd_from(st3[:, 0:3], 3, t_st3, 1.0, st3[:, 4:7])
            yield
            c.cqb, c.cqbt = W.cqb.next()
            vop("dve", "scalar_tensor_tensor", [pjt, t_st3, t_gcq], [c.cqbt], out=c.cqb, in0=pj[:, 0:384], scalar=st3[:, 0:1], in1=gcq,
                op0=ALU.mult, op1=ALU.mult)
            c.cf, c.cft = W.ckvf.next()
            vop("dve", "scalar_tensor_tensor", [pjt, t_st3, t_gckv], [c.cft], out=c.cf, in0=pj[:, 384:640], scalar=st3[:, 1:2], in1=gckv,
                op0=ALU.mult, op1=ALU.mult)
            dma_out(c.ckv_dst, c.cf, c.cft)
            c.ckvb, c.ckvbt = W.ckvb.next()
            vop("pool", "tensor_copy", [c.cft], [c.ckvbt], out=c.ckvb, in_=c.cf)
            c.krn, c.krnt = W.krn.next()
            vop("dve", "scalar_tensor_tensor", [pjt, t_st3, t_gkr], [c.krnt], out=c.krn, in0=pj[:, 640:672], scalar=st3[:, 2:3], in1=gkr,
                op0=ALU.mult, op1=ALU.mult)
            yield

        def s1_e2(c):
            kt, zt = c.kt, c.zt
            if c.kind != "cache":
                transposes(t_tb, [(tb_ap[:, k * 128:(k + 1) * 128], c.cqb[:, k * 128:(k + 1) * 128]) for k in range(3)],
                           [c.cqbt, t_identb], identb)
                evac_copy(cqnT_v[:, :, zt * 128:(zt + 1) * 128], tb_ap[:, 0:384].rearrange("p (k n) -> p k n", n=128), [t_tb], [t_cq[zt]])
                yield
            transposes(t_tb, [(tb_ap[:, k * 128:(k + 1) * 128], c.ckvb[:, k * 128:(k + 1) * 128]) for k in range(2)],
                       [c.ckvbt, t_identb], identb)
            evac_copy(ckvnT_v[:, :, kt * 128:(kt + 1) * 128], tb_ap[:, 0:256].rearrange("p (k n) -> p k n", n=128), [t_tb], [t_ckv[kt]])
            yield
            if c.kind != "cache":
                kr_, krt = W.krf.next()
                k3 = c.krn.rearrange("p (h f) -> p h f", h=1)
                o3 = kr_.rearrange("p (h f) -> p h f", h=1)
                rope(k3, 1, c.cs, o3[:, :, 0:16], o3[:, :, 16:32], [c.krnt], [krt], W.rt)
                dma_out(c.kr_dst, kr_, krt)
                vop("pool", "tensor_copy", [krt], [t_kr[kt]], out=krb_v[:, kt, :], in_=kr_)
                yield

        def sweep2_bufs():
            do_barrier()
            A.reset(work_mark)
            W.qg = ring_f32("qg", 384, 3)
            W.kf = ring_f32("kf", 256, 3)
            W.dsq = ring_f32("dsq", 384, 2)
            W.qnb = ring_bf16("qnb", 384, 3)
            W.knb = ring_bf16("knb", 384, 3)
            W.qT = ring_bf16("qTB", 512, 4)
            for qap_, qt_ in W.qT.items:
                vop("pool", "memset", [], [qt_], qap_, 0.0)
            W.PT = Ring([(A.bf16(512), [T("PTB%d_%d" % (i, k)) for k in range(4)]) for i in range(4)])
            W.don = ring_f32("don", 256, 2)
            W.rt = ring_f32("rt", 256, 2)
            W.st8 = ring_f32("st8_", 16, 3)
            W.st4 = ring_f32("st4_", 8, 3)
            W.rden = ring_f32("rden", 4, 2)

        def p2_mm(c):
            g, kt, zt = c.g, c.kt, c.zt
            if c.kind != "cache":
                pap, pt = mmb.next()
                mm(pt, pap[:, 0:384], [(cqnT_v[:, k, zt * 128:(zt + 1) * 128], wuq_v[:, k, g * 384:(g + 1) * 384]) for k in range(3)],
                   [t_cq[zt], t_wuq[g]])
                c.qg, c.qgt = W.qg.next()
                act(c.qg, pap[:, 0:384], AF.Copy, [pt], [c.qgt])
                yield
            pap, pt = mmb.next()
            mm(pt, pap, [(ckvnT_v[:, k, kt * 128:(kt + 1) * 128], wukv_v[:, k, g * 512:(g + 1) * 512]) for k in range(2)],
               [t_ckv[kt], t_wukv[g]])
            p3 = pap.rearrange("p (h d) -> p h d", d=128)
            c.kf, c.kft = W.kf.next()
            act(c.kf.rearrange("p (h d) -> p h d", d=64), p3[:, :, 0:64], AF.Copy, [pt], [c.kft])
            act(Vg_v[:, kt, :, 0:64], p3[:, :, 64:128], AF.Copy, [pt], [t_Vg[kt]])
            yield

        def p2_norm(c):
            kt = c.kt
            if c.kind != "cache":
                qg, qgt = c.qg, c.qgt
                q3 = qg.rearrange("p (h d) -> p h d", d=96)
                ds, dst_ = W.dsq.next()
                d3 = ds.rearrange("p (h d) -> p h d", d=96)
                st8, t_st8 = W.st8.next()
                vop("dve", "tensor_tensor", [qgt], [dst_], out=ds, in0=qg, in1=qg, op=ALU.mult)
                vop("dve", "tensor_reduce", [dst_], [t_st8], out=st8[:, 0:4], in_=d3[:, :, 0:64], axis=AX.X, op=ALU.add)
                vop("dve", "tensor_reduce", [dst_], [t_st8], out=st8[:, 4:8], in_=d3[:, :, 64:96], axis=AX.X, op=ALU.add)
                vop("dve", "tensor_tensor", [t_st8, t_invn], [t_st8], out=st8[:, 0:8], in0=st8[:, 0:8], in1=invn8, op=ALU.mult)
                rstd_from(st8[:, 0:8], 8, t_st8, 1.0, st8[:, 8:16])
                c.st8, c.t_st8 = st8, t_st8
            kf, kft = c.kf, c.kft
            ds, dst_ = W.dsq.next()
            st4, t_st4 = W.st4.next()
            vop("dve", "tensor_tensor", [kft], [dst_], out=ds[:, 0:256], in0=kf, in1=kf, op=ALU.mult)
            vop("dve", "tensor_reduce", [dst_], [t_st4], out=st4[:, 0:4], in_=ds[:, 0:256].rearrange("p (h d) -> p h d", d=64),
                axis=AX.X, op=ALU.add)
            rstd_from(st4[:, 0:4], 4, t_st4, 1.0 / 64, st4[:, 4:8])
            c.st4, c.t_st4 = st4, t_st4
            yield
            if c.kind != "cache":
                st8, t_st8 = c.st8, c.t_st8
                c.qnb, c.qnbt = W.qnb.next()
                qn3 = c.qnb.rearrange("p (h d) -> p h d", d=96)
                vop("dve", "tensor_tensor", [qgt, t_st8], [qgt], out=q3[:, :, 0:64], in0=q3[:, :, 0:64],
                    in1=bc(st8[:, 0:4].unsqueeze(2), [128, 4, 64]), op=ALU.mult)
                vop("dve", "tensor_tensor", [qgt, t_gqn], [c.qnbt], out=qn3[:, :, 0:64], in0=q3[:, :, 0:64],
                    in1=bc(gqn.unsqueeze(1), [128, 4, 64]), op=ALU.mult)
                vop("dve", "tensor_tensor", [qgt, t_st8], [qgt], out=q3[:, :, 64:96], in0=q3[:, :, 64:96],
                    in1=bc(st8[:, 4:8].unsqueeze(2), [128, 4, 32]), op=ALU.mult)
                vop("dve", "tensor_tensor", [qgt, t_gqr], [qgt], out=q3[:, :, 64:96], in0=q3[:, :, 64:96],
                    in1=bc(gqr.unsqueeze(1), [128, 4, 32]), op=ALU.mult)
                yield
                rope(q3[:, :, 64:96], 4, c.cs, qn3[:, :, 64:80], qn3[:, :, 80:96], [qgt], [c.qnbt], W.rt)
                yield
            k3 = kf.rearrange("p (h d) -> p h d", d=64)
            c.knb, c.knbt = W.knb.next()
            kn3 = c.knb.rearrange("p (h d) -> p h d", d=96)
            vop("dve", "tensor_tensor", [kft, c.t_st4], [kft], out=k3, in0=k3, in1=bc(c.st4[:, 0:4].unsqueeze(2), [128, 4, 64]), op=ALU.mult)
            vop("dve", "tensor_tensor", [kft, t_gkn], [c.knbt], out=kn3[:, :, 0:64], in0=k3, in1=bc(gkn.unsqueeze(1), [128, 4, 64]),
                op=ALU.mult)
            vop("pool", "tensor_copy", [t_kr[kt]], [c.knbt], out=kn3[:, :, 64:96], in_=bc(krb_v[:, kt, :].unsqueeze(1), [128, 4, 32]))
            yield

        def p2_tr(c):
            kt = c.kt
            if c.kind != "cache":
                qn3 = c.qnb.rearrange("p (h d) -> p h d", d=96)
                transposes(t_tb, [(tb_ap[0:96, h * 128:(h + 1) * 128], qn3[:, h, :]) for h in range(4)], [c.qnbt, t_identb], identb)
                qT, qTt = W.qT.next()
                c.qT, c.qTt = qT.rearrange("p (h n) -> p h n", n=128), qTt
                evac_copy(qT[0:96, :], tb_ap[0:96, 0:512], [t_tb], [qTt])
                yield
            kn3 = c.knb.rearrange("p (h d) -> p h d", d=96)
            transposes(t_tb, [(tb_ap[0:96, h * 128:(h + 1) * 128], kn3[:, h, :]) for h in range(4)], [c.knbt, t_identb], identb)
            evac_copy(kTg_v[0:96, :, kt * 128:(kt + 1) * 128], tb_ap[0:96, 0:512].rearrange("p (h n) -> p h n", n=128), [t_tb], [t_kTg[kt]])
            yield

        def a2(c):
            if c.kind == "cache":
                return
            g, kt, zt = c.g, c.kt, c.zt
            kts = list(range(c.kt0, kt + 1))
            oap, ot = obanks.next()

            def norm_hook():
                ov = oap[:, 0:260].rearrange("p (h d) -> p h d", d=65)
                rden, t_rden = W.rden.next()
                don, t_don = W.don.next()
                vop("dve", "reciprocal", [ot], [t_rden], out=rden[:, 0:4], in_=ov[:, :, 64])
                vop("dve", "tensor_tensor", [ot, t_rden], [t_don], out=don.rearrange("p (h d) -> p h d", d=64),
                    in0=ov[:, :, 0:64], in1=bc(rden[:, 0:4].unsqueeze(2), [128, 4, 64]), op=ALU.mult)
                vop("pool", "tensor_tensor", [t_don, t_zsB[zt]], [t_zsB[zt]], out=zsB_v[:, zt, g * 256:(g + 1) * 256],
                    in0=don, in1=zsB_v[:, zt, g * 256:(g + 1) * 256], op=ALU.mult)

            tiles = []
            for hh in range(4):
                for k in kts:
                    masks = []
                    if k == kt:
                        masks = ([(slice(32, 64), 0, 128), (slice(64, 128), 0, 128)] if c.kind == "sample"
                                 else [(slice(64, 128), 0, 64)])
                    tiles.append(dict(l=kTg_v[:, hh, k * 128:(k + 1) * 128], r=c.qT[:, hh, :], rdq=[t_kTg[k], c.qTt],
                                      V=Vg_v[:, k, hh, :], Vt=t_Vg[k], o=oap[:, hh * 65:(hh + 1) * 65], ot=ot,
                                      start=(k == kts[0]), stop=(k == kts[-1]), emul=None, masks=masks,
                                      after=(norm_hook if (hh == 3 and k == kts[-1]) else None)))
            yield from attn_stream(tiles, SCB, sbanks, W.PT)

        def sweep3_bufs():
            do_barrier()
            A.reset(work_mark)
            W.xs = ring_f32("x3_", 1024, 4)
            W.ogT = ring_bf16("ogT", 1024, 3)
            W.dt = ring_f32("dt3_", 1024, 2)

        def b3_load(c):
            if c.kind == "cache":
                return
            c.x, c.xt = W.xs.next()
            dma_in(c.x, c.ysrc, c.xt)
            return
            yield

        def b3a(c):
            if c.kind == "cache":
                return
            zt = c.zt
            transposes(t_tb, [(tb_ap[:, k * 128:(k + 1) * 128], zsB_v[:, zt, k * 128:(k + 1) * 128]) for k in range(8)],
                       [t_zsB[zt], t_identb], identb)
            ogT, ogTt = W.ogT.next()
            c.ogT, c.ogTt = ogT.rearrange("p (k n) -> p k n", n=128), ogTt
            evac_copy(ogT, tb_ap, [t_tb], [ogTt])
            yield

        def b3b(c):
            if c.kind == "cache":
                return
            dt_, dtt = W.dt.next()
            for cg in range(2):
                pap, pt = mmb.next()
                mm(pt, pap, [(c.ogT[:, k, :], wBout_v[:, k, cg * 512:(cg + 1) * 512]) for k in range(8)], [c.ogTt, t_wBout[cg]])
                cs_ = slice(cg * 512, cg * 512 + 512)
                vop("dve", "tensor_tensor", [pt, t_gateB], [dtt], out=dt_[:, cs_], in0=pap, in1=gateB[:, cs_], op=ALU.mult)
                vop("dve", "tensor_tensor", [dtt, c.xt], [c.xt], out=c.x[:, cs_], in0=dt_[:, cs_], in1=c.x[:, cs_], op=ALU.add)
                yield
            dma_out(c.y_dst, c.x, c.xt)

        import os
        DBG_B = os.environ.get("DBG_B", "")

        def run_seqB(s, tiles):
            if DBG_B == "w" or (DBG_B and s > 0):
                return
            sweep1_bufs(s)
            run_pipeline(tiles, [s1_load, s1_norm, s1_tr, s1_mm, s1_e1, s1_e2], late_first=True)
            if DBG_B == "s1":
                return
            sweep2_bufs()
            for g in range(4):
                for c in tiles:
                    c.g = g
                run_pipeline(tiles, [p2_mm, p2_norm, p2_tr, a2], order=[3, 0, 1, 2])
                if DBG_B == "s2":
                    return
            sweep3_bufs()
            run_pipeline(tiles, [b3_load, b3a, b3b], late_first=True)

        for s in range(nseq):
            tiles = []
            for t in range(NT):
                c = Ctx()
                c.kind = "prompt"
                c.pre = None
                c.kt, c.zt, c.cs, c.kt0 = t, t, t, 0
                r = slice(s * S + t * 128, s * S + (t + 1) * 128)
                c.ysrc = y0p[r, :]
                c.ckv_dst, c.kr_dst, c.y_dst = O["bcp"][r, :], O["brp"][r, :], O["yp"][r, :]
                tiles.append(c)
            run_seqB(s, tiles)
        tiles = []
        for t in range(NTC):
            c = Ctx()
            c.kind = "cache"
            c.pre = None
            c.kt = t
            c.zt = 0
            tiles.append(c)
        c = Ctx()
        c.kind = "sample"
        c.pre = None
        c.kt, c.zt, c.cs, c.kt0 = NTC, 0, NT, 0
        c.ysrc = y0s
        c.ckv_dst, c.kr_dst, c.y_dst = O["bcs"], O["brs"], O["ys"]
        tiles.append(c)
        run_seqB(nseq, tiles)

    sch.emit(nc, stack)
    stack.close()
    return nc


def build_layer_B(env):
    raise NotImplementedError


ROPE_THETA = 10000.0
B_COLS = np.concatenate([np.arange(672, 1696), np.arange(0, 672)])


def rope_tables(pos):
    inv = (np.float32(ROPE_THETA) ** (-np.arange(16, dtype=np.float32) / np.float32(16))).astype(np.float32)
    ang = pos.astype(np.float32)[:, None] * inv[None, :]
    return np.cos(ang).astype(np.float32), np.sin(ang).astype(np.float32)


def shared_inputs(inp, S, PAST):
    f = lambda a: np.ascontiguousarray(np.asarray(a, dtype=np.float32))
    sh = {}
    sh["norm_g"] = f(inp["norm_g"])
    sh["ada_w"] = f(inp["ada_w"]).reshape(2 * D, 3 * D)
    sh["ada_b"] = f(inp["ada_b"])
    w = f(inp["a_w_in"])[0]
    w = np.concatenate([w[:, 0:1024][:, PERM_COLS], w[:, 1024:2048][:, PERM_COLS], w[:, 2048:]], axis=1)
    sh["a_w_in"] = f(w)
    sh["a_g_q"] = f(inp["a_g_q"])
    sh["a_g_k"] = f(inp["a_g_k"])
    tab = f(inp["a_rel_bias"])[0]
    ki = np.arange(128)[:, None, None]
    jj = np.arange(2)[None, :, None]
    qi = np.arange(128)[None, None, :]
    rel = np.clip((4 - (3 + jj)) * 128 + qi - ki, -128, 128) + 128
    sh["erel"] = f(np.transpose(tab[:, rel], (1, 0, 2, 3)).reshape(128, 16 * 2 * 128))
    sh["cbias"] = f(tab[:, 256][None, :])
    sh["a_w_out"] = f(inp["a_w_out"])[0]
    sh["b_w_in"] = f(f(inp["b_w_in"])[0][:, B_COLS])
    for k in ("b_g_cq", "b_g_ckv", "b_g_qn", "b_g_qr", "b_g_kn", "b_g_kr"):
        sh[k] = f(inp[k])
    sh["b_w_uq"] = f(inp["b_w_uq"])[0]
    sh["b_w_ukv"] = f(inp["b_w_ukv"])[0]
    sh["b_w_out"] = f(inp["b_w_out"])[0]
    cp, sp_ = rope_tables(np.arange(S))
    NT = S // 128
    sh["cosp"] = f(cp.reshape(NT, 128, 16).transpose(1, 0, 2).reshape(128, NT * 16))
    sh["sinp"] = f(sp_.reshape(NT, 128, 16).transpose(1, 0, 2).reshape(128, NT * 16))
    sh["coss"], sh["sins"] = rope_tables(PAST + np.arange(128))
    sh["ident"] = np.eye(128, dtype=np.float32)
    return sh


def core_inputs(inp, sh, core, nseq, S, PAST):
    f = lambda a: np.ascontiguousarray(np.asarray(a, dtype=np.float32))
    m = dict(sh)
    m["xp"] = f(inp["x_prompt"][core * nseq:(core + 1) * nseq]).reshape(nseq * S, D)
    xs = np.zeros((128, D), np.float32)
    xs[:32] = np.asarray(inp["x_sample"][core])
    m["xs"] = xs
    m["ck"] = f(np.asarray(inp["cache_a_k"])[0, core].reshape(512, D)[:, PERM_COLS])
    m["cv"] = f(np.asarray(inp["cache_a_v"])[0, core].reshape(512, D))
    m["cckv"] = f(np.asarray(inp["cache_mla_ckv"])[0, core])
    m["ckr"] = f(np.asarray(inp["cache_mla_krope"])[0, core])
    c = np.concatenate([np.asarray(inp["c_prompt"])[core * nseq:(core + 1) * nseq], np.asarray(inp["c_sample"])[core:core + 1]], 0)
    NS = nseq + 1
    m["scT"] = f(c.T.reshape(8, 128, NS).transpose(1, 0, 2).reshape(128, 8 * NS))
    return m


_NC_CACHE = {}


def run_cores(inp, ncores, nseq, S, PAST, debug_layers=2):
    key = (nseq, S, PAST, debug_layers)
    if key not in _NC_CACHE:
        _NC_CACHE[key] = build(nseq, S, PAST, debug_layers)
    nc = _NC_CACHE[key]
    sh = shared_inputs(inp, S, PAST)
    in_maps = [core_inputs(inp, sh, c, nseq, S, PAST) for c in range(ncores)]
    res = run_bass_kernel_spmd(nc, in_maps, core_ids=list(range(ncores)))
    return res.results


def assemble(results, ncores, nseq, S):
    AR = min(512, S)
    inv = np.empty(1024, np.int64)
    inv[PERM_COLS] = np.arange(1024)
    cat = lambda k: np.concatenate([r[k] for r in results], 0)
    y_p = cat("yp").reshape(ncores * nseq, S, D)
    y_s = np.stack([r["ys"][:32] for r in results], 0)
    akp = cat("akp")[:, inv].reshape(1, ncores * nseq, AR, 16, 64)
    avp = cat("avp").reshape(1, ncores * nseq, AR, 16, 64)
    aks = np.stack([r["aks"][:32][:, inv] for r in results], 0).reshape(1, ncores, 32, 16, 64)
    avs = np.stack([r["avs"][:32] for r in results], 0).reshape(1, ncores, 32, 16, 64)
    bcp = cat("bcp").reshape(1, ncores * nseq, S, 256)
    brp = cat("brp").reshape(1, ncores * nseq, S, 32)
    bcs = np.stack([r["bcs"][:32] for r in results], 0).reshape(1, ncores, 32, 256)
    brs = np.stack([r["brs"][:32] for r in results], 0).reshape(1, ncores, 32, 32)
    return tuple(np.ascontiguousarray(a, dtype=np.float32) for a in (y_p, y_s, akp, avp, aks, avs, bcp, brp, bcs, brs))


def kernel(**inputs):
    res = run_cores(inputs, NCORES, 4, 2048, 2048)
    return assemble(res, NCORES, 4, 2048)
```
